# Optimizing a Trainium2 kernel written in Bass

```python
import jax, jax.numpy as jnp
from jax import lax
import numpy as np

D_MODEL = 1024
BATCH = 8
SEQ = 4096
DEPTH = 1

D_A = D_MODEL
H_A = 8
GROUP_A = D_A // H_A
CHUNK_A = 128
H_B = 4
KEY_B = D_MODEL // 2
VAL_B = D_MODEL
DK_B = KEY_B // H_B
DV_B = VAL_B // H_B
GATE_RANK = 16
GATE_NORM = 16.0
CHUNK_B = 64
EPS = 1e-6
LN_EPS = 1e-5

N_IN = 3 * D_A + 2 * KEY_B + 2 * VAL_B + GATE_RANK + 2 * D_MODEL
SPLIT_POINTS = (
    D_A,
    2 * D_A,
    3 * D_A,
    3 * D_A + KEY_B,
    3 * D_A + 2 * KEY_B,
    3 * D_A + 2 * KEY_B + VAL_B,
    3 * D_A + 2 * KEY_B + 2 * VAL_B,
    3 * D_A + 2 * KEY_B + 2 * VAL_B + GATE_RANK,
)

kernel_name = 'hybrid_gmlp_gla_gated_parallel'


def _rmsnorm(x, g):
    xf = x.astype(jnp.float32)
    y = xf * lax.rsqrt(jnp.mean(xf * xf, axis=-1, keepdims=True) + EPS)
    return (y * g.astype(jnp.float32)).astype(x.dtype)


def _layernorm(x, g, b):
    xf = x.astype(jnp.float32)
    mu = jnp.mean(xf, axis=-1, keepdims=True)
    xc = xf - mu
    y = xc * lax.rsqrt(jnp.mean(xc * xc, axis=-1, keepdims=True) + LN_EPS)
    return (y * g.astype(jnp.float32) + b.astype(jnp.float32)).astype(x.dtype)


def _spatial_gating(u, v, ln_g, ln_b, w_s, b_s):
    bsz, seq, _ = u.shape
    n_chunks = seq // CHUNK_A
    v = _layernorm(v, ln_g, ln_b)
    vc = v.reshape(bsz, n_chunks, CHUNK_A, H_A, GROUP_A)
    causal = jnp.tril(jnp.ones((CHUNK_A, CHUNK_A), dtype=bool))
    w = jnp.where(causal, w_s, jnp.zeros_like(w_s)).astype(v.dtype)
    mixed = jnp.einsum('hts,bnshc->bnthc', w, vc) + b_s.T.astype(v.dtype)[None, None, :, :, None]
    return u * mixed.reshape(bsz, seq, D_A)


def _gla_chunk_step(state, inp):
    q, k, v, g = inp
    b = jnp.cumsum(g, axis=1)
    b_last = b[:, -1]
    b_mid = b[:, CHUNK_B // 2 - 1][:, None]
    q_i = q * jnp.exp(b - b_mid)
    k_i = k * jnp.exp(b_mid - b)
    scores = jnp.einsum('bthk,bshk->bhts', q_i, k_i)
    causal = jnp.tril(jnp.ones((CHUNK_B, CHUNK_B), dtype=bool))
    scores = jnp.where(causal, scores, 0.0)
    o = (jnp.einsum('bhts,bshv->bthv', scores, v)
         + jnp.einsum('bthk,bhkv->bthv', q * jnp.exp(b), state))
    k_s = k * jnp.exp(b_last[:, None] - b)
    state = jnp.exp(b_last)[..., None] * state + jnp.einsum('bshk,bshv->bhkv', k_s, v)
    return state, o


def _gla(q, k, v, g):
    bsz, seq, _ = q.shape
    n_chunks = seq // CHUNK_B

    def to_chunks(t, d):
        t = t.astype(jnp.float32).reshape(bsz, n_chunks, CHUNK_B, H_B, d)
        return jnp.transpose(t, (1, 0, 2, 3, 4))

    qc = to_chunks(q, DK_B) * (DK_B ** -0.5)
    kc = to_chunks(k, DK_B)
    vc = to_chunks(v, DV_B)
    gc = to_chunks(g, DK_B)
    state0 = jnp.zeros((bsz, H_B, DK_B, DV_B), jnp.float32)
    _, o = lax.scan(_gla_chunk_step, state0, (qc, kc, vc, gc))
    o = jnp.transpose(o, (1, 0, 2, 3, 4)).reshape(bsz, seq, H_B, DV_B)
    return o


def setup_inputs(seed: int = 0) -> dict:
    key = jax.random.key(seed)
    ks = jax.random.split(key, 16)
    f32 = jnp.float32
    nrm = lambda k, shape, s: jax.random.normal(k, shape, f32) * s
    return {
        'x': nrm(ks[0], (BATCH, SEQ, D_MODEL), 1.0),
        'norm_g': 1.0 + nrm(ks[1], (DEPTH, D_MODEL), 0.02),
        'w_in': nrm(ks[2], (DEPTH, D_MODEL, N_IN), D_MODEL ** -0.5),
        'ln_v_g': 1.0 + nrm(ks[3], (DEPTH, D_A), 0.02),
        'ln_v_b': nrm(ks[4], (DEPTH, D_A), 0.02),
        'w_spatial': nrm(ks[5], (DEPTH, H_A, CHUNK_A, CHUNK_A), 0.5 * CHUNK_A ** -0.5),
        'b_spatial': 1.0 + nrm(ks[6], (DEPTH, H_A, CHUNK_A), 0.02),
        'w_gate_up': nrm(ks[7], (DEPTH, GATE_RANK, KEY_B), GATE_RANK ** -0.5),
        'b_gate_up': nrm(ks[8], (DEPTH, KEY_B), 0.01),
        'gla_norm_g': 1.0 + nrm(ks[9], (DEPTH, DV_B), 0.02),
        'w_branch_a': nrm(ks[10], (DEPTH, D_A, D_MODEL), D_A ** -0.5),
        'w_branch_b': nrm(ks[11], (DEPTH, VAL_B, D_MODEL), VAL_B ** -0.5),
        'w_out': nrm(ks[12], (DEPTH, D_MODEL, D_MODEL), D_MODEL ** -0.5),
        'final_norm_g': 1.0 + nrm(ks[13], (D_MODEL,), 0.02),
    }


def reference(x, norm_g, w_in, ln_v_g, ln_v_b, w_spatial, b_spatial, w_gate_up, b_gate_up,
              gla_norm_g, w_branch_a, w_branch_b, w_out, final_norm_g):
    bsz, seq, _ = x.shape
    for l in range(DEPTH):
        h = _rmsnorm(x, norm_g[l])
        proj = h @ w_in[l]
        u, v, z_a, q, k, v_b, z_b, lr, gates = jnp.split(proj, SPLIT_POINTS, axis=-1)
        a = _spatial_gating(jax.nn.gelu(u, approximate=False), jax.nn.gelu(v, approximate=False),
                            ln_v_g[l], ln_v_b[l], w_spatial[l], b_spatial[l])
        a = a * jax.nn.silu(z_a)
        logit = (lr @ w_gate_up[l] + b_gate_up[l]).astype(jnp.float32)
        log_alpha = jax.nn.log_sigmoid(logit) / GATE_NORM
        o = _gla(q, k, v_b, log_alpha)
        o = _rmsnorm(o, gla_norm_g[l]).astype(x.dtype).reshape(bsz, seq, VAL_B)
        o = o * jax.nn.silu(z_b)
        g_a, g_b = jnp.split(jax.nn.sigmoid(gates), 2, axis=-1)
        merged = g_a * (a @ w_branch_a[l]) + g_b * (o @ w_branch_b[l])
        x = x + merged @ w_out[l]
    return _rmsnorm(x, final_norm_g)
```

```python
import numpy as np
from contextlib import ExitStack
import concourse.bass as bass
import concourse.mybir as mybir
from concourse.bass_utils import run_bass_kernel_spmd

F32 = mybir.dt.float32
BF16 = mybir.dt.bfloat16
AF = mybir.ActivationFunctionType
ALU = mybir.AluOpType

D = 1024
NIN = 8208
NT = 4
TT = NT * 128
KC = 8
NB = 5
NS = 8
NCONV = 3
EPS = 1e-6
LN_EPS = 1e-5
C_U, C_V, C_ZA, C_Q, C_K, C_VB, C_ZB, C_LR, C_GA, C_GB = 0, 1024, 2048, 3072, 3584, 4096, 5120, 6144, 6160, 7184


class Prog:
    ENGS = ("pe", "act", "dve", "pool", "sync")

    def __init__(self, nc, es):
        self.nc = nc
        self.es = es
        self.ops = {e: [] for e in self.ENGS}
        self.cnt = {e: 0 for e in self.ENGS}
        self.sem = {e: es.enter_context(nc.semaphore("sem_" + e)) for e in ("pe", "act", "dve", "pool")}
        self.dsem = {}
        self.dcnt = {}
        self.res = {}
        self.known = {e: {} for e in self.ENGS}
        self.final_waits = []

    def _deps(self, reads, writes, eng=None):
        deps = []
        for k in reads:
            r = self.res.get(k)
            if r and r["w"] is not None:
                deps.append(r["w"])
            if r and isinstance(k, tuple) and k[0] == "ps":
                deps.extend(ev for ev in r["r"] if ev[0] != eng)
        for k in writes:
            r = self.res.get(k)
            if r:
                if r["w"] is not None:
                    deps.append(r["w"])
                deps.extend(r["r"])
        return deps

    def _record(self, ev, reads, writes):
        for k in reads:
            r = self.res.setdefault(k, {"w": None, "r": []})
            r["r"].append(ev)
        for k in writes:
            self.res[k] = {"w": ev, "r": []}

    def _waits(self, eng, deps):
        best = {}
        for (semname, val) in deps:
            if eng == "pe" and semname == "pe":
                continue
            if val > best.get(semname, 0):
                best[semname] = val
        waits = []
        kn = self.known[eng]
        for semname, val in best.items():
            if kn.get(semname, 0) >= val:
                continue
            kn[semname] = val
            waits.append((semname, val))
        return waits

    def op(self, eng, fn, reads=(), writes=()):
        deps = self._deps(reads, writes, eng)
        waits = self._waits(eng, deps)
        self.cnt[eng] += 1
        ev = (eng, self.cnt[eng])
        self.ops[eng].append((waits, fn, ev))
        self._record(ev, reads, writes)
        return ev

    def dma(self, eng, fn, semkey, reads=(), writes=(), final=False):
        if semkey not in self.dsem:
            self.dsem[semkey] = self.es.enter_context(self.nc.semaphore("d_" + semkey))
            self.dcnt[semkey] = 0
        deps = self._deps(reads, writes)
        waits = self._waits(eng, deps)
        self.dcnt[semkey] += 16
        ev = ("d:" + semkey, self.dcnt[semkey])
        self.ops[eng].append((waits, fn, ev))
        self._record(ev, reads, writes)
        if final:
            self.final_waits.append(ev)
        return ev

    def _semh(self, name):
        if name.startswith("d:"):
            return self.dsem[name[2:]]
        return self.sem[name]

    def emit(self, block):
        def run(engname):
            def body(e):
                for waits, fn, ev in self.ops[engname]:
                    for (sn, val) in waits:
                        e.wait_ge(self._semh(sn), val)
                    ins = fn(e)
                    ins.then_inc(self._semh(ev[0]), 16 if ev[0].startswith("d:") else 1)
                if engname == "sync":
                    best = {}
                    for sn, val in self.final_waits:
                        best[sn] = max(best.get(sn, 0), val)
                    for sn, val in best.items():
                        e.wait_ge(self._semh(sn), val)
            return body
        block.tensor(run("pe"))
        block.scalar(run("act"))
        block.vector(run("dve"))
        block.gpsimd(run("pool"))
        block.sync(run("sync"))


def build_program(T, dbg=False):
    NST = T // TT
    NTOT = T // 128
    nc = bass.Bass("TRN2", target_bir_lowering=False)

    def din(name, shape):
        return nc.dram_tensor(name, shape, F32, kind="ExternalInput").ap()

    x = din("x", [T, D])
    w_in = din("w_in", [D, NIN])
    w_a = din("w_a", [D, D])
    w_b = din("w_b", [D, D])
    w_o = din("w_o", [D, D])
    norm_g = din("norm_g", [1, D])
    ln_g = din("ln_g", [1, D])
    ln_b = din("ln_b", [1, D])
    w_sp = din("w_sp", [8, 128, 128])
    b_sp = din("b_sp", [1, 1024])
    w_gu = din("w_gu", [16, 512])
    b_gu = din("b_gu", [1, 512])
    gla_g = din("gla_g", [1, 256])
    fin_g = din("fin_g", [1, D])
    consts = din("consts", [128, 512])
    out = nc.dram_tensor("out", [T, D], F32, kind="ExternalOutput").ap()
    if dbg:
        d_hT = nc.dram_tensor("d_hT", [128, KC, TT], BF16, kind="ExternalOutput").ap()
        d_aT = nc.dram_tensor("d_aT", [128, KC, TT], BF16, kind="ExternalOutput").ap()
        d_onT = nc.dram_tensor("d_onT", [128, KC, TT], BF16, kind="ExternalOutput").ap()
        d_mT = nc.dram_tensor("d_mT", [128, KC, TT], BF16, kind="ExternalOutput").ap()

    NCG = 22
    wscr = nc.dram_tensor("wscr", [NCG, 128, KC, 512], BF16).ap()
    w_in_v = w_in.rearrange("(kc p) n -> p kc n", p=128)
    w_a_v = w_a.rearrange("(kc p) n -> p kc n", p=128)
    w_b_v = w_b.rearrange("(kc p) n -> p kc n", p=128)
    w_o_v = w_o.rearrange("(kc p) n -> p kc n", p=128)

    with ExitStack() as es:
        def sb(name, shape, dt):
            return es.enter_context(nc.sbuf_tensor(name, shape, dt))

        P = Prog(nc, es)
        ps = es.enter_context(nc.psum_tensor("ps", [128, 8, 512], F32))

        Gx = sb("Gx", [128, D], F32)
        Gf = sb("Gf", [128, D], F32)
        Gln = sb("Gln", [128, D], F32)
        Bln = sb("Bln", [128, D], F32)
        ggb = sb("ggb", [128, 256], F32)
        cst = sb("cst", [128, 512], F32)
        identf = cst[:, 0:128]
        Uneg = cst[:, 128:256]
        maskf = cst[:, 256:384]
        sel0 = cst[:, 384:512]
        ident = sb("ident", [128, 128], BF16)
        WsT = sb("WsT", [128, 8, 128], BF16)
        bspad = sb("bspad", [128, 1024], BF16)
        onespad = sb("onespad", [128, 128], BF16)
        wgu = sb("wgu", [128, 512], BF16)
        wlr = sb("wlr", [128, KC, 16], BF16)
        ring = [sb("ring%d" % i, [128, KC, 512], BF16) for i in range(NB)]
        hT = [sb("hT%d" % i, [128, KC, TT], BF16) for i in range(2)]
        xt = [sb("xt%d" % i, [128, D], F32) for i in range(2)]
        xr = [sb("xr%d" % i, [128, D], F32) for i in range(2)]
        junk = sb("junk", [128, D], BF16)
        BFs = [sb("BFs%d" % i, [128, D], BF16) for i in range(2)]
        guT = sb("guT", [128, KC, TT], BF16)
        sza = [sb("sza%d" % i, [128, TT], BF16) for i in range(2)]
        F32A = [sb("F32A%d" % i, [128, D], F32) for i in range(2)]
        F32B = [sb("F32B%d" % i, [128, 512], F32) for i in range(3)]
        lrT = [sb("lrT%d" % i, [128, TT], BF16) for i in range(2)]
        Et = sb("Et", [128, 4, TT], F32)
        Einv = sb("Einv", [128, 4, TT], F32)
        qiT = sb("qiT", [128, 4, TT], BF16)
        kiT = sb("kiT", [128, 4, TT], BF16)
        ktm = [sb("ktm%d" % i, [128, 512], BF16) for i in range(2)]
        vb = [sb("vb%d" % i, [128, D], BF16) for i in range(NT)]
        szb = [sb("szb%d" % i, [128, D], BF16) for i in range(NT)]
        scT = [sb("scT%d" % i, [128, 4, 128], BF16) for i in range(2)]
        S = sb("S", [128, 4, 256], F32)
        tkv4 = sb("tkv4", [128, 4, 256], F32)
        Sp = [sb("Sp%d" % i, [128, 4, 256], BF16) for i in range(2)]
        onT = sb("onT", [128, KC, TT], BF16)
        mT = sb("mT", [128, KC, TT], BF16)
        rconst = sb("rconst", [128, 16], F32)
        ssx = sb("ssx", [128, NS], F32)
        rsx = sb("rsx", [128, NS], F32)
        bst = sb("bst", [128, NS * 12], F32)
        mvv = sb("mvv", [128, NS * 2], F32)
        rsv = sb("rsv", [128, NS], F32)
        nbm = sb("nbm", [128, NS * 4], F32)
        pbm = sb("pbm", [128, NS * 4], F32)
        dlt = sb("dlt", [128, NS * 4], F32)
        emid = sb("emid", [128, NS * 4], F32)
        elast = sb("elast", [128, NS * 4], F32)
        edl = sb("edl", [128, NS * 4], F32)
        sso = sb("sso", [128, NS * 4], F32)
        rso = sb("rso", [128, NS * 4], F32)
        ssf = sb("ssf", [128, NS], F32)
        rsf = sb("rsf", [128, NS], F32)

        pstate = {"ptr": 0, "held": set()}

        def bank():
            while pstate["ptr"] % 8 in pstate["held"]:
                pstate["ptr"] += 1
            b = pstate["ptr"] % 8
            pstate["ptr"] += 1
            return b

        def bank_pair():
            while True:
                if pstate["ptr"] % 2 == 1:
                    pstate["ptr"] += 1
                b = pstate["ptr"] % 8
                if b in pstate["held"] or (b + 1) in pstate["held"]:
                    pstate["ptr"] += 2
                    continue
                pstate["ptr"] += 2
                return b

        def PB(b):
            return ("ps", b)

        CG_ORDER = ["vb0", "vb1", "zb0", "zb1", "q", "k", "za0", "za1", "v0", "v1", "u0", "u1",
                    "ga0", "wa0", "gb0", "wb0", "ga1", "wa1", "gb1", "wb1", "wo0", "wo1"]
        CG_SRC = {"vb0": (w_in_v, C_VB), "vb1": (w_in_v, C_VB + 512), "zb0": (w_in_v, C_ZB), "zb1": (w_in_v, C_ZB + 512),
                  "q": (w_in_v, C_Q), "k": (w_in_v, C_K), "v0": (w_in_v, C_V), "v1": (w_in_v, C_V + 512),
                  "u0": (w_in_v, C_U), "u1": (w_in_v, C_U + 512), "za0": (w_in_v, C_ZA), "za1": (w_in_v, C_ZA + 512),
                  "ga0": (w_in_v, C_GA), "ga1": (w_in_v, C_GA + 512), "gb0": (w_in_v, C_GB), "gb1": (w_in_v, C_GB + 512),
                  "wa0": (w_a_v, 0), "wa1": (w_a_v, 512), "wb0": (w_b_v, 0), "wb1": (w_b_v, 512),
                  "wo0": (w_o_v, 0), "wo1": (w_o_v, 512)}
        cgs = []
        cg_index = {}
        for s in range(NST):
            for ci, name in enumerate(CG_ORDER):
                cg_index[(s, name)] = len(cgs)
                cgs.append((s, ci, CG_SRC[name]))
        rstate = {"next_load": 0, "released": set()}

        def ring_pump():
            while rstate["next_load"] < len(cgs):
                i = rstate["next_load"]
                if i >= NB and (i - NB) not in rstate["released"]:
                    break
                st, ci, (view, c0) = cgs[i]
                buf = ring[i % NB]
                kconv = ci % NCONV
                if st <= kconv:
                    P.dma("pool", lambda e, buf=buf, view=view, c0=c0: e.dma_start(out=buf[:], in_=view[:, :, c0:c0 + 512]),
                          "ring%d" % (i % NB), writes=[("ring", i % NB)])
                    if st == kconv and NST > st + 1:
                        P.dma("sync", lambda e, buf=buf, ci=ci: e.dma_start(out=wscr[ci], in_=buf[:]),
                              "scr%d" % ci, reads=[("ring", i % NB)], writes=[("scr", ci)])
                else:
                    P.dma("sync", lambda e, buf=buf, ci=ci: e.dma_start(out=buf[:], in_=wscr[ci]),
                          "ring%d" % (i % NB), reads=[("scr", ci)], writes=[("ring", i % NB)])
                rstate["next_load"] += 1

        def ring_take(s, name):
            i = cg_index[(s, name)]
            assert i < rstate["next_load"], ("CG not loaded yet", s, name)
            return i % NB

        def ring_release(s, name):
            rstate["released"].add(cg_index[(s, name)])
            ring_pump()

        def cload(dst, src, key, eng="sync", writes=()):
            P.dma(eng, lambda e: e.dma_start(out=dst, in_=src), key, writes=list(writes))

        cload(cst[:], consts[:, :], "c_cst", writes=["cst"])
        def bcast_row(dst, src_row, W, stg, stgkey, key, dkey):
            P.op("dve", lambda e: e.memset(stg[:, 0:W], 0.0), writes=[stgkey])
            P.dma("sync", lambda e: e.dma_start(out=stg[0:1, 0:W], in_=src_row), key, writes=[stgkey])
            for c0 in range(0, W, 512):
                w = min(512, W - c0)
                b = bank()
                P.op("pe", lambda e, b=b, c0=c0, w=w: e.matmul(ps[:, b, 0:w], lhsT=sel0, rhs=stg[:, c0:c0 + w], start=True, stop=True),
                     reads=[stgkey, "cst"], writes=[PB(b)])
                P.op("act", lambda e, b=b, c0=c0, w=w: e.activation(out=dst[:, c0:c0 + w], in_=ps[:, b, 0:w], func=AF.Copy),
                     reads=[PB(b)], writes=[dkey])

        bcast_row(Gx, norm_g[0:1, :], D, xr[0], ("xr", 0), "c_gx", "Gx")
        P.op("dve", lambda e: e.memset(wgu[:], 0.0), writes=["wgu"])
        P.op("dve", lambda e: e.memset(bspad[:], 0.0), writes=["bspad"])
        P.op("dve", lambda e: e.memset(onespad[:], 0.0), writes=["onespad"])
        P.op("dve", lambda e: e.memset(onespad[0:1, :], 1.0), writes=["onespad"])
        P.op("dve", lambda e: e.memset(onespad[32:33, :], 1.0), writes=["onespad"])
        for i in range(2):
            P.op("dve", lambda e, i=i: e.memset(lrT[i][:], 0.0), writes=[("lrT", i)])
            P.op("dve", lambda e, i=i: e.memset(lrT[i][32:33, :], 1.0), writes=[("lrT", i)])
        P.op("dve", lambda e: e.memset(S[:], 0.0), writes=["S"])
        P.op("dve", lambda e: e.memset(rconst[:, 0:4], float(D * EPS)), writes=["rconst"])
        P.op("dve", lambda e: e.memset(rconst[:, 4:8], float(LN_EPS)), writes=["rconst"])
        P.op("dve", lambda e: e.memset(rconst[:, 8:12], float(256 * EPS)), writes=["rconst"])
        P.op("dve", lambda e: e.memset(rconst[:, 12:16], -0.5), writes=["rconst"])
        P.op("dve", lambda e: e.tensor_scalar(out=Gx[:], in0=Gx[:], scalar1=float(D ** 0.5), scalar2=None, op0=ALU.mult), reads=["Gx"], writes=["Gx"])
        P.op("dve", lambda e: e.tensor_copy(out=ident[:], in_=identf), reads=["cst"], writes=["ident"])
        cload(wgu[0:16, :], w_gu[:, :], "c_wgu", eng="pool", writes=["wgu"])
        cload(wgu[32:33, :], b_gu[0:1, :], "c_bgu", eng="pool", writes=["wgu"])
        cload(wlr[:], w_in_v[:, :, C_LR:C_LR + 16], "c_wlr", eng="pool", writes=["wlr"])
        hn = [sb("hn%d" % i, [128, D], BF16) for i in range(2)]
        JK = [("junk", h) for h in range(4)]

        def rstd_chain(ss_ap, out_ap, scale, eps, key_in, key_out, w=1):
            keys_in = key_in if isinstance(key_in, list) else [key_in]
            ceps = {1.0 / D: 0, 1.0: 1, 1.0 / 256: 2}[scale]
            P.op("pool", lambda e: e.tensor_tensor(out=out_ap, in0=ss_ap, in1=rconst[:, ceps * 4:ceps * 4 + w], op=ALU.add),
                 reads=keys_in + ["rconst"], writes=[key_out])
            P.op("pool", lambda e: e.tensor_tensor(out=out_ap, in0=out_ap, in1=rconst[:, 12:12 + w], op=ALU.pow),
                 reads=[key_out, "rconst"], writes=[key_out])

        def proj_fm(rb, c_off, M, hbuf, b):
            for kc in range(KC):
                P.op("pe", lambda e, kc=kc: e.matmul(ps[0:M, b, 0:TT], lhsT=ring[rb][:, kc, c_off:c_off + M], rhs=hT[hbuf][:, kc, :],
                                                     start=(kc == 0), stop=(kc == KC - 1)),
                     reads=[("ring", rb), ("hT", hbuf)], writes=[PB(b)])

        def proj_tm(rb, j, hbuf, b):
            for kc in range(KC):
                P.op("pe", lambda e, kc=kc: e.matmul(ps[:, b, :], lhsT=hT[hbuf][:, kc, j * 128:(j + 1) * 128], rhs=ring[rb][:, kc, :],
                                                     start=(kc == 0), stop=(kc == KC - 1)),
                     reads=[("ring", rb), ("hT", hbuf)], writes=[PB(b)])

        def fr_a(s, j):
            G = s * NT + j
            g = G % NS
            r = g % 2
            P.dma("sync", lambda e: e.dma_start(out=xt[r][:], in_=x[G * 128:(G + 1) * 128, :]), "xt%d" % r, writes=[("xt", r)])
            P.op("act", lambda e: e.activation(out=junk[:], in_=xt[r][:], func=AF.Square, accum_out=ssx[:, g:g + 1]),
                 reads=[("xt", r)], writes=[("ssx", g)] + JK)
            rstd_chain(ssx[:, g:g + 1], rsx[:, g:g + 1], 1.0 / D, EPS, ("ssx", g), ("rsx", g))
            P.op("dve", lambda e: e.scalar_tensor_tensor(out=hn[r][:], in0=xt[r][:], scalar=rsx[:, g:g + 1], in1=Gx[:],
                                                         op0=ALU.mult, op1=ALU.mult),
                 reads=[("xt", r), ("rsx", g), "Gx"], writes=[("hn", r)])

        def fr_b(s, j):
            G = s * NT + j
            g = G % NS
            r = g % 2
            hbuf = s % 2
            b = bank()
            pv = ps[:, b, :].bitcast(BF16)
            for kc in range(KC):
                P.op("pe", lambda e, kc=kc: e.transpose(pv[:, kc * 128:(kc + 1) * 128], hn[r][:, kc * 128:(kc + 1) * 128], ident[:]),
                     reads=[("hn", r), "ident"], writes=[PB(b)])
            P.op("dve", lambda e: e.tensor_copy(out=hT[hbuf][:, :, j * 128:(j + 1) * 128],
                                                in_=pv[:, 0:1024].rearrange("p (c t) -> p c t", c=8)),
                 reads=[PB(b)], writes=[("hT", hbuf)])

        def A_za(s, c):
            rb = ring_take(s, "za%d" % (c // 4))
            b = bank()
            proj_fm(rb, (c % 4) * 128, 128, s % 2, b)
            P.op("act", lambda e: e.activation(out=guT[:, c, :], in_=ps[:, b, 0:TT], func=AF.Silu),
                 reads=[PB(b)], writes=[("guT", c)])
            if c % 4 == 3:
                ring_release(s, "za%d" % (c // 4))

        def A_u(s, c):
            rb = ring_take(s, "u%d" % (c // 4))
            b = bank()
            proj_fm(rb, (c % 4) * 128, 128, s % 2, b)
            P.op("act", lambda e: e.activation(out=sza[c % 2][:], in_=ps[:, b, 0:TT], func=AF.Gelu),
                 reads=[PB(b)], writes=[("sza", c % 2)])
            P.op("pool", lambda e: e.tensor_tensor(out=guT[:, c, :], in0=guT[:, c, :], in1=sza[c % 2][:], op=ALU.mult),
                 reads=[("guT", c), ("sza", c % 2)], writes=[("guT", c)])
            if c % 4 == 3:
                ring_release(s, "u%d" % (c // 4))

        def A_v(s, j):
            G = s * NT + j
            g = G % NS
            r = g % 2
            gv = F32A[r]
            for half in range(2):
                rb = ring_take(s, "v%d" % half)
                b = bank()
                proj_tm(rb, j, s % 2, b)
                P.op("act", lambda e, b=b, half=half: e.activation(out=gv[:, half * 512:(half + 1) * 512], in_=ps[:, b, :], func=AF.Gelu),
                     reads=[PB(b)], writes=[("F32A", r)])
            for half in range(2):
                P.op("dve", lambda e, half=half: e.bn_stats(out=bst[:, g * 12 + half * 6:g * 12 + half * 6 + 6],
                                                            in_=gv[:, half * 512:(half + 1) * 512]),
                     reads=[("F32A", r)], writes=[("bst", g, half)])
            P.op("dve", lambda e: e.bn_aggr(out=mvv[:, g * 2:g * 2 + 2], in_=bst[:, g * 12:g * 12 + 12]),
                 reads=[("bst", g, 0), ("bst", g, 1)], writes=[("mvv", g)])
            rstd_chain(mvv[:, g * 2 + 1:g * 2 + 2], rsv[:, g:g + 1], 1.0, LN_EPS, ("mvv", g), ("rsv", g))
            P.op("dve", lambda e: e.scalar_tensor_tensor(out=gv[:], in0=gv[:], scalar=mvv[:, g * 2:g * 2 + 1], in1=Gln[:],
                                                         op0=ALU.subtract, op1=ALU.mult),
                 reads=[("F32A", r), ("mvv", g), "Gln"], writes=[("F32A", r)])
            P.op("dve", lambda e: e.scalar_tensor_tensor(out=BFs[r][:], in0=gv[:], scalar=rsv[:, g:g + 1], in1=Bln[:],
                                                         op0=ALU.mult, op1=ALU.add),
                 reads=[("F32A", r), ("rsv", g), "Bln"], writes=[("BFs", r)])
            if j == NT - 1:
                ring_release(s, "v0")
                ring_release(s, "v1")

        def A_sp(s, j):
            G = s * NT + j
            g = G % NS
            r = g % 2
            b2 = bank_pair()
            for h in range(8):
                bb = b2 + h // 4
                o = ps[:, bb, (h % 4) * 128:(h % 4 + 1) * 128]
                P.op("pe", lambda e, o=o, h=h: e.matmul(o, lhsT=BFs[r][:, h * 128:(h + 1) * 128], rhs=WsT[:, h, :], start=True, stop=False),
                     reads=[("BFs", r), "WsT"], writes=[PB(bb)])
                P.op("pe", lambda e, o=o, h=h: e.matmul(o, lhsT=onespad[:], rhs=bspad[:, h * 128:(h + 1) * 128], start=False, stop=True),
                     reads=["onespad", "bspad"], writes=[PB(bb)])
            for half in range(2):
                bb = b2 + half
                P.op("dve", lambda e, bb=bb, half=half: e.tensor_tensor(
                    out=guT[:, half * 4:(half + 1) * 4, j * 128:(j + 1) * 128],
                    in0=ps[:, bb, :].rearrange("p (h t) -> p h t", h=4),
                    in1=guT[:, half * 4:(half + 1) * 4, j * 128:(j + 1) * 128], op=ALU.mult),
                    reads=[PB(bb)] + [("guT", c) for c in range(half * 4, half * 4 + 4)],
                    writes=[("guT", c) for c in range(half * 4, half * 4 + 4)])

        def B_lr(s):
            hbuf = s % 2
            lb = s % 2
            b = bank()
            for kc in range(KC):
                P.op("pe", lambda e, kc=kc: e.matmul(ps[0:16, b, 0:TT], lhsT=wlr[:, kc, :], rhs=hT[hbuf][:, kc, :],
                                                     start=(kc == 0), stop=(kc == KC - 1)),
                     reads=["wlr", ("hT", hbuf)], writes=[PB(b)])
            P.op("dve", lambda e: e.tensor_copy(out=lrT[lb][0:16, :], in_=ps[0:16, b, 0:TT]), reads=[PB(b)], writes=[("lrT", lb)])

        def B_logit(s, j):
            G = s * NT + j
            g = G % NS
            lb = s % 2
            ev = F32B[0]
            ls = F32B[1 + g % 2]
            lskey = ("F32B", 1 + g % 2)
            b = bank()
            P.op("pe", lambda e: e.matmul(ps[:, b, :], lhsT=lrT[lb][:, j * 128:(j + 1) * 128], rhs=wgu[:], start=True, stop=True),
                 reads=[("lrT", lb), "wgu"], writes=[PB(b)])
            P.op("act", lambda e: e.activation(out=ev[:], in_=ps[:, b, :], func=AF.Exp, scale=-1.0),
                 reads=[PB(b)], writes=[("F32B", 0)])
            P.op("act", lambda e: e.activation(out=ls[:], in_=ev[:], func=AF.Ln, bias=1.0, scale=1.0),
                 reads=[("F32B", 0)], writes=[lskey])

        def B_cum(s, j):
            G = s * NT + j
            g = G % NS
            ls = F32B[1 + g % 2]
            lskey = ("F32B", 1 + g % 2)
            bc = bank()
            for h in range(4):
                P.op("pe", lambda e, h=h: e.matmul(ps[:, bc, h * 128:(h + 1) * 128], lhsT=ls[:, h * 128:(h + 1) * 128], rhs=Uneg,
                                                   start=True, stop=True),
                     reads=[lskey, "cst"], writes=[PB(bc)])
            bv = ps[:, bc, :].rearrange("p (h t) -> p h t", h=4)
            sl = slice(g * 4, g * 4 + 4)
            P.op("dve", lambda e: e.tensor_copy(out=pbm[:, sl], in_=bv[:, :, 63]), reads=[PB(bc)], writes=[("pbm", g)])
            P.op("dve", lambda e: e.tensor_scalar(out=nbm[:, sl], in0=bv[:, :, 63], scalar1=-1.0, scalar2=None, op0=ALU.mult),
                 reads=[PB(bc)], writes=[("nbm", g)])
            P.op("dve", lambda e: e.tensor_tensor(out=dlt[:, sl], in0=bv[:, :, 127], in1=pbm[:, sl], op=ALU.subtract),
                 reads=[PB(bc), ("pbm", g)], writes=[("dlt", g)])
            for h in range(4):
                P.op("act", lambda e, h=h: e.activation(out=Et[:, h, j * 128:(j + 1) * 128], in_=ps[:, bc, h * 128:(h + 1) * 128],
                                                        func=AF.Exp, bias=nbm[:, g * 4 + h:g * 4 + h + 1], scale=1.0),
                     reads=[PB(bc), ("nbm", g)], writes=[("Et", h)])
                P.op("act", lambda e, h=h: e.activation(out=Einv[:, h, j * 128:(j + 1) * 128], in_=ps[:, bc, h * 128:(h + 1) * 128],
                                                        func=AF.Exp, bias=pbm[:, g * 4 + h:g * 4 + h + 1], scale=-1.0),
                     reads=[PB(bc), ("pbm", g)], writes=[("Einv", h)])
            P.op("act", lambda e: e.activation(out=emid[:, sl], in_=pbm[:, sl], func=AF.Exp), reads=[("pbm", g)], writes=[("emid", g)])
            P.op("act", lambda e: e.activation(out=elast[:, sl], in_=bv[:, :, 127], func=AF.Exp), reads=[PB(bc)], writes=[("elast", g)])
            P.op("act", lambda e: e.activation(out=edl[:, sl], in_=dlt[:, sl], func=AF.Exp), reads=[("dlt", g)], writes=[("edl", g)])

        def B_q(s, h):
            rb = ring_take(s, "q")
            b = bank()
            proj_fm(rb, h * 128, 128, s % 2, b)
            P.op("dve", lambda e: e.scalar_tensor_tensor(out=qiT[:, h, :], in0=ps[:, b, 0:TT], scalar=float(128 ** -0.5), in1=Et[:, h, :],
                                                         op0=ALU.mult, op1=ALU.mult),
                 reads=[PB(b), ("Et", h)], writes=[("qiT", h)])
            if h == 3:
                ring_release(s, "q")

        def B_k(s, h):
            rb = ring_take(s, "k")
            b = bank()
            proj_fm(rb, h * 128, 128, s % 2, b)
            P.op("dve", lambda e: e.tensor_tensor(out=kiT[:, h, :], in0=ps[:, b, 0:TT], in1=Einv[:, h, :], op=ALU.mult),
                 reads=[PB(b), ("Einv", h)], writes=[("kiT", h)])
            if h == 3:
                ring_release(s, "k")

        def B_tm(s, j, name, dst, func, key):
            for half in range(2):
                rb = ring_take(s, "%s%d" % (name, half))
                b = bank()
                proj_tm(rb, j, s % 2, b)
                P.op("act", lambda e, b=b, half=half: e.activation(out=dst[j][:, half * 512:(half + 1) * 512], in_=ps[:, b, :], func=func),
                     reads=[PB(b)], writes=[(key, j)])
            if j == NT - 1:
                ring_release(s, name + "0")
                ring_release(s, name + "1")

        def B_vb(s, j):
            B_tm(s, j, "vb", vb, AF.Copy, "vb")

        def B_zb(s, j):
            B_tm(s, j, "zb", szb, AF.Silu, "szb")

        rbank = {}

        def R_sc(s, j):
            G = s * NT + j
            g = G % NS
            r = g % 2
            js = slice(j * 128, (j + 1) * 128)
            P.op("dve", lambda e: e.tensor_tensor(out=Sp[r][:], in0=S[:], in1=emid[:, g * 4:g * 4 + 4].unsqueeze(2).to_broadcast([128, 4, 256]), op=ALU.mult),
                 reads=["S", ("emid", g)], writes=[("Sp", r)])
            bt = bank()
            for h in range(4):
                P.op("pe", lambda e, h=h: e.matmul(ps[:, bt, h * 128:(h + 1) * 128], lhsT=kiT[:, h, js], rhs=ident[:], start=True, stop=True),
                     reads=[("kiT", h), "ident"], writes=[PB(bt)])
            P.op("act", lambda e: e.activation(out=ktm[r][:], in_=ps[:, bt, :], func=AF.Copy), reads=[PB(bt)], writes=[("ktm", r)])
            bs_ = bank()
            for h in range(4):
                P.op("pe", lambda e, h=h: e.matmul(ps[:, bs_, h * 128:(h + 1) * 128], lhsT=kiT[:, h, js], rhs=qiT[:, h, js], start=True, stop=True),
                     reads=[("kiT", h), ("qiT", h)], writes=[PB(bs_)])
            P.op("dve", lambda e: e.tensor_tensor(out=scT[r][:], in0=ps[:, bs_, :].rearrange("p (h t) -> p h t", h=4),
                                                  in1=maskf.unsqueeze(1).to_broadcast([128, 4, 128]), op=ALU.mult),
                 reads=[PB(bs_), "cst"], writes=[("scT", r)])

        def R_o1(s, j):
            G = s * NT + j
            g = G % NS
            r = g % 2
            bo = bank_pair()
            rbank[g] = bo
            pstate["held"].update((bo, bo + 1))
            for h in range(4):
                bb = bo + h // 2
                o = ps[:, bb, (h % 2) * 256:(h % 2 + 1) * 256]
                P.op("pe", lambda e, o=o, h=h: e.matmul(o, lhsT=scT[r][:, h, :], rhs=vb[j][:, h * 256:(h + 1) * 256], start=(h % 2 == 0), stop=False,
                                                        skip_group_check=True),
                     reads=[("scT", r), ("vb", j)], writes=[PB(bb)])
            bk = bank_pair()
            for h in range(4):
                bb = bk + h // 2
                o = ps[:, bb, (h % 2) * 256:(h % 2 + 1) * 256]
                P.op("pe", lambda e, o=o, h=h: e.matmul(o, lhsT=ktm[r][:, h * 128:(h + 1) * 128], rhs=vb[j][:, h * 256:(h + 1) * 256], start=True, stop=True),
                     reads=[("ktm", r), ("vb", j)], writes=[PB(bb)])
            P.op("dve", lambda e: e.tensor_tensor(out=tkv4[:], in0=ps[:, bk:bk + 2, :].rearrange("p a (h v) -> p (a h) v", h=2),
                                                  in1=edl[:, g * 4:g * 4 + 4].unsqueeze(2).to_broadcast([128, 4, 256]), op=ALU.mult),
                 reads=[PB(bk), PB(bk + 1), ("edl", g)], writes=["tkv4"])
            P.op("pool", lambda e: e.tensor_tensor(out=S[:], in0=S[:], in1=elast[:, g * 4:g * 4 + 4].unsqueeze(2).to_broadcast([128, 4, 256]), op=ALU.mult),
                 reads=["S", ("elast", g)], writes=["S"])
            P.op("pool", lambda e: e.tensor_tensor(out=S[:], in0=S[:], in1=tkv4[:], op=ALU.add),
                 reads=["S", "tkv4"], writes=["S"])

        def R_o2(s, j):
            G = s * NT + j
            g = G % NS
            r = g % 2
            js = slice(j * 128, (j + 1) * 128)
            bo = rbank[g]
            for h in range(4):
                bb = bo + h // 2
                o = ps[:, bb, (h % 2) * 256:(h % 2 + 1) * 256]
                P.op("pe", lambda e, o=o, h=h: e.matmul(o, lhsT=qiT[:, h, js], rhs=Sp[r][:, h, :], start=False, stop=True, skip_group_check=True),
                     reads=[("qiT", h), ("Sp", r)], writes=[PB(bb)])
            for h in range(4):
                bb = bo + h // 2
                o = ps[:, bb, (h % 2) * 256:(h % 2 + 1) * 256]
                P.op("act", lambda e, o=o, h=h: e.activation(out=junk[:, h * 256:(h + 1) * 256], in_=o, func=AF.Square, accum_out=sso[:, g * 4 + h:g * 4 + h + 1]),
                     reads=[PB(bb)], writes=[("sso", g, h), ("junk", h)])
            rstd_chain(sso[:, g * 4:g * 4 + 4], rso[:, g * 4:g * 4 + 4], 1.0 / 256, EPS, [("sso", g, h) for h in range(4)], ("rso", g), w=4)

        def R_on(s, j):
            G = s * NT + j
            g = G % NS
            r = g % 2
            bo = rbank[g]
            for h in range(4):
                bb = bo + h // 2
                o = ps[:, bb, (h % 2) * 256:(h % 2 + 1) * 256]
                P.op("dve", lambda e, o=o, h=h: e.scalar_tensor_tensor(out=BFs[r][:, h * 256:(h + 1) * 256], in0=o, scalar=rso[:, g * 4 + h:g * 4 + h + 1],
                                                                      in1=zgs[j][:, h * 256:(h + 1) * 256], op0=ALU.mult, op1=ALU.mult),
                     reads=[PB(bb), ("rso", g), ("szb", j)], writes=[("BFs", r)])
            pstate["held"].difference_update((bo, bo + 1))

        def R_tr(s, j):
            G = s * NT + j
            g = G % NS
            r = g % 2
            js = slice(j * 128, (j + 1) * 128)
            bt2 = bank()
            pv = ps[:, bt2, :].bitcast(BF16)
            for c in range(KC):
                P.op("pe", lambda e, c=c: e.transpose(pv[:, c * 128:(c + 1) * 128], BFs[r][:, c * 128:(c + 1) * 128], ident[:]),
                     reads=[("BFs", r), "ident"], writes=[PB(bt2)])
            P.op("act", lambda e: e.activation(out=onT[:, :, js], in_=pv[:, 0:1024].rearrange("p (c t) -> p c t", c=8), func=AF.Copy),
                 reads=[PB(bt2)], writes=["onT"])

        def M_a(s, c):
            hbuf = s % 2
            half, n = c // 4, c % 4
            r = c % 2
            rga = ring_take(s, "ga%d" % half)
            rwa = ring_take(s, "wa%d" % half)
            rgb = ring_take(s, "gb%d" % half)
            t1 = F32A[r][:, 0:512]
            sga = F32B[0]
            sgb = F32B[1]
            b = bank()
            proj_fm(rga, n * 128, 128, hbuf, b)
            P.op("act", lambda e: e.activation(out=sga[:], in_=ps[:, b, 0:TT], func=AF.Sigmoid), reads=[PB(b)], writes=[("F32B", 0)])
            b2 = bank()
            for kc in range(KC):
                P.op("pe", lambda e, kc=kc: e.matmul(ps[:, b2, 0:TT], lhsT=ring[rwa][:, kc, n * 128:(n + 1) * 128], rhs=guT[:, kc, :],
                                                     start=(kc == 0), stop=(kc == KC - 1)),
                     reads=[("ring", rwa), ("guT", kc)], writes=[PB(b2)])
            P.op("dve", lambda e: e.tensor_tensor(out=t1, in0=ps[:, b2, 0:TT], in1=sga[:], op=ALU.mult),
                 reads=[PB(b2), ("F32B", 0), ("F32A", r)], writes=[("F32A", r, 0)])
            b3 = bank()
            proj_fm(rgb, n * 128, 128, hbuf, b3)
            P.op("act", lambda e: e.activation(out=sgb[:], in_=ps[:, b3, 0:TT], func=AF.Sigmoid), reads=[PB(b3)], writes=[("F32B", 1)])

        def M_b(s, c):
            half, n = c // 4, c % 4
            r = c % 2
            rwb = ring_take(s, "wb%d" % half)
            t1 = F32A[r][:, 0:512]
            t2 = F32A[r][:, 512:1024]
            sgb = F32B[1]
            b4 = bank()
            for kc in range(KC):
                P.op("pe", lambda e, kc=kc: e.matmul(ps[:, b4, 0:TT], lhsT=ring[rwb][:, kc, n * 128:(n + 1) * 128], rhs=onT[:, kc, :],
                                                     start=(kc == 0), stop=(kc == KC - 1)),
                     reads=[("ring", rwb), "onT"], writes=[PB(b4)])
            P.op("dve", lambda e: e.tensor_tensor(out=t2, in0=ps[:, b4, 0:TT], in1=sgb[:], op=ALU.mult),
                 reads=[PB(b4), ("F32B", 1), ("F32A", r)], writes=[("F32A", r, 1)])
            P.op("dve", lambda e: e.tensor_tensor(out=mT[:, c, :], in0=t1, in1=t2, op=ALU.add),
                 reads=[("F32A", r, 0), ("F32A", r, 1)], writes=[("mT", c)])
            if n == 3:
                for nm in ("ga", "wa", "gb", "wb"):
                    ring_release(s, "%s%d" % (nm, half))

        def M_n(s, c):
            M_a(s, c)
            M_b(s, c)

        def Y_load(s, j):
            G = s * NT + j
            g = G % NS
            r = g % 2
            P.dma("sync", lambda e: e.dma_start(out=xr[r][:], in_=x[G * 128:(G + 1) * 128, :]), "xr%d" % r, writes=[("xr", r)])

        def Y(s, j):
            G = s * NT + j
            g = G % NS
            r = g % 2
            js = slice(j * 128, (j + 1) * 128)
            by = bank_pair()
            for half in range(2):
                rw = ring_take(s, "wo%d" % half)
                for kc in range(KC):
                    P.op("pe", lambda e, kc=kc, half=half, rw=rw: e.matmul(ps[:, by + half, :], lhsT=mT[:, kc, js], rhs=ring[rw][:, kc, :],
                                                                          start=(kc == 0), stop=(kc == KC - 1)),
                         reads=[("ring", rw), ("mT", kc)], writes=[PB(by + half)])
            P.op("dve", lambda e: e.tensor_tensor(out=xr[r][:], in0=ps[:, by:by + 2, :].rearrange("p a b -> p (a b)"), in1=xr[r][:], op=ALU.add),
                 reads=[PB(by), PB(by + 1), ("xr", r)], writes=[("xr", r)])
            P.op("act", lambda e: e.activation(out=junk[:], in_=xr[r][:], func=AF.Square, accum_out=ssf[:, g:g + 1]),
                 reads=[("xr", r)], writes=[("ssf", g)] + JK)
            rstd_chain(ssf[:, g:g + 1], rsf[:, g:g + 1], 1.0 / D, EPS, ("ssf", g), ("rsf", g))
            P.op("dve", lambda e: e.scalar_tensor_tensor(out=xr[r][:], in0=xr[r][:], scalar=rsf[:, g:g + 1], in1=Gf[:], op0=ALU.mult, op1=ALU.mult),
                 reads=[("xr", r), ("rsf", g), "Gf"], writes=[("xr", r)])
            P.dma("sync", lambda e: e.dma_start(out=out[G * 128:(G + 1) * 128, :], in_=xr[r][:]), "st%d" % r, reads=[("xr", r)], final=True)
            if j == NT - 1:
                ring_release(s, "wo0")
                ring_release(s, "wo1")

        zgs = szb

        def B_zg(s, j):
            for h in range(4):
                P.op("dve", lambda e, h=h: e.tensor_tensor(out=szb[j][:, h * 256:(h + 1) * 256], in0=szb[j][:, h * 256:(h + 1) * 256], in1=ggb[:], op=ALU.mult),
                     reads=[("szb", j), "ggb"], writes=[("szb", j)])

        for j in range(NT):
            fr_a(0, j)
            fr_b(0, j)
        Wraw = F32A[0][:, :].rearrange("p (h s) -> p h s", h=8)
        cload(Wraw, w_sp.rearrange("h t s -> t h s"), "c_wsp", writes=[("F32A", 0)])
        for hh in range(2):
            b = bank()
            for h4 in range(4):
                h = hh * 4 + h4
                P.op("pe", lambda e, b=b, h=h, h4=h4: e.matmul(ps[:, b, h4 * 128:(h4 + 1) * 128], lhsT=Wraw[:, h, :], rhs=identf,
                                                              start=True, stop=True),
                     reads=[("F32A", 0), "cst"], writes=[PB(b)])
            for h4 in range(4):
                h = hh * 4 + h4
                P.op("dve", lambda e, b=b, h=h, h4=h4: e.tensor_tensor(out=WsT[:, h, :], in0=ps[:, b, h4 * 128:(h4 + 1) * 128], in1=maskf,
                                                                      op=ALU.mult),
                     reads=[PB(b), "cst"], writes=["WsT"])
        bsf = F32A[1]
        cload(bsf[0:1, :], b_sp[0:1, :], "c_bs0", writes=[("F32A", 1)])
        cload(bsf[32:33, :], b_sp[0:1, :], "c_bs1", writes=[("F32A", 1)])
        P.op("dve", lambda e: e.tensor_copy(out=bspad[0:1, :], in_=bsf[0:1, :]), reads=[("F32A", 1)], writes=["bspad"])
        P.op("dve", lambda e: e.tensor_copy(out=BFs[0][32:33, :], in_=bsf[32:33, :]), reads=[("F32A", 1)], writes=[("BFs", 0)])
        P.op("dve", lambda e: e.tensor_tensor(out=bspad[32:33, :], in0=bsf[32:33, :], in1=BFs[0][32:33, :], op=ALU.subtract),
             reads=[("F32A", 1), ("BFs", 0)], writes=["bspad"])

        bcast_row(Gln, ln_g[0:1, :], D, xr[1], ("xr", 1), "c_gln", "Gln")
        bcast_row(Bln, ln_b[0:1, :], D, xr[0], ("xr", 0), "c_bln", "Bln")
        bcast_row(ggb, gla_g[0:1, :], 256, xr[1], ("xr", 1), "c_ggb", "ggb")
        bcast_row(Gf, fin_g[0:1, :], D, xr[0], ("xr", 0), "c_gf", "Gf")
        P.op("dve", lambda e: e.tensor_scalar(out=Gf[:], in0=Gf[:], scalar1=float(D ** 0.5), scalar2=None, op0=ALU.mult), reads=["Gf"], writes=["Gf"])
        P.op("dve", lambda e: e.tensor_scalar(out=ggb[:], in0=ggb[:], scalar1=16.0, scalar2=None, op0=ALU.mult), reads=["ggb"], writes=["ggb"])
        ring_pump()
        for s in range(NST):
            last = (s + 1 == NST)
            if s == 0:
                B_lr(s)
                B_logit(s, 0); B_logit(s, 1)
            B_vb(s, 0)
            B_cum(s, 0); B_cum(s, 1)
            B_logit(s, 2); B_logit(s, 3)
            B_vb(s, 1)
            B_vb(s, 2)
            B_cum(s, 2); B_cum(s, 3)
            B_vb(s, 3)
            B_zb(s, 0); B_zb(s, 1)
            for h in range(4):
                B_q(s, h)
            for h in range(4):
                B_k(s, h)
            B_zb(s, 2); B_zb(s, 3)
            for c in range(8):
                A_za(s, c)
            for j in range(NT):
                B_zg(s, j)
            for j in range(NT):
                R_sc(s, j)
                if j > 0:
                    A_sp(s, j - 1)
                    R_on(s, j - 1)
                A_v(s, j)
                R_o1(s, j)
                if j > 0:
                    R_tr(s, j - 1)
                A_u(s, 2 * j)
                A_u(s, 2 * j + 1)
                R_o2(s, j)
            A_sp(s, NT - 1)
            R_on(s, NT - 1)
            M_a(s, 0)
            R_tr(s, NT - 1)
            seq = {0: ("a", 0), 1: ("a", 1), 2: ("b", 0), 3: ("a", 2), 4: ("b", 1), 5: ("a", 3), 6: ("b", 2), 7: ("b", 3)}
            for c in range(8):
                if c > 0:
                    M_a(s, c)
                M_b(s, c)
                if not last:
                    kind, jj = seq[c]
                    if kind == "a":
                        fr_a(s + 1, jj)
                    else:
                        fr_b(s + 1, jj)
                    if c == 2:
                        pass
                if c >= 4:
                    Y_load(s, c - 4) if c - 4 < 2 else None
            if not last:
                B_lr(s + 1)
            for j in range(NT):
                Y(s, j)
                if j + 2 < NT:
                    Y_load(s, j + 2)
                if not last and j < 2:
                    B_logit(s + 1, j)
            if dbg and s == 0:
                P.dma("sync", lambda e: e.dma_start(out=d_hT[:, :, :], in_=hT[0][:]), "dbg0", reads=[("hT", 0)], final=True)
                P.dma("sync", lambda e: e.dma_start(out=d_aT[:, :, :], in_=guT[:]), "dbg1", reads=[("guT", c) for c in range(8)], final=True)
                P.dma("sync", lambda e: e.dma_start(out=d_onT[:, :, :], in_=onT[:]), "dbg2", reads=["onT"], final=True)
                P.dma("sync", lambda e: e.dma_start(out=d_mT[:, :, :], in_=mT[:]), "dbg3", reads=[("mT", c) for c in range(8)], final=True)

        print("sbuf bytes remaining", nc.sbuf_bytes_remaining)
        block = es.enter_context(nc.Block())
        P.emit(block)
    return nc


def _consts():
    c = np.zeros((128, 512), np.float32)
    c[0, 384:512] = 1.0
    c[:, 0:128] = np.eye(128, dtype=np.float32)
    tri = (np.arange(128)[:, None] <= np.arange(128)[None, :]).astype(np.float32)
    c[:, 128:256] = tri * np.float32(-1.0 / 16.0)
    c[:, 256:384] = tri
    return c


def make_in_maps(x, norm_g, w_in, ln_v_g, ln_v_b, w_spatial, b_spatial, w_gate_up, b_gate_up,
                 gla_norm_g, w_branch_a, w_branch_b, w_out, final_norm_g, n_cores, T):
    f = lambda a: np.ascontiguousarray(np.asarray(a, dtype=np.float32))
    shared = {
        "w_in": f(w_in[0]), "w_a": f(w_branch_a[0]), "w_b": f(w_branch_b[0]), "w_o": f(w_out[0]),
        "norm_g": f(norm_g[0]).reshape(1, D), "ln_g": f(ln_v_g[0]).reshape(1, D), "ln_b": f(ln_v_b[0]).reshape(1, D),
        "w_sp": f(w_spatial[0]), "b_sp": f(b_spatial[0]).reshape(1, 1024),
        "w_gu": f(w_gate_up[0]), "b_gu": f(b_gate_up[0]).reshape(1, 512),
        "gla_g": f(gla_norm_g[0]).reshape(1, 256), "fin_g": f(final_norm_g).reshape(1, D),
        "consts": _consts(),
    }
    maps = []
    for b in range(n_cores):
        m = dict(shared)
        m["x"] = f(np.asarray(x)[b, :T])
        maps.append(m)
    return maps


_NC_CACHE = {}


def kernel(x, norm_g, w_in, ln_v_g, ln_v_b, w_spatial, b_spatial, w_gate_up, b_gate_up,
           gla_norm_g, w_branch_a, w_branch_b, w_out, final_norm_g):
    x = np.asarray(x)
    B, T, _ = x.shape
    nc = build_program(T)
    in_maps = make_in_maps(x, norm_g, w_in, ln_v_g, ln_v_b, w_spatial, b_spatial, w_gate_up, b_gate_up,
                           gla_norm_g, w_branch_a, w_branch_b, w_out, final_norm_g, B, T)
    res = run_bass_kernel_spmd(nc, in_maps, core_ids=list(range(B)))
    return np.stack([np.asarray(r["out"], dtype=np.float32) for r in res.results], axis=0)
```

```python
import numpy as np
from contextlib import ExitStack
import concourse.bass as bass
import concourse.mybir as mybir
from concourse.bass_utils import run_bass_kernel_spmd

F32 = mybir.dt.float32
BF16 = mybir.dt.bfloat16
AF = mybir.ActivationFunctionType
ALU = mybir.AluOpType

D = 1024
NIN = 8208
NT = 4
TT = NT * 128
KC = 8
NB = 5
NS = 8
EPS = 1e-6
LN_EPS = 1e-5
C_U, C_V, C_ZA, C_Q, C_K, C_VB, C_ZB, C_LR, C_GA, C_GB = 0, 1024, 2048, 3072, 3584, 4096, 5120, 6144, 6160, 7184


class Prog:
    ENGS = ("pe", "act", "dve", "pool", "sync")

    def __init__(self, nc, es):
        self.nc = nc
        self.es = es
        self.ops = {e: [] for e in self.ENGS}
        self.cnt = {e: 0 for e in self.ENGS}
        self.sem = {e: es.enter_context(nc.semaphore("sem_" + e)) for e in ("pe", "act", "dve", "pool")}
        self.dsem = {}
        self.dcnt = {}
        self.res = {}
        self.known = {e: {} for e in self.ENGS}
        self.final_waits = []

    def _deps(self, reads, writes, eng=None):
        deps = []
        for k in reads:
            r = self.res.get(k)
            if r and r["w"] is not None:
                deps.append(r["w"])
            if r and isinstance(k, tuple) and k[0] == "ps":
                deps.extend(ev for ev in r["r"] if ev[0] != eng)
        for k in writes:
            r = self.res.get(k)
            if r:
                if r["w"] is not None:
                    deps.append(r["w"])
                deps.extend(r["r"])
        return deps

    def _record(self, ev, reads, writes):
        for k in reads:
            r = self.res.setdefault(k, {"w": None, "r": []})
            r["r"].append(ev)
        for k in writes:
            self.res[k] = {"w": ev, "r": []}

    def _waits(self, eng, deps):
        best = {}
        for (semname, val) in deps:
            if eng == "pe" and semname == "pe":
                continue
            if val > best.get(semname, 0):
                best[semname] = val
        waits = []
        kn = self.known[eng]
        for semname, val in best.items():
            if kn.get(semname, 0) >= val:
                continue
            kn[semname] = val
            waits.append((semname, val))
        return waits

    def op(self, eng, fn, reads=(), writes=()):
        deps = self._deps(reads, writes, eng)
        waits = self._waits(eng, deps)
        self.cnt[eng] += 1
        ev = (eng, self.cnt[eng])
        self.ops[eng].append((waits, fn, ev))
        self._record(ev, reads, writes)
        return ev

    def dma(self, eng, fn, semkey, reads=(), writes=(), final=False):
        if semkey not in self.dsem:
            self.dsem[semkey] = self.es.enter_context(self.nc.semaphore("d_" + semkey))
            self.dcnt[semkey] = 0
        deps = self._deps(reads, writes)
        waits = self._waits(eng, deps)
        self.dcnt[semkey] += 16
        ev = ("d:" + semkey, self.dcnt[semkey])
        self.ops[eng].append((waits, fn, ev))
        self._record(ev, reads, writes)
        if final:
            self.final_waits.append(ev)
        return ev

    def _semh(self, name):
        if name.startswith("d:"):
            return self.dsem[name[2:]]
        return self.sem[name]

    def emit(self, block):
        def run(engname):
            def body(e):
                for waits, fn, ev in self.ops[engname]:
                    for (sn, val) in waits:
                        e.wait_ge(self._semh(sn), val)
                    ins = fn(e)
                    ins.then_inc(self._semh(ev[0]), 16 if ev[0].startswith("d:") else 1)
                if engname == "sync":
                    best = {}
                    for sn, val in self.final_waits:
                        best[sn] = max(best.get(sn, 0), val)
                    for sn, val in best.items():
                        e.wait_ge(self._semh(sn), val)
            return body
        block.tensor(run("pe"))
        block.scalar(run("act"))
        block.vector(run("dve"))
        block.gpsimd(run("pool"))
        block.sync(run("sync"))


def build_program(T, dbg=False):
    NST = T // TT
    NTOT = T // 128
    nc = bass.Bass("TRN2", target_bir_lowering=False)

    def din(name, shape):
        return nc.dram_tensor(name, shape, F32, kind="ExternalInput").ap()

    x = din("x", [T, D])
    w_in = din("w_in", [D, NIN])
    w_a = din("w_a", [D, D])
    w_b = din("w_b", [D, D])
    w_o = din("w_o", [D, D])
    norm_g = din("norm_g", [1, D])
    ln_g = din("ln_g", [1, D])
    ln_b = din("ln_b", [1, D])
    w_sp = din("w_sp", [8, 128, 128])
    b_sp = din("b_sp", [1, 1024])
    w_gu = din("w_gu", [16, 512])
    b_gu = din("b_gu", [1, 512])
    gla_g = din("gla_g", [1, 256])
    fin_g = din("fin_g", [1, D])
    consts = din("consts", [128, 512])
    out = nc.dram_tensor("out", [T, D], F32, kind="ExternalOutput").ap()
    if dbg:
        d_hT = nc.dram_tensor("d_hT", [128, KC, TT], BF16, kind="ExternalOutput").ap()
        d_aT = nc.dram_tensor("d_aT", [128, KC, TT], BF16, kind="ExternalOutput").ap()
        d_onT = nc.dram_tensor("d_onT", [128, KC, TT], BF16, kind="ExternalOutput").ap()
        d_mT = nc.dram_tensor("d_mT", [128, KC, TT], BF16, kind="ExternalOutput").ap()

    NCG = 22
    wscr = nc.dram_tensor("wscr", [NCG, 128, KC, 512], BF16).ap()
    w_in_v = w_in.rearrange("(kc p) n -> p kc n", p=128)
    w_a_v = w_a.rearrange("(kc p) n -> p kc n", p=128)
    w_b_v = w_b.rearrange("(kc p) n -> p kc n", p=128)
    w_o_v = w_o.rearrange("(kc p) n -> p kc n", p=128)

    with ExitStack() as es:
        def sb(name, shape, dt):
            return es.enter_context(nc.sbuf_tensor(name, shape, dt))

        P = Prog(nc, es)
        ps = es.enter_context(nc.psum_tensor("ps", [128, 8, 512], F32))

        Gx = sb("Gx", [128, D], F32)
        Gf = sb("Gf", [128, D], F32)
        Gln = sb("Gln", [128, D], F32)
        Bln = sb("Bln", [128, D], F32)
        ggb = sb("ggb", [128, 256], F32)
        cst = sb("cst", [128, 512], F32)
        identf = cst[:, 0:128]
        Uneg = cst[:, 128:256]
        maskf = cst[:, 256:384]
        sel0 = cst[:, 384:512]
        ident = sb("ident", [128, 128], BF16)
        WsT = sb("WsT", [128, 8, 128], BF16)
        bspad = sb("bspad", [128, 1024], BF16)
        onespad = sb("onespad", [128, 128], BF16)
        wgu = sb("wgu", [128, 512], BF16)
        wlr = sb("wlr", [128, KC, 16], BF16)
        ring = [sb("ring%d" % i, [128, KC, 512], BF16) for i in range(NB)]
        hT = [sb("hT%d" % i, [128, KC, TT], BF16) for i in range(2)]
        xt = [sb("xt%d" % i, [128, D], F32) for i in range(2)]
        xr = [sb("xr%d" % i, [128, D], F32) for i in range(2)]
        junk = sb("junk", [128, D], BF16)
        BFs = [sb("BFs%d" % i, [128, D], BF16) for i in range(2)]
        guT = sb("guT", [128, KC, TT], BF16)
        sza = [sb("sza%d" % i, [128, TT], BF16) for i in range(2)]
        F32A = [sb("F32A%d" % i, [128, D], F32) for i in range(2)]
        F32B = [sb("F32B%d" % i, [128, 512], F32) for i in range(3)]
        lrT = [sb("lrT%d" % i, [128, TT], BF16) for i in range(2)]
        Et = sb("Et", [128, 4, TT], F32)
        Einv = sb("Einv", [128, 4, TT], F32)
        qiT = sb("qiT", [128, 4, TT], BF16)
        kiT = sb("kiT", [128, 4, TT], BF16)
        ktm = [sb("ktm%d" % i, [128, 512], BF16) for i in range(2)]
        vb = [sb("vb%d" % i, [128, D], BF16) for i in range(NT)]
        szb = [sb("szb%d" % i, [128, D], BF16) for i in range(NT)]
        scT = [sb("scT%d" % i, [128, 4, 128], BF16) for i in range(2)]
        S = sb("S", [128, 4, 256], F32)
        tkv4 = sb("tkv4", [128, 4, 256], F32)
        Sp = [sb("Sp%d" % i, [128, 4, 256], BF16) for i in range(2)]
        onT = sb("onT", [128, KC, TT], BF16)
        mT = sb("mT", [128, KC, TT], BF16)
        rconst = sb("rconst", [128, 16], F32)
        ssx = sb("ssx", [128, NS], F32)
        rsx = sb("rsx", [128, NS], F32)
        bst = sb("bst", [128, NS * 12], F32)
        mvv = sb("mvv", [128, NS * 2], F32)
        rsv = sb("rsv", [128, NS], F32)
        nbm = sb("nbm", [128, NS * 4], F32)
        pbm = sb("pbm", [128, NS * 4], F32)
        dlt = sb("dlt", [128, NS * 4], F32)
        emid = sb("emid", [128, NS * 4], F32)
        elast = sb("elast", [128, NS * 4], F32)
        edl = sb("edl", [128, NS * 4], F32)
        sso = sb("sso", [128, NS * 4], F32)
        rso = sb("rso", [128, NS * 4], F32)
        ssf = sb("ssf", [128, NS], F32)
        rsf = sb("rsf", [128, NS], F32)

        pstate = {"ptr": 0, "held": set()}

        def bank():
            while pstate["ptr"] % 8 in pstate["held"]:
                pstate["ptr"] += 1
            b = pstate["ptr"] % 8
            pstate["ptr"] += 1
            return b

        def bank_pair():
            while True:
                if pstate["ptr"] % 2 == 1:
                    pstate["ptr"] += 1
                b = pstate["ptr"] % 8
                if b in pstate["held"] or (b + 1) in pstate["held"]:
                    pstate["ptr"] += 2
                    continue
                pstate["ptr"] += 2
                return b

        def PB(b):
            return ("ps", b)

        CG_ORDER = ["vb0", "vb1", "zb0", "zb1", "q", "k", "za0", "za1", "v0", "v1", "u0", "u1",
                    "ga0", "wa0", "gb0", "wb0", "ga1", "wa1", "gb1", "wb1", "wo0", "wo1"]
        CG_SRC = {"vb0": (w_in_v, C_VB), "vb1": (w_in_v, C_VB + 512), "zb0": (w_in_v, C_ZB), "zb1": (w_in_v, C_ZB + 512),
                  "q": (w_in_v, C_Q), "k": (w_in_v, C_K), "v0": (w_in_v, C_V), "v1": (w_in_v, C_V + 512),
                  "u0": (w_in_v, C_U), "u1": (w_in_v, C_U + 512), "za0": (w_in_v, C_ZA), "za1": (w_in_v, C_ZA + 512),
                  "ga0": (w_in_v, C_GA), "ga1": (w_in_v, C_GA + 512), "gb0": (w_in_v, C_GB), "gb1": (w_in_v, C_GB + 512),
                  "wa0": (w_a_v, 0), "wa1": (w_a_v, 512), "wb0": (w_b_v, 0), "wb1": (w_b_v, 512),
                  "wo0": (w_o_v, 0), "wo1": (w_o_v, 512)}
        cgs = []
        cg_index = {}
        for s in range(NST):
            for ci, name in enumerate(CG_ORDER):
                cg_index[(s, name)] = len(cgs)
                cgs.append((s, ci, CG_SRC[name]))
        rstate = {"next_load": 0, "released": set()}

        def ring_pump():
            while rstate["next_load"] < len(cgs):
                i = rstate["next_load"]
                if i >= NB and (i - NB) not in rstate["released"]:
                    break
                st, ci, (view, c0) = cgs[i]
                buf = ring[i % NB]
                if st == 0:
                    P.dma("pool", lambda e, buf=buf, view=view, c0=c0: e.dma_start(out=buf[:], in_=view[:, :, c0:c0 + 512]),
                          "ring%d" % (i % NB), writes=[("ring", i % NB)])
                    if NST > 1:
                        P.dma("sync", lambda e, buf=buf, ci=ci: e.dma_start(out=wscr[ci], in_=buf[:]),
                              "scr%d" % ci, reads=[("ring", i % NB)], writes=[("scr", ci)])
                else:
                    P.dma("sync", lambda e, buf=buf, ci=ci: e.dma_start(out=buf[:], in_=wscr[ci]),
                          "ring%d" % (i % NB), reads=[("scr", ci)], writes=[("ring", i % NB)])
                rstate["next_load"] += 1

        def ring_take(s, name):
            i = cg_index[(s, name)]
            assert i < rstate["next_load"], ("CG not loaded yet", s, name)
            return i % NB

        def ring_release(s, name):
            rstate["released"].add(cg_index[(s, name)])
            ring_pump()

        def cload(dst, src, key, eng="sync", writes=()):
            P.dma(eng, lambda e: e.dma_start(out=dst, in_=src), key, writes=list(writes))

        cload(cst[:], consts[:, :], "c_cst", writes=["cst"])
        def bcast_row(dst, src_row, W, stg, stgkey, key, dkey):
            P.op("dve", lambda e: e.memset(stg[:, 0:W], 0.0), writes=[stgkey])
            P.dma("sync", lambda e: e.dma_start(out=stg[0:1, 0:W], in_=src_row), key, writes=[stgkey])
            for c0 in range(0, W, 512):
                w = min(512, W - c0)
                b = bank()
                P.op("pe", lambda e, b=b, c0=c0, w=w: e.matmul(ps[:, b, 0:w], lhsT=sel0, rhs=stg[:, c0:c0 + w], start=True, stop=True),
                     reads=[stgkey, "cst"], writes=[PB(b)])
                P.op("act", lambda e, b=b, c0=c0, w=w: e.activation(out=dst[:, c0:c0 + w], in_=ps[:, b, 0:w], func=AF.Copy),
                     reads=[PB(b)], writes=[dkey])

        bcast_row(Gx, norm_g[0:1, :], D, xr[0], ("xr", 0), "c_gx", "Gx")
        P.op("dve", lambda e: e.memset(wgu[:], 0.0), writes=["wgu"])
        P.op("dve", lambda e: e.memset(bspad[:], 0.0), writes=["bspad"])
        P.op("dve", lambda e: e.memset(onespad[:], 0.0), writes=["onespad"])
        P.op("dve", lambda e: e.memset(onespad[0:1, :], 1.0), writes=["onespad"])
        P.op("dve", lambda e: e.memset(onespad[32:33, :], 1.0), writes=["onespad"])
        for i in range(2):
            P.op("dve", lambda e, i=i: e.memset(lrT[i][:], 0.0), writes=[("lrT", i)])
            P.op("dve", lambda e, i=i: e.memset(lrT[i][32:33, :], 1.0), writes=[("lrT", i)])
        P.op("dve", lambda e: e.memset(S[:], 0.0), writes=["S"])
        P.op("dve", lambda e: e.memset(rconst[:, 0:4], float(D * EPS)), writes=["rconst"])
        P.op("dve", lambda e: e.memset(rconst[:, 4:8], float(LN_EPS)), writes=["rconst"])
        P.op("dve", lambda e: e.memset(rconst[:, 8:12], float(256 * EPS)), writes=["rconst"])
        P.op("dve", lambda e: e.memset(rconst[:, 12:16], -0.5), writes=["rconst"])
        P.op("dve", lambda e: e.tensor_scalar(out=Gx[:], in0=Gx[:], scalar1=float(D ** 0.5), scalar2=None, op0=ALU.mult), reads=["Gx"], writes=["Gx"])
        P.op("dve", lambda e: e.tensor_copy(out=ident[:], in_=identf), reads=["cst"], writes=["ident"])
        cload(wgu[0:16, :], w_gu[:, :], "c_wgu", eng="pool", writes=["wgu"])
        cload(wgu[32:33, :], b_gu[0:1, :], "c_bgu", eng="pool", writes=["wgu"])
        cload(wlr[:], w_in_v[:, :, C_LR:C_LR + 16], "c_wlr", eng="pool", writes=["wlr"])
        hn = [sb("hn%d" % i, [128, D], BF16) for i in range(2)]
        JK = [("junk", h) for h in range(4)]

        def rstd_chain(ss_ap, out_ap, scale, eps, key_in, key_out, w=1):
            keys_in = key_in if isinstance(key_in, list) else [key_in]
            ceps = {1.0 / D: 0, 1.0: 1, 1.0 / 256: 2}[scale]
            P.op("pool", lambda e: e.tensor_tensor(out=out_ap, in0=ss_ap, in1=rconst[:, ceps * 4:ceps * 4 + w], op=ALU.add),
                 reads=keys_in + ["rconst"], writes=[key_out])
            P.op("pool", lambda e: e.tensor_tensor(out=out_ap, in0=out_ap, in1=rconst[:, 12:12 + w], op=ALU.pow),
                 reads=[key_out, "rconst"], writes=[key_out])

        def proj_fm(rb, c_off, M, hbuf, b):
            for kc in range(KC):
                P.op("pe", lambda e, kc=kc: e.matmul(ps[0:M, b, 0:TT], lhsT=ring[rb][:, kc, c_off:c_off + M], rhs=hT[hbuf][:, kc, :],
                                                     start=(kc == 0), stop=(kc == KC - 1)),
                     reads=[("ring", rb), ("hT", hbuf)], writes=[PB(b)])

        def proj_tm(rb, j, hbuf, b):
            for kc in range(KC):
                P.op("pe", lambda e, kc=kc: e.matmul(ps[:, b, :], lhsT=hT[hbuf][:, kc, j * 128:(j + 1) * 128], rhs=ring[rb][:, kc, :],
                                                     start=(kc == 0), stop=(kc == KC - 1)),
                     reads=[("ring", rb), ("hT", hbuf)], writes=[PB(b)])

        def fr_a(s, j):
            G = s * NT + j
            g = G % NS
            r = g % 2
            P.dma("sync", lambda e: e.dma_start(out=xt[r][:], in_=x[G * 128:(G + 1) * 128, :]), "xt%d" % r, writes=[("xt", r)])
            P.op("act", lambda e: e.activation(out=junk[:], in_=xt[r][:], func=AF.Square, accum_out=ssx[:, g:g + 1]),
                 reads=[("xt", r)], writes=[("ssx", g)] + JK)
            rstd_chain(ssx[:, g:g + 1], rsx[:, g:g + 1], 1.0 / D, EPS, ("ssx", g), ("rsx", g))
            P.op("dve", lambda e: e.scalar_tensor_tensor(out=hn[r][:], in0=xt[r][:], scalar=rsx[:, g:g + 1], in1=Gx[:],
                                                         op0=ALU.mult, op1=ALU.mult),
                 reads=[("xt", r), ("rsx", g), "Gx"], writes=[("hn", r)])

        def fr_b(s, j):
            G = s * NT + j
            g = G % NS
            r = g % 2
            hbuf = s % 2
            b = bank()
            pv = ps[:, b, :].bitcast(BF16)
            for kc in range(KC):
                P.op("pe", lambda e, kc=kc: e.transpose(pv[:, kc * 128:(kc + 1) * 128], hn[r][:, kc * 128:(kc + 1) * 128], ident[:]),
                     reads=[("hn", r), "ident"], writes=[PB(b)])
            P.op("dve", lambda e: e.tensor_copy(out=hT[hbuf][:, :, j * 128:(j + 1) * 128],
                                                in_=pv[:, 0:1024].rearrange("p (c t) -> p c t", c=8)),
                 reads=[PB(b)], writes=[("hT", hbuf)])

        def A_za(s, c):
            rb = ring_take(s, "za%d" % (c // 4))
            b = bank()
            proj_fm(rb, (c % 4) * 128, 128, s % 2, b)
            P.op("act", lambda e: e.activation(out=guT[:, c, :], in_=ps[:, b, 0:TT], func=AF.Silu),
                 reads=[PB(b)], writes=[("guT", c)])
            if c % 4 == 3:
                ring_release(s, "za%d" % (c // 4))

        def A_u(s, c):
            rb = ring_take(s, "u%d" % (c // 4))
            b = bank()
            proj_fm(rb, (c % 4) * 128, 128, s % 2, b)
            P.op("act", lambda e: e.activation(out=sza[c % 2][:], in_=ps[:, b, 0:TT], func=AF.Gelu),
                 reads=[PB(b)], writes=[("sza", c % 2)])
            P.op("pool", lambda e: e.tensor_tensor(out=guT[:, c, :], in0=guT[:, c, :], in1=sza[c % 2][:], op=ALU.mult),
                 reads=[("guT", c), ("sza", c % 2)], writes=[("guT", c)])
            if c % 4 == 3:
                ring_release(s, "u%d" % (c // 4))

        def A_v(s, j):
            G = s * NT + j
            g = G % NS
            r = g % 2
            gv = F32A[r]
            for half in range(2):
                rb = ring_take(s, "v%d" % half)
                b = bank()
                proj_tm(rb, j, s % 2, b)
                P.op("act", lambda e, b=b, half=half: e.activation(out=gv[:, half * 512:(half + 1) * 512], in_=ps[:, b, :], func=AF.Gelu),
                     reads=[PB(b)], writes=[("F32A", r)])
            for half in range(2):
                P.op("dve", lambda e, half=half: e.bn_stats(out=bst[:, g * 12 + half * 6:g * 12 + half * 6 + 6],
                                                            in_=gv[:, half * 512:(half + 1) * 512]),
                     reads=[("F32A", r)], writes=[("bst", g, half)])
            P.op("dve", lambda e: e.bn_aggr(out=mvv[:, g * 2:g * 2 + 2], in_=bst[:, g * 12:g * 12 + 12]),
                 reads=[("bst", g, 0), ("bst", g, 1)], writes=[("mvv", g)])
            rstd_chain(mvv[:, g * 2 + 1:g * 2 + 2], rsv[:, g:g + 1], 1.0, LN_EPS, ("mvv", g), ("rsv", g))
            P.op("dve", lambda e: e.scalar_tensor_tensor(out=gv[:], in0=gv[:], scalar=mvv[:, g * 2:g * 2 + 1], in1=Gln[:],
                                                         op0=ALU.subtract, op1=ALU.mult),
                 reads=[("F32A", r), ("mvv", g), "Gln"], writes=[("F32A", r)])
            P.op("dve", lambda e: e.scalar_tensor_tensor(out=BFs[r][:], in0=gv[:], scalar=rsv[:, g:g + 1], in1=Bln[:],
                                                         op0=ALU.mult, op1=ALU.add),
                 reads=[("F32A", r), ("rsv", g), "Bln"], writes=[("BFs", r)])
            if j == NT - 1:
                ring_release(s, "v0")
                ring_release(s, "v1")

        def A_sp(s, j):
            G = s * NT + j
            g = G % NS
            r = g % 2
            b2 = bank_pair()
            for h in range(8):
                bb = b2 + h // 4
                o = ps[:, bb, (h % 4) * 128:(h % 4 + 1) * 128]
                P.op("pe", lambda e, o=o, h=h: e.matmul(o, lhsT=BFs[r][:, h * 128:(h + 1) * 128], rhs=WsT[:, h, :], start=True, stop=False),
                     reads=[("BFs", r), "WsT"], writes=[PB(bb)])
                P.op("pe", lambda e, o=o, h=h: e.matmul(o, lhsT=onespad[:], rhs=bspad[:, h * 128:(h + 1) * 128], start=False, stop=True),
                     reads=["onespad", "bspad"], writes=[PB(bb)])
            for half in range(2):
                bb = b2 + half
                P.op("dve", lambda e, bb=bb, half=half: e.tensor_tensor(
                    out=guT[:, half * 4:(half + 1) * 4, j * 128:(j + 1) * 128],
                    in0=ps[:, bb, :].rearrange("p (h t) -> p h t", h=4),
                    in1=guT[:, half * 4:(half + 1) * 4, j * 128:(j + 1) * 128], op=ALU.mult),
                    reads=[PB(bb)] + [("guT", c) for c in range(half * 4, half * 4 + 4)],
                    writes=[("guT", c) for c in range(half * 4, half * 4 + 4)])

        def B_lr(s):
            hbuf = s % 2
            lb = s % 2
            b = bank()
            for kc in range(KC):
                P.op("pe", lambda e, kc=kc: e.matmul(ps[0:16, b, 0:TT], lhsT=wlr[:, kc, :], rhs=hT[hbuf][:, kc, :],
                                                     start=(kc == 0), stop=(kc == KC - 1)),
                     reads=["wlr", ("hT", hbuf)], writes=[PB(b)])
            P.op("dve", lambda e: e.tensor_copy(out=lrT[lb][0:16, :], in_=ps[0:16, b, 0:TT]), reads=[PB(b)], writes=[("lrT", lb)])

        def B_logit(s, j):
            G = s * NT + j
            g = G % NS
            lb = s % 2
            ev = F32B[0]
            ls = F32B[1 + g % 2]
            lskey = ("F32B", 1 + g % 2)
            b = bank()
            P.op("pe", lambda e: e.matmul(ps[:, b, :], lhsT=lrT[lb][:, j * 128:(j + 1) * 128], rhs=wgu[:], start=True, stop=True),
                 reads=[("lrT", lb), "wgu"], writes=[PB(b)])
            P.op("act", lambda e: e.activation(out=ev[:], in_=ps[:, b, :], func=AF.Exp, scale=-1.0),
                 reads=[PB(b)], writes=[("F32B", 0)])
            P.op("act", lambda e: e.activation(out=ls[:], in_=ev[:], func=AF.Ln, bias=1.0, scale=1.0),
                 reads=[("F32B", 0)], writes=[lskey])

        def B_cum(s, j):
            G = s * NT + j
            g = G % NS
            ls = F32B[1 + g % 2]
            lskey = ("F32B", 1 + g % 2)
            bc = bank()
            for h in range(4):
                P.op("pe", lambda e, h=h: e.matmul(ps[:, bc, h * 128:(h + 1) * 128], lhsT=ls[:, h * 128:(h + 1) * 128], rhs=Uneg,
                                                   start=True, stop=True),
                     reads=[lskey, "cst"], writes=[PB(bc)])
            bv = ps[:, bc, :].rearrange("p (h t) -> p h t", h=4)
            sl = slice(g * 4, g * 4 + 4)
            P.op("dve", lambda e: e.tensor_copy(out=pbm[:, sl], in_=bv[:, :, 63]), reads=[PB(bc)], writes=[("pbm", g)])
            P.op("dve", lambda e: e.tensor_scalar(out=nbm[:, sl], in0=bv[:, :, 63], scalar1=-1.0, scalar2=None, op0=ALU.mult),
                 reads=[PB(bc)], writes=[("nbm", g)])
            P.op("dve", lambda e: e.tensor_tensor(out=dlt[:, sl], in0=bv[:, :, 127], in1=pbm[:, sl], op=ALU.subtract),
                 reads=[PB(bc), ("pbm", g)], writes=[("dlt", g)])
            for h in range(4):
                P.op("act", lambda e, h=h: e.activation(out=Et[:, h, j * 128:(j + 1) * 128], in_=ps[:, bc, h * 128:(h + 1) * 128],
                                                        func=AF.Exp, bias=nbm[:, g * 4 + h:g * 4 + h + 1], scale=1.0),
                     reads=[PB(bc), ("nbm", g)], writes=[("Et", h)])
                P.op("act", lambda e, h=h: e.activation(out=Einv[:, h, j * 128:(j + 1) * 128], in_=ps[:, bc, h * 128:(h + 1) * 128],
                                                        func=AF.Exp, bias=pbm[:, g * 4 + h:g * 4 + h + 1], scale=-1.0),
                     reads=[PB(bc), ("pbm", g)], writes=[("Einv", h)])
            P.op("act", lambda e: e.activation(out=emid[:, sl], in_=pbm[:, sl], func=AF.Exp), reads=[("pbm", g)], writes=[("emid", g)])
            P.op("act", lambda e: e.activation(out=elast[:, sl], in_=bv[:, :, 127], func=AF.Exp), reads=[PB(bc)], writes=[("elast", g)])
            P.op("act", lambda e: e.activation(out=edl[:, sl], in_=dlt[:, sl], func=AF.Exp), reads=[("dlt", g)], writes=[("edl", g)])

        def B_q(s, h):
            rb = ring_take(s, "q")
            b = bank()
            proj_fm(rb, h * 128, 128, s % 2, b)
            P.op("dve", lambda e: e.scalar_tensor_tensor(out=qiT[:, h, :], in0=ps[:, b, 0:TT], scalar=float(128 ** -0.5), in1=Et[:, h, :],
                                                         op0=ALU.mult, op1=ALU.mult),
                 reads=[PB(b), ("Et", h)], writes=[("qiT", h)])
            if h == 3:
                ring_release(s, "q")

        def B_k(s, h):
            rb = ring_take(s, "k")
            b = bank()
            proj_fm(rb, h * 128, 128, s % 2, b)
            P.op("dve", lambda e: e.tensor_tensor(out=kiT[:, h, :], in0=ps[:, b, 0:TT], in1=Einv[:, h, :], op=ALU.mult),
                 reads=[PB(b), ("Einv", h)], writes=[("kiT", h)])
            if h == 3:
                ring_release(s, "k")

        def B_tm(s, j, name, dst, func, key):
            for half in range(2):
                rb = ring_take(s, "%s%d" % (name, half))
                b = bank()
                proj_tm(rb, j, s % 2, b)
                P.op("act", lambda e, b=b, half=half: e.activation(out=dst[j][:, half * 512:(half + 1) * 512], in_=ps[:, b, :], func=func),
                     reads=[PB(b)], writes=[(key, j)])
            if j == NT - 1:
                ring_release(s, name + "0")
                ring_release(s, name + "1")

        def B_vb(s, j):
            B_tm(s, j, "vb", vb, AF.Copy, "vb")

        def B_zb(s, j):
            B_tm(s, j, "zb", szb, AF.Silu, "szb")

        rbank = {}

        def R_sc(s, j):
            G = s * NT + j
            g = G % NS
            r = g % 2
            js = slice(j * 128, (j + 1) * 128)
            P.op("dve", lambda e: e.tensor_tensor(out=Sp[r][:], in0=S[:], in1=emid[:, g * 4:g * 4 + 4].unsqueeze(2).to_broadcast([128, 4, 256]), op=ALU.mult),
                 reads=["S", ("emid", g)], writes=[("Sp", r)])
            bt = bank()
            for h in range(4):
                P.op("pe", lambda e, h=h: e.matmul(ps[:, bt, h * 128:(h + 1) * 128], lhsT=kiT[:, h, js], rhs=ident[:], start=True, stop=True),
                     reads=[("kiT", h), "ident"], writes=[PB(bt)])
            P.op("act", lambda e: e.activation(out=ktm[r][:], in_=ps[:, bt, :], func=AF.Copy), reads=[PB(bt)], writes=[("ktm", r)])
            bs_ = bank()
            for h in range(4):
                P.op("pe", lambda e, h=h: e.matmul(ps[:, bs_, h * 128:(h + 1) * 128], lhsT=kiT[:, h, js], rhs=qiT[:, h, js], start=True, stop=True),
                     reads=[("kiT", h), ("qiT", h)], writes=[PB(bs_)])
            P.op("dve", lambda e: e.tensor_tensor(out=scT[r][:], in0=ps[:, bs_, :].rearrange("p (h t) -> p h t", h=4),
                                                  in1=maskf.unsqueeze(1).to_broadcast([128, 4, 128]), op=ALU.mult),
                 reads=[PB(bs_), "cst"], writes=[("scT", r)])

        def R_o1(s, j):
            G = s * NT + j
            g = G % NS
            r = g % 2
            bo = bank_pair()
            rbank[g] = bo
            pstate["held"].update((bo, bo + 1))
            for h in range(4):
                bb = bo + h // 2
                o = ps[:, bb, (h % 2) * 256:(h % 2 + 1) * 256]
                P.op("pe", lambda e, o=o, h=h: e.matmul(o, lhsT=scT[r][:, h, :], rhs=vb[j][:, h * 256:(h + 1) * 256], start=(h % 2 == 0), stop=False,
                                                        skip_group_check=True),
                     reads=[("scT", r), ("vb", j)], writes=[PB(bb)])
            bk = bank_pair()
            for h in range(4):
                bb = bk + h // 2
                o = ps[:, bb, (h % 2) * 256:(h % 2 + 1) * 256]
                P.op("pe", lambda e, o=o, h=h: e.matmul(o, lhsT=ktm[r][:, h * 128:(h + 1) * 128], rhs=vb[j][:, h * 256:(h + 1) * 256], start=True, stop=True),
                     reads=[("ktm", r), ("vb", j)], writes=[PB(bb)])
            for h in range(4):
                bb = bk + h // 2
                o = ps[:, bb, (h % 2) * 256:(h % 2 + 1) * 256]
                P.op("act", lambda e, o=o, h=h: e.activation(out=tkv4[:, h, :], in_=o, func=AF.Copy, scale=edl[:, g * 4 + h:g * 4 + h + 1]),
                     reads=[PB(bb), ("edl", g)], writes=[("tkv", h)])
            for h in range(4):
                P.op("dve", lambda e, h=h: e.scalar_tensor_tensor(out=S[:, h, :], in0=S[:, h, :], scalar=elast[:, g * 4 + h:g * 4 + h + 1],
                                                                 in1=tkv4[:, h, :], op0=ALU.mult, op1=ALU.add),
                     reads=["S", ("elast", g), ("tkv", h)], writes=["S"])

        def R_o2(s, j):
            G = s * NT + j
            g = G % NS
            r = g % 2
            js = slice(j * 128, (j + 1) * 128)
            bo = rbank[g]
            for h in range(4):
                bb = bo + h // 2
                o = ps[:, bb, (h % 2) * 256:(h % 2 + 1) * 256]
                P.op("pe", lambda e, o=o, h=h: e.matmul(o, lhsT=qiT[:, h, js], rhs=Sp[r][:, h, :], start=False, stop=True, skip_group_check=True),
                     reads=[("qiT", h), ("Sp", r)], writes=[PB(bb)])
            for h in range(4):
                bb = bo + h // 2
                o = ps[:, bb, (h % 2) * 256:(h % 2 + 1) * 256]
                P.op("act", lambda e, o=o, h=h: e.activation(out=junk[:, h * 256:(h + 1) * 256], in_=o, func=AF.Square, accum_out=sso[:, g * 4 + h:g * 4 + h + 1]),
                     reads=[PB(bb)], writes=[("sso", g, h), ("junk", h)])
            rstd_chain(sso[:, g * 4:g * 4 + 4], rso[:, g * 4:g * 4 + 4], 1.0 / 256, EPS, [("sso", g, h) for h in range(4)], ("rso", g), w=4)

        def R_on(s, j):
            G = s * NT + j
            g = G % NS
            r = g % 2
            bo = rbank[g]
            for h in range(4):
                bb = bo + h // 2
                o = ps[:, bb, (h % 2) * 256:(h % 2 + 1) * 256]
                P.op("dve", lambda e, o=o, h=h: e.scalar_tensor_tensor(out=BFs[r][:, h * 256:(h + 1) * 256], in0=o, scalar=rso[:, g * 4 + h:g * 4 + h + 1],
                                                                      in1=zgs[j][:, h * 256:(h + 1) * 256], op0=ALU.mult, op1=ALU.mult),
                     reads=[PB(bb), ("rso", g), ("szb", j)], writes=[("BFs", r)])
            pstate["held"].difference_update((bo, bo + 1))

        def R_tr(s, j):
            G = s * NT + j
            g = G % NS
            r = g % 2
            js = slice(j * 128, (j + 1) * 128)
            bt2 = bank()
            pv = ps[:, bt2, :].bitcast(BF16)
            for c in range(KC):
                P.op("pe", lambda e, c=c: e.transpose(pv[:, c * 128:(c + 1) * 128], BFs[r][:, c * 128:(c + 1) * 128], ident[:]),
                     reads=[("BFs", r), "ident"], writes=[PB(bt2)])
            P.op("act", lambda e: e.activation(out=onT[:, :, js], in_=pv[:, 0:1024].rearrange("p (c t) -> p c t", c=8), func=AF.Copy),
                 reads=[PB(bt2)], writes=["onT"])

        def M_a(s, c):
            hbuf = s % 2
            half, n = c // 4, c % 4
            r = c % 2
            rga = ring_take(s, "ga%d" % half)
            rwa = ring_take(s, "wa%d" % half)
            rgb = ring_take(s, "gb%d" % half)
            t1 = F32A[r][:, 0:512]
            sga = F32B[0]
            sgb = F32B[1]
            b = bank()
            proj_fm(rga, n * 128, 128, hbuf, b)
            P.op("act", lambda e: e.activation(out=sga[:], in_=ps[:, b, 0:TT], func=AF.Sigmoid), reads=[PB(b)], writes=[("F32B", 0)])
            b2 = bank()
            for kc in range(KC):
                P.op("pe", lambda e, kc=kc: e.matmul(ps[:, b2, 0:TT], lhsT=ring[rwa][:, kc, n * 128:(n + 1) * 128], rhs=guT[:, kc, :],
                                                     start=(kc == 0), stop=(kc == KC - 1)),
                     reads=[("ring", rwa), ("guT", kc)], writes=[PB(b2)])
            P.op("dve", lambda e: e.tensor_tensor(out=t1, in0=ps[:, b2, 0:TT], in1=sga[:], op=ALU.mult),
                 reads=[PB(b2), ("F32B", 0), ("F32A", r)], writes=[("F32A", r, 0)])
            b3 = bank()
            proj_fm(rgb, n * 128, 128, hbuf, b3)
            P.op("act", lambda e: e.activation(out=sgb[:], in_=ps[:, b3, 0:TT], func=AF.Sigmoid), reads=[PB(b3)], writes=[("F32B", 1)])

        def M_b(s, c):
            half, n = c // 4, c % 4
            r = c % 2
            rwb = ring_take(s, "wb%d" % half)
            t1 = F32A[r][:, 0:512]
            t2 = F32A[r][:, 512:1024]
            sgb = F32B[1]
            b4 = bank()
            for kc in range(KC):
                P.op("pe", lambda e, kc=kc: e.matmul(ps[:, b4, 0:TT], lhsT=ring[rwb][:, kc, n * 128:(n + 1) * 128], rhs=onT[:, kc, :],
                                                     start=(kc == 0), stop=(kc == KC - 1)),
                     reads=[("ring", rwb), "onT"], writes=[PB(b4)])
            P.op("dve", lambda e: e.tensor_tensor(out=t2, in0=ps[:, b4, 0:TT], in1=sgb[:], op=ALU.mult),
                 reads=[PB(b4), ("F32B", 1), ("F32A", r)], writes=[("F32A", r, 1)])
            P.op("dve", lambda e: e.tensor_tensor(out=mT[:, c, :], in0=t1, in1=t2, op=ALU.add),
                 reads=[("F32A", r, 0), ("F32A", r, 1)], writes=[("mT", c)])
            if n == 3:
                for nm in ("ga", "wa", "gb", "wb"):
                    ring_release(s, "%s%d" % (nm, half))

        def M_n(s, c):
            M_a(s, c)
            M_b(s, c)

        def Y_load(s, j):
            G = s * NT + j
            g = G % NS
            r = g % 2
            P.dma("sync", lambda e: e.dma_start(out=xr[r][:], in_=x[G * 128:(G + 1) * 128, :]), "xr%d" % r, writes=[("xr", r)])

        def Y(s, j):
            G = s * NT + j
            g = G % NS
            r = g % 2
            js = slice(j * 128, (j + 1) * 128)
            by = bank_pair()
            for half in range(2):
                rw = ring_take(s, "wo%d" % half)
                for kc in range(KC):
                    P.op("pe", lambda e, kc=kc, half=half, rw=rw: e.matmul(ps[:, by + half, :], lhsT=mT[:, kc, js], rhs=ring[rw][:, kc, :],
                                                                          start=(kc == 0), stop=(kc == KC - 1)),
                         reads=[("ring", rw), ("mT", kc)], writes=[PB(by + half)])
            P.op("dve", lambda e: e.tensor_tensor(out=xr[r][:], in0=ps[:, by:by + 2, :].rearrange("p a b -> p (a b)"), in1=xr[r][:], op=ALU.add),
                 reads=[PB(by), PB(by + 1), ("xr", r)], writes=[("xr", r)])
            P.op("act", lambda e: e.activation(out=junk[:], in_=xr[r][:], func=AF.Square, accum_out=ssf[:, g:g + 1]),
                 reads=[("xr", r)], writes=[("ssf", g)] + JK)
            rstd_chain(ssf[:, g:g + 1], rsf[:, g:g + 1], 1.0 / D, EPS, ("ssf", g), ("rsf", g))
            P.op("dve", lambda e: e.scalar_tensor_tensor(out=xr[r][:], in0=xr[r][:], scalar=rsf[:, g:g + 1], in1=Gf[:], op0=ALU.mult, op1=ALU.mult),
                 reads=[("xr", r), ("rsf", g), "Gf"], writes=[("xr", r)])
            P.dma("sync", lambda e: e.dma_start(out=out[G * 128:(G + 1) * 128, :], in_=xr[r][:]), "st%d" % r, reads=[("xr", r)], final=True)
            if j == NT - 1:
                ring_release(s, "wo0")
                ring_release(s, "wo1")

        zgs = szb

        def B_zg(s, j):
            for h in range(4):
                P.op("dve", lambda e, h=h: e.tensor_tensor(out=szb[j][:, h * 256:(h + 1) * 256], in0=szb[j][:, h * 256:(h + 1) * 256], in1=ggb[:], op=ALU.mult),
                     reads=[("szb", j), "ggb"], writes=[("szb", j)])

        for j in range(NT):
            fr_a(0, j)
            fr_b(0, j)
        Wraw = F32A[0][:, :].rearrange("p (h s) -> p h s", h=8)
        cload(Wraw, w_sp.rearrange("h t s -> t h s"), "c_wsp", writes=[("F32A", 0)])
        for hh in range(2):
            b = bank()
            for h4 in range(4):
                h = hh * 4 + h4
                P.op("pe", lambda e, b=b, h=h, h4=h4: e.matmul(ps[:, b, h4 * 128:(h4 + 1) * 128], lhsT=Wraw[:, h, :], rhs=identf,
                                                              start=True, stop=True),
                     reads=[("F32A", 0), "cst"], writes=[PB(b)])
            for h4 in range(4):
                h = hh * 4 + h4
                P.op("dve", lambda e, b=b, h=h, h4=h4: e.tensor_tensor(out=WsT[:, h, :], in0=ps[:, b, h4 * 128:(h4 + 1) * 128], in1=maskf,
                                                                      op=ALU.mult),
                     reads=[PB(b), "cst"], writes=["WsT"])
        bsf = F32A[1]
        cload(bsf[0:1, :], b_sp[0:1, :], "c_bs0", writes=[("F32A", 1)])
        cload(bsf[32:33, :], b_sp[0:1, :], "c_bs1", writes=[("F32A", 1)])
        P.op("dve", lambda e: e.tensor_copy(out=bspad[0:1, :], in_=bsf[0:1, :]), reads=[("F32A", 1)], writes=["bspad"])
        P.op("dve", lambda e: e.tensor_copy(out=BFs[0][32:33, :], in_=bsf[32:33, :]), reads=[("F32A", 1)], writes=[("BFs", 0)])
        P.op("dve", lambda e: e.tensor_tensor(out=bspad[32:33, :], in0=bsf[32:33, :], in1=BFs[0][32:33, :], op=ALU.subtract),
             reads=[("F32A", 1), ("BFs", 0)], writes=["bspad"])

        bcast_row(Gln, ln_g[0:1, :], D, xr[1], ("xr", 1), "c_gln", "Gln")
        bcast_row(Bln, ln_b[0:1, :], D, xr[0], ("xr", 0), "c_bln", "Bln")
        bcast_row(ggb, gla_g[0:1, :], 256, xr[1], ("xr", 1), "c_ggb", "ggb")
        bcast_row(Gf, fin_g[0:1, :], D, xr[0], ("xr", 0), "c_gf", "Gf")
        P.op("dve", lambda e: e.tensor_scalar(out=Gf[:], in0=Gf[:], scalar1=float(D ** 0.5), scalar2=None, op0=ALU.mult), reads=["Gf"], writes=["Gf"])
        P.op("dve", lambda e: e.tensor_scalar(out=ggb[:], in0=ggb[:], scalar1=16.0, scalar2=None, op0=ALU.mult), reads=["ggb"], writes=["ggb"])
        ring_pump()
        for s in range(NST):
            last = (s + 1 == NST)
            if s == 0:
                B_lr(s)
                B_logit(s, 0); B_logit(s, 1)
            B_vb(s, 0)
            B_cum(s, 0); B_cum(s, 1)
            B_logit(s, 2); B_logit(s, 3)
            B_vb(s, 1)
            B_vb(s, 2)
            B_cum(s, 2); B_cum(s, 3)
            B_vb(s, 3)
            B_zb(s, 0); B_zb(s, 1)
            for h in range(4):
                B_q(s, h)
            for h in range(4):
                B_k(s, h)
            B_zb(s, 2); B_zb(s, 3)
            for c in range(8):
                A_za(s, c)
            for j in range(NT):
                B_zg(s, j)
            for j in range(NT):
                R_sc(s, j)
                if j > 0:
                    A_sp(s, j - 1)
                    R_on(s, j - 1)
                A_v(s, j)
                R_o1(s, j)
                if j > 0:
                    R_tr(s, j - 1)
                A_u(s, 2 * j)
                A_u(s, 2 * j + 1)
                R_o2(s, j)
            A_sp(s, NT - 1)
            R_on(s, NT - 1)
            M_a(s, 0)
            R_tr(s, NT - 1)
            seq = {0: ("a", 0), 1: ("a", 1), 2: ("b", 0), 3: ("a", 2), 4: ("b", 1), 5: ("a", 3), 6: ("b", 2), 7: ("b", 3)}
            for c in range(8):
                if c > 0:
                    M_a(s, c)
                M_b(s, c)
                if not last:
                    kind, jj = seq[c]
                    if kind == "a":
                        fr_a(s + 1, jj)
                    else:
                        fr_b(s + 1, jj)
                    if c == 2:
                        pass
                if c >= 4:
                    Y_load(s, c - 4) if c - 4 < 2 else None
            if not last:
                B_lr(s + 1)
            for j in range(NT):
                Y(s, j)
                if j + 2 < NT:
                    Y_load(s, j + 2)
                if not last and j < 2:
                    B_logit(s + 1, j)
            if dbg and s == 0:
                P.dma("sync", lambda e: e.dma_start(out=d_hT[:, :, :], in_=hT[0][:]), "dbg0", reads=[("hT", 0)], final=True)
                P.dma("sync", lambda e: e.dma_start(out=d_aT[:, :, :], in_=guT[:]), "dbg1", reads=[("guT", c) for c in range(8)], final=True)
                P.dma("sync", lambda e: e.dma_start(out=d_onT[:, :, :], in_=onT[:]), "dbg2", reads=["onT"], final=True)
                P.dma("sync", lambda e: e.dma_start(out=d_mT[:, :, :], in_=mT[:]), "dbg3", reads=[("mT", c) for c in range(8)], final=True)

        print("sbuf bytes remaining", nc.sbuf_bytes_remaining)
        block = es.enter_context(nc.Block())
        P.emit(block)
    return nc


def _consts():
    c = np.zeros((128, 512), np.float32)
    c[0, 384:512] = 1.0
    c[:, 0:128] = np.eye(128, dtype=np.float32)
    tri = (np.arange(128)[:, None] <= np.arange(128)[None, :]).astype(np.float32)
    c[:, 128:256] = tri * np.float32(-1.0 / 16.0)
    c[:, 256:384] = tri
    return c


def make_in_maps(x, norm_g, w_in, ln_v_g, ln_v_b, w_spatial, b_spatial, w_gate_up, b_gate_up,
                 gla_norm_g, w_branch_a, w_branch_b, w_out, final_norm_g, n_cores, T):
    f = lambda a: np.ascontiguousarray(np.asarray(a, dtype=np.float32))
    shared = {
        "w_in": f(w_in[0]), "w_a": f(w_branch_a[0]), "w_b": f(w_branch_b[0]), "w_o": f(w_out[0]),
        "norm_g": f(norm_g[0]).reshape(1, D), "ln_g": f(ln_v_g[0]).reshape(1, D), "ln_b": f(ln_v_b[0]).reshape(1, D),
        "w_sp": f(w_spatial[0]), "b_sp": f(b_spatial[0]).reshape(1, 1024),
        "w_gu": f(w_gate_up[0]), "b_gu": f(b_gate_up[0]).reshape(1, 512),
        "gla_g": f(gla_norm_g[0]).reshape(1, 256), "fin_g": f(final_norm_g).reshape(1, D),
        "consts": _consts(),
    }
    maps = []
    for b in range(n_cores):
        m = dict(shared)
        m["x"] = f(np.asarray(x)[b, :T])
        maps.append(m)
    return maps


_NC_CACHE = {}


def kernel(x, norm_g, w_in, ln_v_g, ln_v_b, w_spatial, b_spatial, w_gate_up, b_gate_up,
           gla_norm_g, w_branch_a, w_branch_b, w_out, final_norm_g):
    x = np.asarray(x)
    B, T, _ = x.shape
    nc = build_program(T)
    in_maps = make_in_maps(x, norm_g, w_in, ln_v_g, ln_v_b, w_spatial, b_spatial, w_gate_up, b_gate_up,
                           gla_norm_g, w_branch_a, w_branch_b, w_out, final_norm_g, B, T)
    res = run_bass_kernel_spmd(nc, in_maps, core_ids=list(range(B)))
    return np.stack([np.asarray(r["out"], dtype=np.float32) for r in res.results], axis=0)
```

```python
import numpy as np
from contextlib import ExitStack
import concourse.bass as bass
import concourse.mybir as mybir
from concourse.bass_utils import run_bass_kernel_spmd

F32 = mybir.dt.float32
BF16 = mybir.dt.bfloat16
AF = mybir.ActivationFunctionType
ALU = mybir.AluOpType

D = 1024
NIN = 8208
NT = 4
TT = NT * 128
KC = 8
NB = 5
NS = 8
EPS = 1e-6
LN_EPS = 1e-5
C_U, C_V, C_ZA, C_Q, C_K, C_VB, C_ZB, C_LR, C_GA, C_GB = 0, 1024, 2048, 3072, 3584, 4096, 5120, 6144, 6160, 7184


class Prog:
    ENGS = ("pe", "act", "dve", "pool", "sync")

    def __init__(self, nc, es):
        self.nc = nc
        self.es = es
        self.ops = {e: [] for e in self.ENGS}
        self.cnt = {e: 0 for e in self.ENGS}
        self.sem = {e: es.enter_context(nc.semaphore("sem_" + e)) for e in ("pe", "act", "dve", "pool")}
        self.dsem = {}
        self.dcnt = {}
        self.res = {}
        self.known = {e: {} for e in self.ENGS}
        self.final_waits = []

    def _deps(self, reads, writes, eng=None):
        deps = []
        for k in reads:
            r = self.res.get(k)
            if r and r["w"] is not None:
                deps.append(r["w"])
            if r and isinstance(k, tuple) and k[0] == "ps":
                deps.extend(ev for ev in r["r"] if ev[0] != eng)
        for k in writes:
            r = self.res.get(k)
            if r:
                if r["w"] is not None:
                    deps.append(r["w"])
                deps.extend(r["r"])
        return deps

    def _record(self, ev, reads, writes):
        for k in reads:
            r = self.res.setdefault(k, {"w": None, "r": []})
            r["r"].append(ev)
        for k in writes:
            self.res[k] = {"w": ev, "r": []}

    def _waits(self, eng, deps):
        best = {}
        for (semname, val) in deps:
            if eng == "pe" and semname == "pe":
                continue
            if val > best.get(semname, 0):
                best[semname] = val
        waits = []
        kn = self.known[eng]
        for semname, val in best.items():
            if kn.get(semname, 0) >= val:
                continue
            kn[semname] = val
            waits.append((semname, val))
        return waits

    def op(self, eng, fn, reads=(), writes=()):
        deps = self._deps(reads, writes, eng)
        waits = self._waits(eng, deps)
        self.cnt[eng] += 1
        ev = (eng, self.cnt[eng])
        self.ops[eng].append((waits, fn, ev))
        self._record(ev, reads, writes)
        return ev

    def dma(self, eng, fn, semkey, reads=(), writes=(), final=False):
        if semkey not in self.dsem:
            self.dsem[semkey] = self.es.enter_context(self.nc.semaphore("d_" + semkey))
            self.dcnt[semkey] = 0
        deps = self._deps(reads, writes)
        waits = self._waits(eng, deps)
        self.dcnt[semkey] += 16
        ev = ("d:" + semkey, self.dcnt[semkey])
        self.ops[eng].append((waits, fn, ev))
        self._record(ev, reads, writes)
        if final:
            self.final_waits.append(ev)
        return ev

    def _semh(self, name):
        if name.startswith("d:"):
            return self.dsem[name[2:]]
        return self.sem[name]

    def emit(self, block):
        def run(engname):
            def body(e):
                for waits, fn, ev in self.ops[engname]:
                    for (sn, val) in waits:
                        e.wait_ge(self._semh(sn), val)
                    ins = fn(e)
                    ins.then_inc(self._semh(ev[0]), 16 if ev[0].startswith("d:") else 1)
                if engname == "sync":
                    best = {}
                    for sn, val in self.final_waits:
                        best[sn] = max(best.get(sn, 0), val)
                    for sn, val in best.items():
                        e.wait_ge(self._semh(sn), val)
            return body
        block.tensor(run("pe"))
        block.scalar(run("act"))
        block.vector(run("dve"))
        block.gpsimd(run("pool"))
        block.sync(run("sync"))


def build_program(T, dbg=False):
    NST = T // TT
    NTOT = T // 128
    nc = bass.Bass("TRN2", target_bir_lowering=False)

    def din(name, shape):
        return nc.dram_tensor(name, shape, F32, kind="ExternalInput").ap()

    x = din("x", [T, D])
    w_in = din("w_in", [D, NIN])
    w_a = din("w_a", [D, D])
    w_b = din("w_b", [D, D])
    w_o = din("w_o", [D, D])
    norm_g = din("norm_g", [1, D])
    ln_g = din("ln_g", [1, D])
    ln_b = din("ln_b", [1, D])
    w_sp = din("w_sp", [8, 128, 128])
    b_sp = din("b_sp", [1, 1024])
    w_gu = din("w_gu", [16, 512])
    b_gu = din("b_gu", [1, 512])
    gla_g = din("gla_g", [1, 256])
    fin_g = din("fin_g", [1, D])
    consts = din("consts", [128, 512])
    out = nc.dram_tensor("out", [T, D], F32, kind="ExternalOutput").ap()
    if dbg:
        d_hT = nc.dram_tensor("d_hT", [128, KC, TT], BF16, kind="ExternalOutput").ap()
        d_aT = nc.dram_tensor("d_aT", [128, KC, TT], BF16, kind="ExternalOutput").ap()
        d_onT = nc.dram_tensor("d_onT", [128, KC, TT], BF16, kind="ExternalOutput").ap()
        d_mT = nc.dram_tensor("d_mT", [128, KC, TT], BF16, kind="ExternalOutput").ap()

    NCG = 22
    wscr = nc.dram_tensor("wscr", [NCG, 128, KC, 512], BF16).ap()
    w_in_v = w_in.rearrange("(kc p) n -> p kc n", p=128)
    w_a_v = w_a.rearrange("(kc p) n -> p kc n", p=128)
    w_b_v = w_b.rearrange("(kc p) n -> p kc n", p=128)
    w_o_v = w_o.rearrange("(kc p) n -> p kc n", p=128)

    with ExitStack() as es:
        def sb(name, shape, dt):
            return es.enter_context(nc.sbuf_tensor(name, shape, dt))

        P = Prog(nc, es)
        ps = es.enter_context(nc.psum_tensor("ps", [128, 8, 512], F32))

        Gx = sb("Gx", [128, D], F32)
        Gf = sb("Gf", [128, D], F32)
        Gln = sb("Gln", [128, D], F32)
        Bln = sb("Bln", [128, D], F32)
        ggb = sb("ggb", [128, 256], F32)
        cst = sb("cst", [128, 512], F32)
        identf = cst[:, 0:128]
        Uneg = cst[:, 128:256]
        maskf = cst[:, 256:384]
        sel0 = cst[:, 384:512]
        ident = sb("ident", [128, 128], BF16)
        WsT = sb("WsT", [128, 8, 128], BF16)
        bspad = sb("bspad", [128, 1024], BF16)
        onespad = sb("onespad", [128, 128], BF16)
        wgu = sb("wgu", [128, 512], BF16)
        wlr = sb("wlr", [128, KC, 16], BF16)
        ring = [sb("ring%d" % i, [128, KC, 512], BF16) for i in range(NB)]
        hT = [sb("hT%d" % i, [128, KC, TT], BF16) for i in range(2)]
        xt = [sb("xt%d" % i, [128, D], F32) for i in range(2)]
        xr = [sb("xr%d" % i, [128, D], F32) for i in range(2)]
        junk = sb("junk", [128, D], BF16)
        BFs = [sb("BFs%d" % i, [128, D], BF16) for i in range(2)]
        guT = sb("guT", [128, KC, TT], BF16)
        sza = [sb("sza%d" % i, [128, TT], BF16) for i in range(2)]
        F32A = [sb("F32A%d" % i, [128, D], F32) for i in range(2)]
        F32B = [sb("F32B%d" % i, [128, 512], F32) for i in range(3)]
        lrT = [sb("lrT%d" % i, [128, TT], BF16) for i in range(2)]
        Et = sb("Et", [128, 4, TT], F32)
        Einv = sb("Einv", [128, 4, TT], F32)
        qiT = sb("qiT", [128, 4, TT], BF16)
        kiT = sb("kiT", [128, 4, TT], BF16)
        ktm = [sb("ktm%d" % i, [128, 512], BF16) for i in range(2)]
        vb = [sb("vb%d" % i, [128, D], BF16) for i in range(NT)]
        szb = [sb("szb%d" % i, [128, D], BF16) for i in range(NT)]
        scT = [sb("scT%d" % i, [128, 4, 128], BF16) for i in range(2)]
        S = sb("S", [128, 4, 256], F32)
        tkv4 = sb("tkv4", [128, 4, 256], F32)
        Sp = [sb("Sp%d" % i, [128, 4, 256], BF16) for i in range(2)]
        onT = sb("onT", [128, KC, TT], BF16)
        mT = sb("mT", [128, KC, TT], BF16)
        rconst = sb("rconst", [128, 16], F32)
        ssx = sb("ssx", [128, NS], F32)
        rsx = sb("rsx", [128, NS], F32)
        bst = sb("bst", [128, NS * 12], F32)
        mvv = sb("mvv", [128, NS * 2], F32)
        rsv = sb("rsv", [128, NS], F32)
        nbm = sb("nbm", [128, NS * 4], F32)
        pbm = sb("pbm", [128, NS * 4], F32)
        dlt = sb("dlt", [128, NS * 4], F32)
        emid = sb("emid", [128, NS * 4], F32)
        elast = sb("elast", [128, NS * 4], F32)
        edl = sb("edl", [128, NS * 4], F32)
        sso = sb("sso", [128, NS * 4], F32)
        rso = sb("rso", [128, NS * 4], F32)
        ssf = sb("ssf", [128, NS], F32)
        rsf = sb("rsf", [128, NS], F32)

        pstate = {"ptr": 0, "held": set()}

        def bank():
            while pstate["ptr"] % 8 in pstate["held"]:
                pstate["ptr"] += 1
            b = pstate["ptr"] % 8
            pstate["ptr"] += 1
            return b

        def bank_pair():
            while True:
                if pstate["ptr"] % 2 == 1:
                    pstate["ptr"] += 1
                b = pstate["ptr"] % 8
                if b in pstate["held"] or (b + 1) in pstate["held"]:
                    pstate["ptr"] += 2
                    continue
                pstate["ptr"] += 2
                return b

        def PB(b):
            return ("ps", b)

        CG_ORDER = ["vb0", "vb1", "zb0", "zb1", "q", "k", "za0", "za1", "v0", "v1", "u0", "u1",
                    "ga0", "wa0", "gb0", "wb0", "ga1", "wa1", "gb1", "wb1", "wo0", "wo1"]
        CG_SRC = {"vb0": (w_in_v, C_VB), "vb1": (w_in_v, C_VB + 512), "zb0": (w_in_v, C_ZB), "zb1": (w_in_v, C_ZB + 512),
                  "q": (w_in_v, C_Q), "k": (w_in_v, C_K), "v0": (w_in_v, C_V), "v1": (w_in_v, C_V + 512),
                  "u0": (w_in_v, C_U), "u1": (w_in_v, C_U + 512), "za0": (w_in_v, C_ZA), "za1": (w_in_v, C_ZA + 512),
                  "ga0": (w_in_v, C_GA), "ga1": (w_in_v, C_GA + 512), "gb0": (w_in_v, C_GB), "gb1": (w_in_v, C_GB + 512),
                  "wa0": (w_a_v, 0), "wa1": (w_a_v, 512), "wb0": (w_b_v, 0), "wb1": (w_b_v, 512),
                  "wo0": (w_o_v, 0), "wo1": (w_o_v, 512)}
        cgs = []
        cg_index = {}
        for s in range(NST):
            for ci, name in enumerate(CG_ORDER):
                cg_index[(s, name)] = len(cgs)
                cgs.append((s, ci, CG_SRC[name]))
        rstate = {"next_load": 0, "released": set()}

        def ring_pump():
            while rstate["next_load"] < len(cgs):
                i = rstate["next_load"]
                if i >= NB and (i - NB) not in rstate["released"]:
                    break
                st, ci, (view, c0) = cgs[i]
                buf = ring[i % NB]
                if st == 0:
                    P.dma("pool", lambda e, buf=buf, view=view, c0=c0: e.dma_start(out=buf[:], in_=view[:, :, c0:c0 + 512]),
                          "rp%d" % (i % NB), writes=[("ring", i % NB)])
                    if NST > 1:
                        P.dma("sync", lambda e, buf=buf, ci=ci: e.dma_start(out=wscr[ci], in_=buf[:]),
                              "scr%d" % ci, reads=[("ring", i % NB)], writes=[("scr", ci)])
                else:
                    P.dma("sync", lambda e, buf=buf, ci=ci: e.dma_start(out=buf[:], in_=wscr[ci]),
                          "rs%d" % (i % NB), reads=[("scr", ci)], writes=[("ring", i % NB)])
                rstate["next_load"] += 1

        def ring_take(s, name):
            i = cg_index[(s, name)]
            assert i < rstate["next_load"], ("CG not loaded yet", s, name)
            return i % NB

        def ring_release(s, name):
            rstate["released"].add(cg_index[(s, name)])
            ring_pump()

        def cload(dst, src, key, eng="sync", writes=()):
            P.dma(eng, lambda e: e.dma_start(out=dst, in_=src), key, writes=list(writes))

        cload(cst[:], consts[:, :], "c_cst", writes=["cst"])
        def bcast_row(dst, src_row, W, stg, stgkey, key, dkey):
            P.op("dve", lambda e: e.memset(stg[:, 0:W], 0.0), writes=[stgkey])
            P.dma("sync", lambda e: e.dma_start(out=stg[0:1, 0:W], in_=src_row), key, writes=[stgkey])
            for c0 in range(0, W, 512):
                w = min(512, W - c0)
                b = bank()
                P.op("pe", lambda e, b=b, c0=c0, w=w: e.matmul(ps[:, b, 0:w], lhsT=sel0, rhs=stg[:, c0:c0 + w], start=True, stop=True),
                     reads=[stgkey, "cst"], writes=[PB(b)])
                P.op("act", lambda e, b=b, c0=c0, w=w: e.activation(out=dst[:, c0:c0 + w], in_=ps[:, b, 0:w], func=AF.Copy),
                     reads=[PB(b)], writes=[dkey])

        bcast_row(Gx, norm_g[0:1, :], D, xr[0], ("xr", 0), "c_gx", "Gx")
        P.op("dve", lambda e: e.memset(wgu[:], 0.0), writes=["wgu"])
        P.op("dve", lambda e: e.memset(bspad[:], 0.0), writes=["bspad"])
        P.op("dve", lambda e: e.memset(onespad[:], 0.0), writes=["onespad"])
        P.op("dve", lambda e: e.memset(onespad[0:1, :], 1.0), writes=["onespad"])
        P.op("dve", lambda e: e.memset(onespad[32:33, :], 1.0), writes=["onespad"])
        for i in range(2):
            P.op("dve", lambda e, i=i: e.memset(lrT[i][:], 0.0), writes=[("lrT", i)])
            P.op("dve", lambda e, i=i: e.memset(lrT[i][32:33, :], 1.0), writes=[("lrT", i)])
        P.op("dve", lambda e: e.memset(S[:], 0.0), writes=["S"])
        P.op("dve", lambda e: e.memset(rconst[:, 0:4], float(D * EPS)), writes=["rconst"])
        P.op("dve", lambda e: e.memset(rconst[:, 4:8], float(LN_EPS)), writes=["rconst"])
        P.op("dve", lambda e: e.memset(rconst[:, 8:12], float(256 * EPS)), writes=["rconst"])
        P.op("dve", lambda e: e.memset(rconst[:, 12:16], -0.5), writes=["rconst"])
        P.op("dve", lambda e: e.tensor_scalar(out=Gx[:], in0=Gx[:], scalar1=float(D ** 0.5), scalar2=None, op0=ALU.mult), reads=["Gx"], writes=["Gx"])
        P.op("dve", lambda e: e.tensor_copy(out=ident[:], in_=identf), reads=["cst"], writes=["ident"])
        cload(wgu[0:16, :], w_gu[:, :], "c_wgu", eng="pool", writes=["wgu"])
        cload(wgu[32:33, :], b_gu[0:1, :], "c_bgu", eng="pool", writes=["wgu"])
        cload(wlr[:], w_in_v[:, :, C_LR:C_LR + 16], "c_wlr", eng="pool", writes=["wlr"])
        hn = [sb("hn%d" % i, [128, D], BF16) for i in range(2)]
        JK = [("junk", h) for h in range(4)]

        def rstd_chain(ss_ap, out_ap, scale, eps, key_in, key_out, w=1):
            keys_in = key_in if isinstance(key_in, list) else [key_in]
            ceps = {1.0 / D: 0, 1.0: 1, 1.0 / 256: 2}[scale]
            P.op("pool", lambda e: e.tensor_tensor(out=out_ap, in0=ss_ap, in1=rconst[:, ceps * 4:ceps * 4 + w], op=ALU.add),
                 reads=keys_in + ["rconst"], writes=[key_out])
            P.op("pool", lambda e: e.tensor_tensor(out=out_ap, in0=out_ap, in1=rconst[:, 12:12 + w], op=ALU.pow),
                 reads=[key_out, "rconst"], writes=[key_out])

        def proj_fm(rb, c_off, M, hbuf, b):
            for kc in range(KC):
                P.op("pe", lambda e, kc=kc: e.matmul(ps[0:M, b, 0:TT], lhsT=ring[rb][:, kc, c_off:c_off + M], rhs=hT[hbuf][:, kc, :],
                                                     start=(kc == 0), stop=(kc == KC - 1)),
                     reads=[("ring", rb), ("hT", hbuf)], writes=[PB(b)])

        def proj_tm(rb, j, hbuf, b):
            for kc in range(KC):
                P.op("pe", lambda e, kc=kc: e.matmul(ps[:, b, :], lhsT=hT[hbuf][:, kc, j * 128:(j + 1) * 128], rhs=ring[rb][:, kc, :],
                                                     start=(kc == 0), stop=(kc == KC - 1)),
                     reads=[("ring", rb), ("hT", hbuf)], writes=[PB(b)])

        def fr_a(s, j):
            G = s * NT + j
            g = G % NS
            r = g % 2
            P.dma("sync", lambda e: e.dma_start(out=xt[r][:], in_=x[G * 128:(G + 1) * 128, :]), "xt%d" % r, writes=[("xt", r)])
            P.op("act", lambda e: e.activation(out=junk[:], in_=xt[r][:], func=AF.Square, accum_out=ssx[:, g:g + 1]),
                 reads=[("xt", r)], writes=[("ssx", g)] + JK)
            rstd_chain(ssx[:, g:g + 1], rsx[:, g:g + 1], 1.0 / D, EPS, ("ssx", g), ("rsx", g))
            P.op("dve", lambda e: e.scalar_tensor_tensor(out=hn[r][:], in0=xt[r][:], scalar=rsx[:, g:g + 1], in1=Gx[:],
                                                         op0=ALU.mult, op1=ALU.mult),
                 reads=[("xt", r), ("rsx", g), "Gx"], writes=[("hn", r)])

        def fr_b(s, j):
            G = s * NT + j
            g = G % NS
            r = g % 2
            hbuf = s % 2
            b = bank()
            pv = ps[:, b, :].bitcast(BF16)
            for kc in range(KC):
                P.op("pe", lambda e, kc=kc: e.transpose(pv[:, kc * 128:(kc + 1) * 128], hn[r][:, kc * 128:(kc + 1) * 128], ident[:]),
                     reads=[("hn", r), "ident"], writes=[PB(b)])
            P.op("dve", lambda e: e.tensor_copy(out=hT[hbuf][:, :, j * 128:(j + 1) * 128],
                                                in_=pv[:, 0:1024].rearrange("p (c t) -> p c t", c=8)),
                 reads=[PB(b)], writes=[("hT", hbuf)])

        def A_za(s, c):
            rb = ring_take(s, "za%d" % (c // 4))
            b = bank()
            proj_fm(rb, (c % 4) * 128, 128, s % 2, b)
            P.op("act", lambda e: e.activation(out=guT[:, c, :], in_=ps[:, b, 0:TT], func=AF.Silu),
                 reads=[PB(b)], writes=[("guT", c)])
            if c % 4 == 3:
                ring_release(s, "za%d" % (c // 4))

        def A_u(s, c):
            rb = ring_take(s, "u%d" % (c // 4))
            b = bank()
            proj_fm(rb, (c % 4) * 128, 128, s % 2, b)
            P.op("act", lambda e: e.activation(out=sza[c % 2][:], in_=ps[:, b, 0:TT], func=AF.Gelu),
                 reads=[PB(b)], writes=[("sza", c % 2)])
            P.op("pool", lambda e: e.tensor_tensor(out=guT[:, c, :], in0=guT[:, c, :], in1=sza[c % 2][:], op=ALU.mult),
                 reads=[("guT", c), ("sza", c % 2)], writes=[("guT", c)])
            if c % 4 == 3:
                ring_release(s, "u%d" % (c // 4))

        def A_v(s, j):
            G = s * NT + j
            g = G % NS
            r = g % 2
            gv = F32A[r]
            for half in range(2):
                rb = ring_take(s, "v%d" % half)
                b = bank()
                proj_tm(rb, j, s % 2, b)
                P.op("act", lambda e, b=b, half=half: e.activation(out=gv[:, half * 512:(half + 1) * 512], in_=ps[:, b, :], func=AF.Gelu),
                     reads=[PB(b)], writes=[("F32A", r)])
            for half in range(2):
                P.op("dve", lambda e, half=half: e.bn_stats(out=bst[:, g * 12 + half * 6:g * 12 + half * 6 + 6],
                                                            in_=gv[:, half * 512:(half + 1) * 512]),
                     reads=[("F32A", r)], writes=[("bst", g, half)])
            P.op("dve", lambda e: e.bn_aggr(out=mvv[:, g * 2:g * 2 + 2], in_=bst[:, g * 12:g * 12 + 12]),
                 reads=[("bst", g, 0), ("bst", g, 1)], writes=[("mvv", g)])
            rstd_chain(mvv[:, g * 2 + 1:g * 2 + 2], rsv[:, g:g + 1], 1.0, LN_EPS, ("mvv", g), ("rsv", g))
            P.op("dve", lambda e: e.scalar_tensor_tensor(out=gv[:], in0=gv[:], scalar=mvv[:, g * 2:g * 2 + 1], in1=Gln[:],
                                                         op0=ALU.subtract, op1=ALU.mult),
                 reads=[("F32A", r), ("mvv", g), "Gln"], writes=[("F32A", r)])
            P.op("dve", lambda e: e.scalar_tensor_tensor(out=BFs[r][:], in0=gv[:], scalar=rsv[:, g:g + 1], in1=Bln[:],
                                                         op0=ALU.mult, op1=ALU.add),
                 reads=[("F32A", r), ("rsv", g), "Bln"], writes=[("BFs", r)])
            if j == NT - 1:
                ring_release(s, "v0")
                ring_release(s, "v1")

        def A_sp(s, j):
            G = s * NT + j
            g = G % NS
            r = g % 2
            b2 = bank_pair()
            for h in range(8):
                bb = b2 + h // 4
                o = ps[:, bb, (h % 4) * 128:(h % 4 + 1) * 128]
                P.op("pe", lambda e, o=o, h=h: e.matmul(o, lhsT=BFs[r][:, h * 128:(h + 1) * 128], rhs=WsT[:, h, :], start=True, stop=False),
                     reads=[("BFs", r), "WsT"], writes=[PB(bb)])
                P.op("pe", lambda e, o=o, h=h: e.matmul(o, lhsT=onespad[:], rhs=bspad[:, h * 128:(h + 1) * 128], start=False, stop=True),
                     reads=["onespad", "bspad"], writes=[PB(bb)])
            for half in range(2):
                bb = b2 + half
                P.op("dve", lambda e, bb=bb, half=half: e.tensor_tensor(
                    out=guT[:, half * 4:(half + 1) * 4, j * 128:(j + 1) * 128],
                    in0=ps[:, bb, :].rearrange("p (h t) -> p h t", h=4),
                    in1=guT[:, half * 4:(half + 1) * 4, j * 128:(j + 1) * 128], op=ALU.mult),
                    reads=[PB(bb)] + [("guT", c) for c in range(half * 4, half * 4 + 4)],
                    writes=[("guT", c) for c in range(half * 4, half * 4 + 4)])

        def B_lr(s):
            hbuf = s % 2
            lb = s % 2
            b = bank()
            for kc in range(KC):
                P.op("pe", lambda e, kc=kc: e.matmul(ps[0:16, b, 0:TT], lhsT=wlr[:, kc, :], rhs=hT[hbuf][:, kc, :],
                                                     start=(kc == 0), stop=(kc == KC - 1)),
                     reads=["wlr", ("hT", hbuf)], writes=[PB(b)])
            P.op("dve", lambda e: e.tensor_copy(out=lrT[lb][0:16, :], in_=ps[0:16, b, 0:TT]), reads=[PB(b)], writes=[("lrT", lb)])

        def B_logit(s, j):
            G = s * NT + j
            g = G % NS
            lb = s % 2
            ev = F32B[0]
            ls = F32B[1 + g % 2]
            lskey = ("F32B", 1 + g % 2)
            b = bank()
            P.op("pe", lambda e: e.matmul(ps[:, b, :], lhsT=lrT[lb][:, j * 128:(j + 1) * 128], rhs=wgu[:], start=True, stop=True),
                 reads=[("lrT", lb), "wgu"], writes=[PB(b)])
            P.op("act", lambda e: e.activation(out=ev[:], in_=ps[:, b, :], func=AF.Exp, scale=-1.0),
                 reads=[PB(b)], writes=[("F32B", 0)])
            P.op("act", lambda e: e.activation(out=ls[:], in_=ev[:], func=AF.Ln, bias=1.0, scale=1.0),
                 reads=[("F32B", 0)], writes=[lskey])

        cumbank = {}

        def B_cum_a(s, j):
            G = s * NT + j
            g = G % NS
            ls = F32B[1 + g % 2]
            lskey = ("F32B", 1 + g % 2)
            bc = bank()
            for h in range(4):
                P.op("pe", lambda e, h=h: e.matmul(ps[:, bc, h * 128:(h + 1) * 128], lhsT=ls[:, h * 128:(h + 1) * 128], rhs=Uneg,
                                                   start=True, stop=True),
                     reads=[lskey, "cst"], writes=[PB(bc)])
            cumbank[g] = bc
            pstate["held"].add(bc)

        def B_cum_b(s, j):
            G = s * NT + j
            g = G % NS
            bc = cumbank[g]
            bv = ps[:, bc, :].rearrange("p (h t) -> p h t", h=4)
            sl = slice(g * 4, g * 4 + 4)
            P.op("dve", lambda e: e.tensor_copy(out=pbm[:, sl], in_=bv[:, :, 63]), reads=[PB(bc)], writes=[("pbm", g)])
            P.op("dve", lambda e: e.tensor_scalar(out=nbm[:, sl], in0=bv[:, :, 63], scalar1=-1.0, scalar2=None, op0=ALU.mult),
                 reads=[PB(bc)], writes=[("nbm", g)])
            P.op("dve", lambda e: e.tensor_tensor(out=dlt[:, sl], in0=bv[:, :, 127], in1=pbm[:, sl], op=ALU.subtract),
                 reads=[PB(bc), ("pbm", g)], writes=[("dlt", g)])
            for h in range(4):
                P.op("act", lambda e, h=h: e.activation(out=Et[:, h, j * 128:(j + 1) * 128], in_=ps[:, bc, h * 128:(h + 1) * 128],
                                                        func=AF.Exp, bias=nbm[:, g * 4 + h:g * 4 + h + 1], scale=1.0),
                     reads=[PB(bc), ("nbm", g)], writes=[("Et", h)])
                P.op("act", lambda e, h=h: e.activation(out=Einv[:, h, j * 128:(j + 1) * 128], in_=ps[:, bc, h * 128:(h + 1) * 128],
                                                        func=AF.Exp, bias=pbm[:, g * 4 + h:g * 4 + h + 1], scale=-1.0),
                     reads=[PB(bc), ("pbm", g)], writes=[("Einv", h)])
            P.op("act", lambda e: e.activation(out=emid[:, sl], in_=pbm[:, sl], func=AF.Exp), reads=[("pbm", g)], writes=[("emid", g)])
            P.op("act", lambda e: e.activation(out=elast[:, sl], in_=bv[:, :, 127], func=AF.Exp), reads=[PB(bc)], writes=[("elast", g)])
            P.op("act", lambda e: e.activation(out=edl[:, sl], in_=dlt[:, sl], func=AF.Exp), reads=[("dlt", g)], writes=[("edl", g)])
            pstate["held"].discard(bc)

        def B_q(s, h):
            rb = ring_take(s, "q")
            b = bank()
            proj_fm(rb, h * 128, 128, s % 2, b)
            P.op("dve", lambda e: e.scalar_tensor_tensor(out=qiT[:, h, :], in0=ps[:, b, 0:TT], scalar=float(128 ** -0.5), in1=Et[:, h, :],
                                                         op0=ALU.mult, op1=ALU.mult),
                 reads=[PB(b), ("Et", h)], writes=[("qiT", h)])
            if h == 3:
                ring_release(s, "q")

        def B_k(s, h):
            rb = ring_take(s, "k")
            b = bank()
            proj_fm(rb, h * 128, 128, s % 2, b)
            P.op("dve", lambda e: e.tensor_tensor(out=kiT[:, h, :], in0=ps[:, b, 0:TT], in1=Einv[:, h, :], op=ALU.mult),
                 reads=[PB(b), ("Einv", h)], writes=[("kiT", h)])
            if h == 3:
                ring_release(s, "k")

        def B_tm(s, j, name, dst, func, key):
            for half in range(2):
                rb = ring_take(s, "%s%d" % (name, half))
                b = bank()
                proj_tm(rb, j, s % 2, b)
                P.op("act", lambda e, b=b, half=half: e.activation(out=dst[j][:, half * 512:(half + 1) * 512], in_=ps[:, b, :], func=func),
                     reads=[PB(b)], writes=[(key, j)])
            if j == NT - 1:
                ring_release(s, name + "0")
                ring_release(s, name + "1")

        def B_vb(s, j):
            B_tm(s, j, "vb", vb, AF.Copy, "vb")

        def B_zb(s, j):
            B_tm(s, j, "zb", szb, AF.Silu, "szb")

        rbank = {}

        def R_sc(s, j):
            G = s * NT + j
            g = G % NS
            r = g % 2
            js = slice(j * 128, (j + 1) * 128)
            P.op("dve", lambda e: e.tensor_tensor(out=Sp[r][:], in0=S[:], in1=emid[:, g * 4:g * 4 + 4].unsqueeze(2).to_broadcast([128, 4, 256]), op=ALU.mult),
                 reads=["S", ("emid", g)], writes=[("Sp", r)])
            bt = bank()
            for h in range(4):
                P.op("pe", lambda e, h=h: e.matmul(ps[:, bt, h * 128:(h + 1) * 128], lhsT=kiT[:, h, js], rhs=ident[:], start=True, stop=True),
                     reads=[("kiT", h), "ident"], writes=[PB(bt)])
            P.op("act", lambda e: e.activation(out=ktm[r][:], in_=ps[:, bt, :], func=AF.Copy), reads=[PB(bt)], writes=[("ktm", r)])
            bs_ = bank()
            for h in range(4):
                P.op("pe", lambda e, h=h: e.matmul(ps[:, bs_, h * 128:(h + 1) * 128], lhsT=kiT[:, h, js], rhs=qiT[:, h, js], start=True, stop=True),
                     reads=[("kiT", h), ("qiT", h)], writes=[PB(bs_)])
            P.op("dve", lambda e: e.tensor_tensor(out=scT[r][:], in0=ps[:, bs_, :].rearrange("p (h t) -> p h t", h=4),
                                                  in1=maskf.unsqueeze(1).to_broadcast([128, 4, 128]), op=ALU.mult),
                 reads=[PB(bs_), "cst"], writes=[("scT", r)])

        def R_o1(s, j):
            G = s * NT + j
            g = G % NS
            r = g % 2
            bo = bank_pair()
            rbank[g] = bo
            pstate["held"].update((bo, bo + 1))
            for h in range(4):
                bb = bo + h // 2
                o = ps[:, bb, (h % 2) * 256:(h % 2 + 1) * 256]
                P.op("pe", lambda e, o=o, h=h: e.matmul(o, lhsT=scT[r][:, h, :], rhs=vb[j][:, h * 256:(h + 1) * 256], start=(h % 2 == 0), stop=False,
                                                        skip_group_check=True),
                     reads=[("scT", r), ("vb", j)], writes=[PB(bb)])
            bk = bank_pair()
            for h in range(4):
                bb = bk + h // 2
                o = ps[:, bb, (h % 2) * 256:(h % 2 + 1) * 256]
                P.op("pe", lambda e, o=o, h=h: e.matmul(o, lhsT=ktm[r][:, h * 128:(h + 1) * 128], rhs=vb[j][:, h * 256:(h + 1) * 256], start=True, stop=True),
                     reads=[("ktm", r), ("vb", j)], writes=[PB(bb)])
            for h in range(4):
                bb = bk + h // 2
                o = ps[:, bb, (h % 2) * 256:(h % 2 + 1) * 256]
                P.op("act", lambda e, o=o, h=h: e.activation(out=tkv4[:, h, :], in_=o, func=AF.Copy, scale=edl[:, g * 4 + h:g * 4 + h + 1]),
                     reads=[PB(bb), ("edl", g)], writes=[("tkv", h)])
            for h in range(4):
                P.op("dve", lambda e, h=h: e.scalar_tensor_tensor(out=S[:, h, :], in0=S[:, h, :], scalar=elast[:, g * 4 + h:g * 4 + h + 1],
                                                                 in1=tkv4[:, h, :], op0=ALU.mult, op1=ALU.add),
                     reads=["S", ("elast", g), ("tkv", h)], writes=["S"])

        def R_o2(s, j):
            G = s * NT + j
            g = G % NS
            r = g % 2
            js = slice(j * 128, (j + 1) * 128)
            bo = rbank[g]
            for h in range(4):
                bb = bo + h // 2
                o = ps[:, bb, (h % 2) * 256:(h % 2 + 1) * 256]
                P.op("pe", lambda e, o=o, h=h: e.matmul(o, lhsT=qiT[:, h, js], rhs=Sp[r][:, h, :], start=False, stop=True, skip_group_check=True),
                     reads=[("qiT", h), ("Sp", r)], writes=[PB(bb)])
            for h in range(4):
                bb = bo + h // 2
                o = ps[:, bb, (h % 2) * 256:(h % 2 + 1) * 256]
                P.op("act", lambda e, o=o, h=h: e.activation(out=junk[:, h * 256:(h + 1) * 256], in_=o, func=AF.Square, accum_out=sso[:, g * 4 + h:g * 4 + h + 1]),
                     reads=[PB(bb)], writes=[("sso", g, h), ("junk", h)])
            rstd_chain(sso[:, g * 4:g * 4 + 4], rso[:, g * 4:g * 4 + 4], 1.0 / 256, EPS, [("sso", g, h) for h in range(4)], ("rso", g), w=4)

        def R_on(s, j):
            G = s * NT + j
            g = G % NS
            r = g % 2
            bo = rbank[g]
            for h in range(4):
                bb = bo + h // 2
                o = ps[:, bb, (h % 2) * 256:(h % 2 + 1) * 256]
                P.op("dve", lambda e, o=o, h=h: e.scalar_tensor_tensor(out=BFs[r][:, h * 256:(h + 1) * 256], in0=o, scalar=rso[:, g * 4 + h:g * 4 + h + 1],
                                                                      in1=zgs[j][:, h * 256:(h + 1) * 256], op0=ALU.mult, op1=ALU.mult),
                     reads=[PB(bb), ("rso", g), ("szb", j)], writes=[("BFs", r)])
            pstate["held"].difference_update((bo, bo + 1))

        def R_tr(s, j):
            G = s * NT + j
            g = G % NS
            r = g % 2
            js = slice(j * 128, (j + 1) * 128)
            bt2 = bank()
            pv = ps[:, bt2, :].bitcast(BF16)
            for c in range(KC):
                P.op("pe", lambda e, c=c: e.transpose(pv[:, c * 128:(c + 1) * 128], BFs[r][:, c * 128:(c + 1) * 128], ident[:]),
                     reads=[("BFs", r), "ident"], writes=[PB(bt2)])
            P.op("act", lambda e: e.activation(out=onT[:, :, js], in_=pv[:, 0:1024].rearrange("p (c t) -> p c t", c=8), func=AF.Copy),
                 reads=[PB(bt2)], writes=["onT"])

        def M_a(s, c):
            hbuf = s % 2
            half, n = c // 4, c % 4
            r = c % 2
            rga = ring_take(s, "ga%d" % half)
            rwa = ring_take(s, "wa%d" % half)
            rgb = ring_take(s, "gb%d" % half)
            t1 = F32A[r][:, 0:512]
            sga = F32B[0]
            sgb = F32B[1]
            b = bank()
            proj_fm(rga, n * 128, 128, hbuf, b)
            P.op("act", lambda e: e.activation(out=sga[:], in_=ps[:, b, 0:TT], func=AF.Sigmoid), reads=[PB(b)], writes=[("F32B", 0)])
            b2 = bank()
            for kc in range(KC):
                P.op("pe", lambda e, kc=kc: e.matmul(ps[:, b2, 0:TT], lhsT=ring[rwa][:, kc, n * 128:(n + 1) * 128], rhs=guT[:, kc, :],
                                                     start=(kc == 0), stop=(kc == KC - 1)),
                     reads=[("ring", rwa), ("guT", kc)], writes=[PB(b2)])
            P.op("dve", lambda e: e.tensor_tensor(out=t1, in0=ps[:, b2, 0:TT], in1=sga[:], op=ALU.mult),
                 reads=[PB(b2), ("F32B", 0), ("F32A", r)], writes=[("F32A", r, 0)])
            b3 = bank()
            proj_fm(rgb, n * 128, 128, hbuf, b3)
            P.op("act", lambda e: e.activation(out=sgb[:], in_=ps[:, b3, 0:TT], func=AF.Sigmoid), reads=[PB(b3)], writes=[("F32B", 1)])

        def M_b(s, c):
            half, n = c // 4, c % 4
            r = c % 2
            rwb = ring_take(s, "wb%d" % half)
            t1 = F32A[r][:, 0:512]
            t2 = F32A[r][:, 512:1024]
            sgb = F32B[1]
            b4 = bank()
            for kc in range(KC):
                P.op("pe", lambda e, kc=kc: e.matmul(ps[:, b4, 0:TT], lhsT=ring[rwb][:, kc, n * 128:(n + 1) * 128], rhs=onT[:, kc, :],
                                                     start=(kc == 0), stop=(kc == KC - 1)),
                     reads=[("ring", rwb), "onT"], writes=[PB(b4)])
            P.op("dve", lambda e: e.tensor_tensor(out=t2, in0=ps[:, b4, 0:TT], in1=sgb[:], op=ALU.mult),
                 reads=[PB(b4), ("F32B", 1), ("F32A", r)], writes=[("F32A", r, 1)])
            P.op("dve", lambda e: e.tensor_tensor(out=mT[:, c, :], in0=t1, in1=t2, op=ALU.add),
                 reads=[("F32A", r, 0), ("F32A", r, 1)], writes=[("mT", c)])
            if n == 3:
                for nm in ("ga", "wa", "gb", "wb"):
                    ring_release(s, "%s%d" % (nm, half))

        def M_n(s, c):
            M_a(s, c)
            M_b(s, c)

        def Y_load(s, j):
            G = s * NT + j
            g = G % NS
            r = g % 2
            P.dma("sync", lambda e: e.dma_start(out=xr[r][:], in_=x[G * 128:(G + 1) * 128, :]), "xr%d" % r, writes=[("xr", r)])

        def Y(s, j):
            G = s * NT + j
            g = G % NS
            r = g % 2
            js = slice(j * 128, (j + 1) * 128)
            by = bank_pair()
            for half in range(2):
                rw = ring_take(s, "wo%d" % half)
                for kc in range(KC):
                    P.op("pe", lambda e, kc=kc, half=half, rw=rw: e.matmul(ps[:, by + half, :], lhsT=mT[:, kc, js], rhs=ring[rw][:, kc, :],
                                                                          start=(kc == 0), stop=(kc == KC - 1)),
                         reads=[("ring", rw), ("mT", kc)], writes=[PB(by + half)])
            P.op("dve", lambda e: e.tensor_tensor(out=xr[r][:], in0=ps[:, by:by + 2, :].rearrange("p a b -> p (a b)"), in1=xr[r][:], op=ALU.add),
                 reads=[PB(by), PB(by + 1), ("xr", r)], writes=[("xr", r)])
            P.op("act", lambda e: e.activation(out=junk[:], in_=xr[r][:], func=AF.Square, accum_out=ssf[:, g:g + 1]),
                 reads=[("xr", r)], writes=[("ssf", g)] + JK)
            rstd_chain(ssf[:, g:g + 1], rsf[:, g:g + 1], 1.0 / D, EPS, ("ssf", g), ("rsf", g))
            P.op("dve", lambda e: e.scalar_tensor_tensor(out=xr[r][:], in0=xr[r][:], scalar=rsf[:, g:g + 1], in1=Gf[:], op0=ALU.mult, op1=ALU.mult),
                 reads=[("xr", r), ("rsf", g), "Gf"], writes=[("xr", r)])
            P.dma("sync", lambda e: e.dma_start(out=out[G * 128:(G + 1) * 128, :], in_=xr[r][:]), "st%d" % r, reads=[("xr", r)], final=True)
            if j == NT - 1:
                ring_release(s, "wo0")
                ring_release(s, "wo1")

        zgs = szb

        def B_zg(s, j):
            for h in range(4):
                P.op("dve", lambda e, h=h: e.tensor_tensor(out=szb[j][:, h * 256:(h + 1) * 256], in0=szb[j][:, h * 256:(h + 1) * 256], in1=ggb[:], op=ALU.mult),
                     reads=[("szb", j), "ggb"], writes=[("szb", j)])

        for j in range(NT):
            fr_a(0, j)
            fr_b(0, j)
        Wraw = F32A[0][:, :].rearrange("p (h s) -> p h s", h=8)
        cload(Wraw, w_sp.rearrange("h t s -> t h s"), "c_wsp", writes=[("F32A", 0)])
        for hh in range(2):
            b = bank()
            for h4 in range(4):
                h = hh * 4 + h4
                P.op("pe", lambda e, b=b, h=h, h4=h4: e.matmul(ps[:, b, h4 * 128:(h4 + 1) * 128], lhsT=Wraw[:, h, :], rhs=identf,
                                                              start=True, stop=True),
                     reads=[("F32A", 0), "cst"], writes=[PB(b)])
            for h4 in range(4):
                h = hh * 4 + h4
                P.op("dve", lambda e, b=b, h=h, h4=h4: e.tensor_tensor(out=WsT[:, h, :], in0=ps[:, b, h4 * 128:(h4 + 1) * 128], in1=maskf,
                                                                      op=ALU.mult),
                     reads=[PB(b), "cst"], writes=["WsT"])
        bsf = F32A[1]
        cload(bsf[0:1, :], b_sp[0:1, :], "c_bs0", writes=[("F32A", 1)])
        cload(bsf[32:33, :], b_sp[0:1, :], "c_bs1", writes=[("F32A", 1)])
        P.op("dve", lambda e: e.tensor_copy(out=bspad[0:1, :], in_=bsf[0:1, :]), reads=[("F32A", 1)], writes=["bspad"])
        P.op("dve", lambda e: e.tensor_copy(out=BFs[0][32:33, :], in_=bsf[32:33, :]), reads=[("F32A", 1)], writes=[("BFs", 0)])
        P.op("dve", lambda e: e.tensor_tensor(out=bspad[32:33, :], in0=bsf[32:33, :], in1=BFs[0][32:33, :], op=ALU.subtract),
             reads=[("F32A", 1), ("BFs", 0)], writes=["bspad"])

        bcast_row(Gln, ln_g[0:1, :], D, xr[1], ("xr", 1), "c_gln", "Gln")
        bcast_row(Bln, ln_b[0:1, :], D, xr[0], ("xr", 0), "c_bln", "Bln")
        bcast_row(ggb, gla_g[0:1, :], 256, xr[1], ("xr", 1), "c_ggb", "ggb")
        bcast_row(Gf, fin_g[0:1, :], D, xr[0], ("xr", 0), "c_gf", "Gf")
        P.op("dve", lambda e: e.tensor_scalar(out=Gf[:], in0=Gf[:], scalar1=float(D ** 0.5), scalar2=None, op0=ALU.mult), reads=["Gf"], writes=["Gf"])
        P.op("dve", lambda e: e.tensor_scalar(out=ggb[:], in0=ggb[:], scalar1=16.0, scalar2=None, op0=ALU.mult), reads=["ggb"], writes=["ggb"])
        ring_pump()
        for s in range(NST):
            last = (s + 1 == NST)
            if s == 0:
                B_lr(s)
                B_logit(s, 0); B_logit(s, 1)
            B_vb(s, 0)
            B_cum_a(s, 0); B_cum_a(s, 1)
            B_logit(s, 2); B_logit(s, 3)
            B_cum_b(s, 0); B_cum_b(s, 1)
            B_vb(s, 1)
            B_vb(s, 2)
            B_cum_a(s, 2); B_cum_a(s, 3)
            B_cum_b(s, 2); B_cum_b(s, 3)
            B_vb(s, 3)
            B_zb(s, 0); B_zb(s, 1)
            for h in range(4):
                B_q(s, h)
            for h in range(4):
                B_k(s, h)
            B_zb(s, 2); B_zb(s, 3)
            for c in range(8):
                A_za(s, c)
            for j in range(NT):
                B_zg(s, j)
            for j in range(NT):
                R_sc(s, j)
                if j > 0:
                    A_sp(s, j - 1)
                    R_on(s, j - 1)
                A_v(s, j)
                R_o1(s, j)
                if j > 0:
                    R_tr(s, j - 1)
                A_u(s, 2 * j)
                A_u(s, 2 * j + 1)
                R_o2(s, j)
            A_sp(s, NT - 1)
            R_on(s, NT - 1)
            M_a(s, 0)
            R_tr(s, NT - 1)
            seq = {0: ("a", 0), 1: ("a", 1), 2: ("b", 0), 3: ("a", 2), 4: ("b", 1), 5: ("a", 3), 6: ("b", 2), 7: ("b", 3)}
            for c in range(8):
                if c > 0:
                    M_a(s, c)
                M_b(s, c)
                if not last:
                    kind, jj = seq[c]
                    if kind == "a":
                        fr_a(s + 1, jj)
                    else:
                        fr_b(s + 1, jj)
                    if c == 2:
                        pass
                if c >= 4:
                    Y_load(s, c - 4) if c - 4 < 2 else None
            if not last:
                B_lr(s + 1)
            for j in range(NT):
                Y(s, j)
                if j + 2 < NT:
                    Y_load(s, j + 2)
                if not last and j < 2:
                    B_logit(s + 1, j)
            if dbg and s == 0:
                P.dma("sync", lambda e: e.dma_start(out=d_hT[:, :, :], in_=hT[0][:]), "dbg0", reads=[("hT", 0)], final=True)
                P.dma("sync", lambda e: e.dma_start(out=d_aT[:, :, :], in_=guT[:]), "dbg1", reads=[("guT", c) for c in range(8)], final=True)
                P.dma("sync", lambda e: e.dma_start(out=d_onT[:, :, :], in_=onT[:]), "dbg2", reads=["onT"], final=True)
                P.dma("sync", lambda e: e.dma_start(out=d_mT[:, :, :], in_=mT[:]), "dbg3", reads=[("mT", c) for c in range(8)], final=True)

        print("sbuf bytes remaining", nc.sbuf_bytes_remaining)
        block = es.enter_context(nc.Block())
        P.emit(block)
    return nc


def _consts():
    c = np.zeros((128, 512), np.float32)
    c[0, 384:512] = 1.0
    c[:, 0:128] = np.eye(128, dtype=np.float32)
    tri = (np.arange(128)[:, None] <= np.arange(128)[None, :]).astype(np.float32)
    c[:, 128:256] = tri * np.float32(-1.0 / 16.0)
    c[:, 256:384] = tri
    return c


def make_in_maps(x, norm_g, w_in, ln_v_g, ln_v_b, w_spatial, b_spatial, w_gate_up, b_gate_up,
                 gla_norm_g, w_branch_a, w_branch_b, w_out, final_norm_g, n_cores, T):
    f = lambda a: np.ascontiguousarray(np.asarray(a, dtype=np.float32))
    shared = {
        "w_in": f(w_in[0]), "w_a": f(w_branch_a[0]), "w_b": f(w_branch_b[0]), "w_o": f(w_out[0]),
        "norm_g": f(norm_g[0]).reshape(1, D), "ln_g": f(ln_v_g[0]).reshape(1, D), "ln_b": f(ln_v_b[0]).reshape(1, D),
        "w_sp": f(w_spatial[0]), "b_sp": f(b_spatial[0]).reshape(1, 1024),
        "w_gu": f(w_gate_up[0]), "b_gu": f(b_gate_up[0]).reshape(1, 512),
        "gla_g": f(gla_norm_g[0]).reshape(1, 256), "fin_g": f(final_norm_g).reshape(1, D),
        "consts": _consts(),
    }
    maps = []
    for b in range(n_cores):
        m = dict(shared)
        m["x"] = f(np.asarray(x)[b, :T])
        maps.append(m)
    return maps


_NC_CACHE = {}


def kernel(x, norm_g, w_in, ln_v_g, ln_v_b, w_spatial, b_spatial, w_gate_up, b_gate_up,
           gla_norm_g, w_branch_a, w_branch_b, w_out, final_norm_g):
    x = np.asarray(x)
    B, T, _ = x.shape
    nc = build_program(T)
    in_maps = make_in_maps(x, norm_g, w_in, ln_v_g, ln_v_b, w_spatial, b_spatial, w_gate_up, b_gate_up,
                           gla_norm_g, w_branch_a, w_branch_b, w_out, final_norm_g, B, T)
    res = run_bass_kernel_spmd(nc, in_maps, core_ids=list(range(B)))
    return np.stack([np.asarray(r["out"], dtype=np.float32) for r in res.results], axis=0)
```

```python
import numpy as np
from contextlib import ExitStack
import concourse.bass as bass
import concourse.mybir as mybir
from concourse.bass_utils import run_bass_kernel_spmd

F32 = mybir.dt.float32
BF16 = mybir.dt.bfloat16
AF = mybir.ActivationFunctionType
ALU = mybir.AluOpType

D = 1024
NIN = 8208
NT = 4
TT = NT * 128
KC = 8
NB = 5
NS = 8
EPS = 1e-6
LN_EPS = 1e-5
C_U, C_V, C_ZA, C_Q, C_K, C_VB, C_ZB, C_LR, C_GA, C_GB = 0, 1024, 2048, 3072, 3584, 4096, 5120, 6144, 6160, 7184


class Prog:
    ENGS = ("pe", "act", "dve", "pool", "sync")

    def __init__(self, nc, es):
        self.nc = nc
        self.es = es
        self.ops = {e: [] for e in self.ENGS}
        self.cnt = {e: 0 for e in self.ENGS}
        self.sem = {e: es.enter_context(nc.semaphore("sem_" + e)) for e in ("pe", "act", "dve", "pool")}
        self.dsem = {}
        self.dcnt = {}
        self.res = {}
        self.known = {e: {} for e in self.ENGS}
        self.final_waits = []

    def _deps(self, reads, writes, eng=None):
        deps = []
        for k in reads:
            r = self.res.get(k)
            if r and r["w"] is not None:
                deps.append(r["w"])
            if r and isinstance(k, tuple) and k[0] == "ps":
                deps.extend(ev for ev in r["r"] if ev[0] != eng)
        for k in writes:
            r = self.res.get(k)
            if r:
                if r["w"] is not None:
                    deps.append(r["w"])
                deps.extend(r["r"])
        return deps

    def _record(self, ev, reads, writes):
        for k in reads:
            r = self.res.setdefault(k, {"w": None, "r": []})
            r["r"].append(ev)
        for k in writes:
            self.res[k] = {"w": ev, "r": []}

    def _waits(self, eng, deps):
        best = {}
        for (semname, val) in deps:
            if eng == "pe" and semname == "pe":
                continue
            if val > best.get(semname, 0):
                best[semname] = val
        waits = []
        kn = self.known[eng]
        for semname, val in best.items():
            if kn.get(semname, 0) >= val:
                continue
            kn[semname] = val
            waits.append((semname, val))
        return waits

    def op(self, eng, fn, reads=(), writes=()):
        deps = self._deps(reads, writes, eng)
        waits = self._waits(eng, deps)
        self.cnt[eng] += 1
        ev = (eng, self.cnt[eng])
        self.ops[eng].append((waits, fn, ev))
        self._record(ev, reads, writes)
        return ev

    def dma(self, eng, fn, semkey, reads=(), writes=(), final=False):
        if semkey not in self.dsem:
            self.dsem[semkey] = self.es.enter_context(self.nc.semaphore("d_" + semkey))
            self.dcnt[semkey] = 0
        deps = self._deps(reads, writes)
        waits = self._waits(eng, deps)
        self.dcnt[semkey] += 16
        ev = ("d:" + semkey, self.dcnt[semkey])
        self.ops[eng].append((waits, fn, ev))
        self._record(ev, reads, writes)
        if final:
            self.final_waits.append(ev)
        return ev

    def _semh(self, name):
        if name.startswith("d:"):
            return self.dsem[name[2:]]
        return self.sem[name]

    def emit(self, block):
        def run(engname):
            def body(e):
                for waits, fn, ev in self.ops[engname]:
                    for (sn, val) in waits:
                        e.wait_ge(self._semh(sn), val)
                    ins = fn(e)
                    ins.then_inc(self._semh(ev[0]), 16 if ev[0].startswith("d:") else 1)
                if engname == "sync":
                    best = {}
                    for sn, val in self.final_waits:
                        best[sn] = max(best.get(sn, 0), val)
                    for sn, val in best.items():
                        e.wait_ge(self._semh(sn), val)
            return body
        block.tensor(run("pe"))
        block.scalar(run("act"))
        block.vector(run("dve"))
        block.gpsimd(run("pool"))
        block.sync(run("sync"))


def build_program(T, dbg=False):
    NST = T // TT
    NTOT = T // 128
    nc = bass.Bass("TRN2", target_bir_lowering=False)

    def din(name, shape):
        return nc.dram_tensor(name, shape, F32, kind="ExternalInput").ap()

    x = din("x", [T, D])
    w_in = din("w_in", [D, NIN])
    w_a = din("w_a", [D, D])
    w_b = din("w_b", [D, D])
    w_o = din("w_o", [D, D])
    norm_g = din("norm_g", [1, D])
    ln_g = din("ln_g", [1, D])
    ln_b = din("ln_b", [1, D])
    w_sp = din("w_sp", [8, 128, 128])
    b_sp = din("b_sp", [1, 1024])
    w_gu = din("w_gu", [16, 512])
    b_gu = din("b_gu", [1, 512])
    gla_g = din("gla_g", [1, 256])
    fin_g = din("fin_g", [1, D])
    consts = din("consts", [128, 512])
    out = nc.dram_tensor("out", [T, D], F32, kind="ExternalOutput").ap()
    if dbg:
        d_hT = nc.dram_tensor("d_hT", [128, KC, TT], BF16, kind="ExternalOutput").ap()
        d_aT = nc.dram_tensor("d_aT", [128, KC, TT], BF16, kind="ExternalOutput").ap()
        d_onT = nc.dram_tensor("d_onT", [128, KC, TT], BF16, kind="ExternalOutput").ap()
        d_mT = nc.dram_tensor("d_mT", [128, KC, TT], BF16, kind="ExternalOutput").ap()

    NCG = 22
    wscr = nc.dram_tensor("wscr", [NCG, 128, KC, 512], BF16).ap()
    w_in_v = w_in.rearrange("(kc p) n -> p kc n", p=128)
    w_a_v = w_a.rearrange("(kc p) n -> p kc n", p=128)
    w_b_v = w_b.rearrange("(kc p) n -> p kc n", p=128)
    w_o_v = w_o.rearrange("(kc p) n -> p kc n", p=128)

    with ExitStack() as es:
        def sb(name, shape, dt):
            return es.enter_context(nc.sbuf_tensor(name, shape, dt))

        P = Prog(nc, es)
        ps = es.enter_context(nc.psum_tensor("ps", [128, 8, 512], F32))

        Gx = sb("Gx", [128, D], F32)
        Gf = sb("Gf", [128, D], F32)
        Gln = sb("Gln", [128, D], F32)
        Bln = sb("Bln", [128, D], F32)
        ggb = sb("ggb", [128, 256], F32)
        cst = sb("cst", [128, 512], F32)
        identf = cst[:, 0:128]
        Uneg = cst[:, 128:256]
        maskf = cst[:, 256:384]
        sel0 = cst[:, 384:512]
        ident = sb("ident", [128, 128], BF16)
        WsT = sb("WsT", [128, 8, 128], BF16)
        bspad = sb("bspad", [128, 1024], BF16)
        onespad = sb("onespad", [128, 128], BF16)
        wgu = sb("wgu", [128, 512], BF16)
        wlr = sb("wlr", [128, KC, 16], BF16)
        ring = [sb("ring%d" % i, [128, KC, 512], BF16) for i in range(NB)]
        hT = [sb("hT%d" % i, [128, KC, TT], BF16) for i in range(2)]
        xt = [sb("xt%d" % i, [128, D], F32) for i in range(2)]
        xr = [sb("xr%d" % i, [128, D], F32) for i in range(2)]
        junk = sb("junk", [128, D], BF16)
        BFs = [sb("BFs%d" % i, [128, D], BF16) for i in range(2)]
        guT = sb("guT", [128, KC, TT], BF16)
        sza = [sb("sza%d" % i, [128, TT], BF16) for i in range(2)]
        F32A = [sb("F32A%d" % i, [128, D], F32) for i in range(2)]
        F32B = [sb("F32B%d" % i, [128, 512], F32) for i in range(3)]
        lrT = [sb("lrT%d" % i, [128, TT], BF16) for i in range(2)]
        Et = sb("Et", [128, 4, TT], F32)
        Einv = sb("Einv", [128, 4, TT], F32)
        qiT = sb("qiT", [128, 4, TT], BF16)
        kiT = sb("kiT", [128, 4, TT], BF16)
        ktm = [sb("ktm%d" % i, [128, 512], BF16) for i in range(2)]
        vb = [sb("vb%d" % i, [128, D], BF16) for i in range(NT)]
        szb = [sb("szb%d" % i, [128, D], BF16) for i in range(NT)]
        scT = [sb("scT%d" % i, [128, 4, 128], BF16) for i in range(2)]
        S = sb("S", [128, 4, 256], F32)
        tkv4 = sb("tkv4", [128, 4, 256], F32)
        Sp = [sb("Sp%d" % i, [128, 4, 256], BF16) for i in range(2)]
        onT = sb("onT", [128, KC, TT], BF16)
        mT = sb("mT", [128, KC, TT], BF16)
        rconst = sb("rconst", [128, 16], F32)
        ssx = sb("ssx", [128, NS], F32)
        rsx = sb("rsx", [128, NS], F32)
        bst = sb("bst", [128, NS * 12], F32)
        mvv = sb("mvv", [128, NS * 2], F32)
        rsv = sb("rsv", [128, NS], F32)
        nbm = sb("nbm", [128, NS * 4], F32)
        pbm = sb("pbm", [128, NS * 4], F32)
        dlt = sb("dlt", [128, NS * 4], F32)
        emid = sb("emid", [128, NS * 4], F32)
        elast = sb("elast", [128, NS * 4], F32)
        edl = sb("edl", [128, NS * 4], F32)
        sso = sb("sso", [128, NS * 4], F32)
        rso = sb("rso", [128, NS * 4], F32)
        ssf = sb("ssf", [128, NS], F32)
        rsf = sb("rsf", [128, NS], F32)

        pstate = {"ptr": 0, "held": set()}

        def bank():
            while pstate["ptr"] % 8 in pstate["held"]:
                pstate["ptr"] += 1
            b = pstate["ptr"] % 8
            pstate["ptr"] += 1
            return b

        def bank_pair():
            while True:
                if pstate["ptr"] % 2 == 1:
                    pstate["ptr"] += 1
                b = pstate["ptr"] % 8
                if b in pstate["held"] or (b + 1) in pstate["held"]:
                    pstate["ptr"] += 2
                    continue
                pstate["ptr"] += 2
                return b

        def PB(b):
            return ("ps", b)

        CG_ORDER = ["vb0", "vb1", "zb0", "zb1", "q", "k", "za0", "za1", "v0", "v1", "u0", "u1",
                    "ga0", "wa0", "gb0", "wb0", "ga1", "wa1", "gb1", "wb1", "wo0", "wo1"]
        CG_SRC = {"vb0": (w_in_v, C_VB), "vb1": (w_in_v, C_VB + 512), "zb0": (w_in_v, C_ZB), "zb1": (w_in_v, C_ZB + 512),
                  "q": (w_in_v, C_Q), "k": (w_in_v, C_K), "v0": (w_in_v, C_V), "v1": (w_in_v, C_V + 512),
                  "u0": (w_in_v, C_U), "u1": (w_in_v, C_U + 512), "za0": (w_in_v, C_ZA), "za1": (w_in_v, C_ZA + 512),
                  "ga0": (w_in_v, C_GA), "ga1": (w_in_v, C_GA + 512), "gb0": (w_in_v, C_GB), "gb1": (w_in_v, C_GB + 512),
                  "wa0": (w_a_v, 0), "wa1": (w_a_v, 512), "wb0": (w_b_v, 0), "wb1": (w_b_v, 512),
                  "wo0": (w_o_v, 0), "wo1": (w_o_v, 512)}
        cgs = []
        cg_index = {}
        for s in range(NST):
            for ci, name in enumerate(CG_ORDER):
                cg_index[(s, name)] = len(cgs)
                cgs.append((s, ci, CG_SRC[name]))
        rstate = {"next_load": 0, "released": set()}

        def ring_pump():
            while rstate["next_load"] < len(cgs):
                i = rstate["next_load"]
                if i >= NB and (i - NB) not in rstate["released"]:
                    break
                st, ci, (view, c0) = cgs[i]
                buf = ring[i % NB]
                if st == 0:
                    P.dma("pool", lambda e, buf=buf, view=view, c0=c0: e.dma_start(out=buf[:], in_=view[:, :, c0:c0 + 512]),
                          "rp%d" % (i % NB), writes=[("ring", i % NB)])
                    if NST > 1:
                        P.dma("sync", lambda e, buf=buf, ci=ci: e.dma_start(out=wscr[ci], in_=buf[:]),
                              "scr%d" % ci, reads=[("ring", i % NB)], writes=[("scr", ci)])
                else:
                    P.dma("sync", lambda e, buf=buf, ci=ci: e.dma_start(out=buf[:], in_=wscr[ci]),
                          "rs%d" % (i % NB), reads=[("scr", ci)], writes=[("ring", i % NB)])
                rstate["next_load"] += 1

        def ring_take(s, name):
            i = cg_index[(s, name)]
            assert i < rstate["next_load"], ("CG not loaded yet", s, name)
            return i % NB

        def ring_release(s, name):
            rstate["released"].add(cg_index[(s, name)])
            ring_pump()

        def cload(dst, src, key, eng="sync", writes=()):
            P.dma(eng, lambda e: e.dma_start(out=dst, in_=src), key, writes=list(writes))

        cload(cst[:], consts[:, :], "c_cst", writes=["cst"])
        def bcast_row(dst, src_row, W, stg, stgkey, key, dkey):
            P.op("dve", lambda e: e.memset(stg[:, 0:W], 0.0), writes=[stgkey])
            P.dma("sync", lambda e: e.dma_start(out=stg[0:1, 0:W], in_=src_row), key, writes=[stgkey])
            for c0 in range(0, W, 512):
                w = min(512, W - c0)
                b = bank()
                P.op("pe", lambda e, b=b, c0=c0, w=w: e.matmul(ps[:, b, 0:w], lhsT=sel0, rhs=stg[:, c0:c0 + w], start=True, stop=True),
                     reads=[stgkey, "cst"], writes=[PB(b)])
                P.op("act", lambda e, b=b, c0=c0, w=w: e.activation(out=dst[:, c0:c0 + w], in_=ps[:, b, 0:w], func=AF.Copy),
                     reads=[PB(b)], writes=[dkey])

        bcast_row(Gx, norm_g[0:1, :], D, xr[0], ("xr", 0), "c_gx", "Gx")
        P.op("dve", lambda e: e.memset(wgu[:], 0.0), writes=["wgu"])
        P.op("dve", lambda e: e.memset(bspad[:], 0.0), writes=["bspad"])
        P.op("dve", lambda e: e.memset(onespad[:], 0.0), writes=["onespad"])
        P.op("dve", lambda e: e.memset(onespad[0:1, :], 1.0), writes=["onespad"])
        P.op("dve", lambda e: e.memset(onespad[32:33, :], 1.0), writes=["onespad"])
        for i in range(2):
            P.op("dve", lambda e, i=i: e.memset(lrT[i][:], 0.0), writes=[("lrT", i)])
            P.op("dve", lambda e, i=i: e.memset(lrT[i][32:33, :], 1.0), writes=[("lrT", i)])
        P.op("dve", lambda e: e.memset(S[:], 0.0), writes=["S"])
        P.op("dve", lambda e: e.memset(rconst[:, 0:4], float(D * EPS)), writes=["rconst"])
        P.op("dve", lambda e: e.memset(rconst[:, 4:8], float(LN_EPS)), writes=["rconst"])
        P.op("dve", lambda e: e.memset(rconst[:, 8:12], float(256 * EPS)), writes=["rconst"])
        P.op("dve", lambda e: e.memset(rconst[:, 12:16], -0.5), writes=["rconst"])
        P.op("dve", lambda e: e.tensor_scalar(out=Gx[:], in0=Gx[:], scalar1=float(D ** 0.5), scalar2=None, op0=ALU.mult), reads=["Gx"], writes=["Gx"])
        P.op("dve", lambda e: e.tensor_copy(out=ident[:], in_=identf), reads=["cst"], writes=["ident"])
        cload(wgu[0:16, :], w_gu[:, :], "c_wgu", eng="pool", writes=["wgu"])
        cload(wgu[32:33, :], b_gu[0:1, :], "c_bgu", eng="pool", writes=["wgu"])
        cload(wlr[:], w_in_v[:, :, C_LR:C_LR + 16], "c_wlr", eng="pool", writes=["wlr"])
        hn = [sb("hn%d" % i, [128, D], BF16) for i in range(2)]
        JK = [("junk", h) for h in range(4)]

        def rstd_chain(ss_ap, out_ap, scale, eps, key_in, key_out, w=1):
            keys_in = key_in if isinstance(key_in, list) else [key_in]
            ceps = {1.0 / D: 0, 1.0: 1, 1.0 / 256: 2}[scale]
            P.op("pool", lambda e: e.tensor_tensor(out=out_ap, in0=ss_ap, in1=rconst[:, ceps * 4:ceps * 4 + w], op=ALU.add),
                 reads=keys_in + ["rconst"], writes=[key_out])
            P.op("pool", lambda e: e.tensor_tensor(out=out_ap, in0=out_ap, in1=rconst[:, 12:12 + w], op=ALU.pow),
                 reads=[key_out, "rconst"], writes=[key_out])

        def proj_fm(rb, c_off, M, hbuf, b):
            for kc in range(KC):
                P.op("pe", lambda e, kc=kc: e.matmul(ps[0:M, b, 0:TT], lhsT=ring[rb][:, kc, c_off:c_off + M], rhs=hT[hbuf][:, kc, :],
                                                     start=(kc == 0), stop=(kc == KC - 1)),
                     reads=[("ring", rb), ("hT", hbuf)], writes=[PB(b)])

        def proj_tm(rb, j, hbuf, b):
            for kc in range(KC):
                P.op("pe", lambda e, kc=kc: e.matmul(ps[:, b, :], lhsT=hT[hbuf][:, kc, j * 128:(j + 1) * 128], rhs=ring[rb][:, kc, :],
                                                     start=(kc == 0), stop=(kc == KC - 1)),
                     reads=[("ring", rb), ("hT", hbuf)], writes=[PB(b)])

        def fr_a(s, j):
            G = s * NT + j
            g = G % NS
            r = g % 2
            P.dma("sync", lambda e: e.dma_start(out=xt[r][:], in_=x[G * 128:(G + 1) * 128, :]), "xt%d" % r, writes=[("xt", r)])
            P.op("act", lambda e: e.activation(out=junk[:], in_=xt[r][:], func=AF.Square, accum_out=ssx[:, g:g + 1]),
                 reads=[("xt", r)], writes=[("ssx", g)] + JK)
            rstd_chain(ssx[:, g:g + 1], rsx[:, g:g + 1], 1.0 / D, EPS, ("ssx", g), ("rsx", g))
            P.op("dve", lambda e: e.scalar_tensor_tensor(out=hn[r][:], in0=xt[r][:], scalar=rsx[:, g:g + 1], in1=Gx[:],
                                                         op0=ALU.mult, op1=ALU.mult),
                 reads=[("xt", r), ("rsx", g), "Gx"], writes=[("hn", r)])

        def fr_b(s, j):
            G = s * NT + j
            g = G % NS
            r = g % 2
            hbuf = s % 2
            b = bank()
            pv = ps[:, b, :].bitcast(BF16)
            for kc in range(KC):
                P.op("pe", lambda e, kc=kc: e.transpose(pv[:, kc * 128:(kc + 1) * 128], hn[r][:, kc * 128:(kc + 1) * 128], ident[:]),
                     reads=[("hn", r), "ident"], writes=[PB(b)])
            P.op("dve", lambda e: e.tensor_copy(out=hT[hbuf][:, :, j * 128:(j + 1) * 128],
                                                in_=pv[:, 0:1024].rearrange("p (c t) -> p c t", c=8)),
                 reads=[PB(b)], writes=[("hT", hbuf)])

        def A_za(s, c):
            rb = ring_take(s, "za%d" % (c // 4))
            b = bank()
            proj_fm(rb, (c % 4) * 128, 128, s % 2, b)
            P.op("act", lambda e: e.activation(out=guT[:, c, :], in_=ps[:, b, 0:TT], func=AF.Silu),
                 reads=[PB(b)], writes=[("guT", c)])
            if c % 4 == 3:
                ring_release(s, "za%d" % (c // 4))

        def A_u(s, c):
            rb = ring_take(s, "u%d" % (c // 4))
            b = bank()
            proj_fm(rb, (c % 4) * 128, 128, s % 2, b)
            P.op("act", lambda e: e.activation(out=sza[c % 2][:], in_=ps[:, b, 0:TT], func=AF.Gelu),
                 reads=[PB(b)], writes=[("sza", c % 2)])
            P.op("pool", lambda e: e.tensor_tensor(out=guT[:, c, :], in0=guT[:, c, :], in1=sza[c % 2][:], op=ALU.mult),
                 reads=[("guT", c), ("sza", c % 2)], writes=[("guT", c)])
            if c % 4 == 3:
                ring_release(s, "u%d" % (c // 4))

        def A_v(s, j):
            G = s * NT + j
            g = G % NS
            r = g % 2
            gv = F32A[r]
            for half in range(2):
                rb = ring_take(s, "v%d" % half)
                b = bank()
                proj_tm(rb, j, s % 2, b)
                P.op("act", lambda e, b=b, half=half: e.activation(out=gv[:, half * 512:(half + 1) * 512], in_=ps[:, b, :], func=AF.Gelu),
                     reads=[PB(b)], writes=[("F32A", r)])
            for half in range(2):
                P.op("dve", lambda e, half=half: e.bn_stats(out=bst[:, g * 12 + half * 6:g * 12 + half * 6 + 6],
                                                            in_=gv[:, half * 512:(half + 1) * 512]),
                     reads=[("F32A", r)], writes=[("bst", g, half)])
            P.op("dve", lambda e: e.bn_aggr(out=mvv[:, g * 2:g * 2 + 2], in_=bst[:, g * 12:g * 12 + 12]),
                 reads=[("bst", g, 0), ("bst", g, 1)], writes=[("mvv", g)])
            rstd_chain(mvv[:, g * 2 + 1:g * 2 + 2], rsv[:, g:g + 1], 1.0, LN_EPS, ("mvv", g), ("rsv", g))
            P.op("dve", lambda e: e.scalar_tensor_tensor(out=gv[:], in0=gv[:], scalar=mvv[:, g * 2:g * 2 + 1], in1=Gln[:],
                                                         op0=ALU.subtract, op1=ALU.mult),
                 reads=[("F32A", r), ("mvv", g), "Gln"], writes=[("F32A", r)])
            P.op("dve", lambda e: e.scalar_tensor_tensor(out=BFs[r][:], in0=gv[:], scalar=rsv[:, g:g + 1], in1=Bln[:],
                                                         op0=ALU.mult, op1=ALU.add),
                 reads=[("F32A", r), ("rsv", g), "Bln"], writes=[("BFs", r)])
            if j == NT - 1:
                ring_release(s, "v0")
                ring_release(s, "v1")

        def A_sp(s, j):
            G = s * NT + j
            g = G % NS
            r = g % 2
            b2 = bank_pair()
            for h in range(8):
                bb = b2 + h // 4
                o = ps[:, bb, (h % 4) * 128:(h % 4 + 1) * 128]
                P.op("pe", lambda e, o=o, h=h: e.matmul(o, lhsT=BFs[r][:, h * 128:(h + 1) * 128], rhs=WsT[:, h, :], start=True, stop=False),
                     reads=[("BFs", r), "WsT"], writes=[PB(bb)])
                P.op("pe", lambda e, o=o, h=h: e.matmul(o, lhsT=onespad[:], rhs=bspad[:, h * 128:(h + 1) * 128], start=False, stop=True),
                     reads=["onespad", "bspad"], writes=[PB(bb)])
            for half in range(2):
                bb = b2 + half
                P.op("dve", lambda e, bb=bb, half=half: e.tensor_tensor(
                    out=guT[:, half * 4:(half + 1) * 4, j * 128:(j + 1) * 128],
                    in0=ps[:, bb, :].rearrange("p (h t) -> p h t", h=4),
                    in1=guT[:, half * 4:(half + 1) * 4, j * 128:(j + 1) * 128], op=ALU.mult),
                    reads=[PB(bb)] + [("guT", c) for c in range(half * 4, half * 4 + 4)],
                    writes=[("guT", c) for c in range(half * 4, half * 4 + 4)])

        def B_lr(s):
            hbuf = s % 2
            lb = s % 2
            b = bank()
            for kc in range(KC):
                P.op("pe", lambda e, kc=kc: e.matmul(ps[0:16, b, 0:TT], lhsT=wlr[:, kc, :], rhs=hT[hbuf][:, kc, :],
                                                     start=(kc == 0), stop=(kc == KC - 1)),
                     reads=["wlr", ("hT", hbuf)], writes=[PB(b)])
            P.op("dve", lambda e: e.tensor_copy(out=lrT[lb][0:16, :], in_=ps[0:16, b, 0:TT]), reads=[PB(b)], writes=[("lrT", lb)])

        def B_logit(s, j):
            G = s * NT + j
            g = G % NS
            lb = s % 2
            ev = F32B[0]
            ls = F32B[1 + g % 2]
            lskey = ("F32B", 1 + g % 2)
            b = bank()
            P.op("pe", lambda e: e.matmul(ps[:, b, :], lhsT=lrT[lb][:, j * 128:(j + 1) * 128], rhs=wgu[:], start=True, stop=True),
                 reads=[("lrT", lb), "wgu"], writes=[PB(b)])
            P.op("act", lambda e: e.activation(out=ev[:], in_=ps[:, b, :], func=AF.Exp, scale=-1.0),
                 reads=[PB(b)], writes=[("F32B", 0)])
            P.op("act", lambda e: e.activation(out=ls[:], in_=ev[:], func=AF.Ln, bias=1.0, scale=1.0),
                 reads=[("F32B", 0)], writes=[lskey])

        cumbank = {}

        def B_cum_a(s, j):
            G = s * NT + j
            g = G % NS
            ls = F32B[1 + g % 2]
            lskey = ("F32B", 1 + g % 2)
            bc = bank()
            for h in range(4):
                P.op("pe", lambda e, h=h: e.matmul(ps[:, bc, h * 128:(h + 1) * 128], lhsT=ls[:, h * 128:(h + 1) * 128], rhs=Uneg,
                                                   start=True, stop=True),
                     reads=[lskey, "cst"], writes=[PB(bc)])
            cumbank[g] = bc
            pstate["held"].add(bc)

        def B_cum_b(s, j):
            G = s * NT + j
            g = G % NS
            bc = cumbank[g]
            bv = ps[:, bc, :].rearrange("p (h t) -> p h t", h=4)
            sl = slice(g * 4, g * 4 + 4)
            P.op("dve", lambda e: e.tensor_copy(out=pbm[:, sl], in_=bv[:, :, 63]), reads=[PB(bc)], writes=[("pbm", g)])
            P.op("dve", lambda e: e.tensor_scalar(out=nbm[:, sl], in0=bv[:, :, 63], scalar1=-1.0, scalar2=None, op0=ALU.mult),
                 reads=[PB(bc)], writes=[("nbm", g)])
            P.op("dve", lambda e: e.tensor_tensor(out=dlt[:, sl], in0=bv[:, :, 127], in1=pbm[:, sl], op=ALU.subtract),
                 reads=[PB(bc), ("pbm", g)], writes=[("dlt", g)])
            for h in range(4):
                P.op("act", lambda e, h=h: e.activation(out=Et[:, h, j * 128:(j + 1) * 128], in_=ps[:, bc, h * 128:(h + 1) * 128],
                                                        func=AF.Exp, bias=nbm[:, g * 4 + h:g * 4 + h + 1], scale=1.0),
                     reads=[PB(bc), ("nbm", g)], writes=[("Et", h)])
                P.op("act", lambda e, h=h: e.activation(out=Einv[:, h, j * 128:(j + 1) * 128], in_=ps[:, bc, h * 128:(h + 1) * 128],
                                                        func=AF.Exp, bias=pbm[:, g * 4 + h:g * 4 + h + 1], scale=-1.0),
                     reads=[PB(bc), ("pbm", g)], writes=[("Einv", h)])
            P.op("act", lambda e: e.activation(out=emid[:, sl], in_=pbm[:, sl], func=AF.Exp), reads=[("pbm", g)], writes=[("emid", g)])
            P.op("act", lambda e: e.activation(out=elast[:, sl], in_=bv[:, :, 127], func=AF.Exp), reads=[PB(bc)], writes=[("elast", g)])
            P.op("act", lambda e: e.activation(out=edl[:, sl], in_=dlt[:, sl], func=AF.Exp), reads=[("dlt", g)], writes=[("edl", g)])
            pstate["held"].discard(bc)

        def B_q(s, h):
            rb = ring_take(s, "q")
            b = bank()
            proj_fm(rb, h * 128, 128, s % 2, b)
            P.op("dve", lambda e: e.scalar_tensor_tensor(out=qiT[:, h, :], in0=ps[:, b, 0:TT], scalar=float(128 ** -0.5), in1=Et[:, h, :],
                                                         op0=ALU.mult, op1=ALU.mult),
                 reads=[PB(b), ("Et", h)], writes=[("qiT", h)])
            if h == 3:
                ring_release(s, "q")

        def B_k(s, h):
            rb = ring_take(s, "k")
            b = bank()
            proj_fm(rb, h * 128, 128, s % 2, b)
            P.op("dve", lambda e: e.tensor_tensor(out=kiT[:, h, :], in0=ps[:, b, 0:TT], in1=Einv[:, h, :], op=ALU.mult),
                 reads=[PB(b), ("Einv", h)], writes=[("kiT", h)])
            if h == 3:
                ring_release(s, "k")

        def B_tm(s, j, name, dst, func, key):
            for half in range(2):
                rb = ring_take(s, "%s%d" % (name, half))
                b = bank()
                proj_tm(rb, j, s % 2, b)
                P.op("act", lambda e, b=b, half=half: e.activation(out=dst[j][:, half * 512:(half + 1) * 512], in_=ps[:, b, :], func=func),
                     reads=[PB(b)], writes=[(key, j)])
            if j == NT - 1:
                ring_release(s, name + "0")
                ring_release(s, name + "1")

        def B_vb(s, j):
            B_tm(s, j, "vb", vb, AF.Copy, "vb")

        def B_zb(s, j):
            B_tm(s, j, "zb", szb, AF.Silu, "szb")

        rbank = {}

        def R_sc(s, j):
            G = s * NT + j
            g = G % NS
            r = g % 2
            js = slice(j * 128, (j + 1) * 128)
            P.op("dve", lambda e: e.tensor_tensor(out=Sp[r][:], in0=S[:], in1=emid[:, g * 4:g * 4 + 4].unsqueeze(2).to_broadcast([128, 4, 256]), op=ALU.mult),
                 reads=["S", ("emid", g)], writes=[("Sp", r)])
            bt = bank()
            for h in range(4):
                P.op("pe", lambda e, h=h: e.matmul(ps[:, bt, h * 128:(h + 1) * 128], lhsT=kiT[:, h, js], rhs=ident[:], start=True, stop=True),
                     reads=[("kiT", h), "ident"], writes=[PB(bt)])
            P.op("act", lambda e: e.activation(out=ktm[r][:], in_=ps[:, bt, :], func=AF.Copy), reads=[PB(bt)], writes=[("ktm", r)])
            bs_ = bank()
            for h in range(4):
                P.op("pe", lambda e, h=h: e.matmul(ps[:, bs_, h * 128:(h + 1) * 128], lhsT=kiT[:, h, js], rhs=qiT[:, h, js], start=True, stop=True),
                     reads=[("kiT", h), ("qiT", h)], writes=[PB(bs_)])
            P.op("dve", lambda e: e.tensor_tensor(out=scT[r][:], in0=ps[:, bs_, :].rearrange("p (h t) -> p h t", h=4),
                                                  in1=maskf.unsqueeze(1).to_broadcast([128, 4, 128]), op=ALU.mult),
                 reads=[PB(bs_), "cst"], writes=[("scT", r)])

        def R_o1(s, j):
            G = s * NT + j
            g = G % NS
            r = g % 2
            bo = bank_pair()
            rbank[g] = bo
            pstate["held"].update((bo, bo + 1))
            for h in range(4):
                bb = bo + h // 2
                o = ps[:, bb, (h % 2) * 256:(h % 2 + 1) * 256]
                P.op("pe", lambda e, o=o, h=h: e.matmul(o, lhsT=scT[r][:, h, :], rhs=vb[j][:, h * 256:(h + 1) * 256], start=(h % 2 == 0), stop=False,
                                                        skip_group_check=True),
                     reads=[("scT", r), ("vb", j)], writes=[PB(bb)])
            bk = bank_pair()
            for h in range(4):
                bb = bk + h // 2
                o = ps[:, bb, (h % 2) * 256:(h % 2 + 1) * 256]
                P.op("pe", lambda e, o=o, h=h: e.matmul(o, lhsT=ktm[r][:, h * 128:(h + 1) * 128], rhs=vb[j][:, h * 256:(h + 1) * 256], start=True, stop=True),
                     reads=[("ktm", r), ("vb", j)], writes=[PB(bb)])
            for h in range(4):
                bb = bk + h // 2
                o = ps[:, bb, (h % 2) * 256:(h % 2 + 1) * 256]
                P.op("act", lambda e, o=o, h=h: e.activation(out=tkv4[:, h, :], in_=o, func=AF.Copy, scale=edl[:, g * 4 + h:g * 4 + h + 1]),
                     reads=[PB(bb), ("edl", g)], writes=[("tkv", h)])
            for h in range(4):
                P.op("dve", lambda e, h=h: e.scalar_tensor_tensor(out=S[:, h, :], in0=S[:, h, :], scalar=elast[:, g * 4 + h:g * 4 + h + 1],
                                                                 in1=tkv4[:, h, :], op0=ALU.mult, op1=ALU.add),
                     reads=["S", ("elast", g), ("tkv", h)], writes=["S"])

        def R_o2(s, j):
            G = s * NT + j
            g = G % NS
            r = g % 2
            js = slice(j * 128, (j + 1) * 128)
            bo = rbank[g]
            for h in range(4):
                bb = bo + h // 2
                o = ps[:, bb, (h % 2) * 256:(h % 2 + 1) * 256]
                P.op("pe", lambda e, o=o, h=h: e.matmul(o, lhsT=qiT[:, h, js], rhs=Sp[r][:, h, :], start=False, stop=True, skip_group_check=True),
                     reads=[("qiT", h), ("Sp", r)], writes=[PB(bb)])
            for h in range(4):
                bb = bo + h // 2
                o = ps[:, bb, (h % 2) * 256:(h % 2 + 1) * 256]
                P.op("act", lambda e, o=o, h=h: e.activation(out=junk[:, h * 256:(h + 1) * 256], in_=o, func=AF.Square, accum_out=sso[:, g * 4 + h:g * 4 + h + 1]),
                     reads=[PB(bb)], writes=[("sso", g, h), ("junk", h)])
            rstd_chain(sso[:, g * 4:g * 4 + 4], rso[:, g * 4:g * 4 + 4], 1.0 / 256, EPS, [("sso", g, h) for h in range(4)], ("rso", g), w=4)

        def R_on(s, j):
            G = s * NT + j
            g = G % NS
            r = g % 2
            bo = rbank[g]
            for h in range(4):
                bb = bo + h // 2
                o = ps[:, bb, (h % 2) * 256:(h % 2 + 1) * 256]
                P.op("dve", lambda e, o=o, h=h: e.scalar_tensor_tensor(out=BFs[r][:, h * 256:(h + 1) * 256], in0=o, scalar=rso[:, g * 4 + h:g * 4 + h + 1],
                                                                      in1=zgs[j][:, h * 256:(h + 1) * 256], op0=ALU.mult, op1=ALU.mult),
                     reads=[PB(bb), ("rso", g), ("szb", j)], writes=[("BFs", r)])
            pstate["held"].difference_update((bo, bo + 1))

        def R_tr(s, j):
            G = s * NT + j
            g = G % NS
            r = g % 2
            js = slice(j * 128, (j + 1) * 128)
            bt2 = bank()
            pv = ps[:, bt2, :].bitcast(BF16)
            for c in range(KC):
                P.op("pe", lambda e, c=c: e.transpose(pv[:, c * 128:(c + 1) * 128], BFs[r][:, c * 128:(c + 1) * 128], ident[:]),
                     reads=[("BFs", r), "ident"], writes=[PB(bt2)])
            P.op("act", lambda e: e.activation(out=onT[:, :, js], in_=pv[:, 0:1024].rearrange("p (c t) -> p c t", c=8), func=AF.Copy),
                 reads=[PB(bt2)], writes=["onT"])

        def M_a(s, c):
            hbuf = s % 2
            half, n = c // 4, c % 4
            r = c % 2
            rga = ring_take(s, "ga%d" % half)
            rwa = ring_take(s, "wa%d" % half)
            rgb = ring_take(s, "gb%d" % half)
            t1 = F32A[r][:, 0:512]
            sga = F32B[0]
            sgb = F32B[1]
            b = bank()
            proj_fm(rga, n * 128, 128, hbuf, b)
            P.op("act", lambda e: e.activation(out=sga[:], in_=ps[:, b, 0:TT], func=AF.Sigmoid), reads=[PB(b)], writes=[("F32B", 0)])
            b2 = bank()
            for kc in range(KC):
                P.op("pe", lambda e, kc=kc: e.matmul(ps[:, b2, 0:TT], lhsT=ring[rwa][:, kc, n * 128:(n + 1) * 128], rhs=guT[:, kc, :],
                                                     start=(kc == 0), stop=(kc == KC - 1)),
                     reads=[("ring", rwa), ("guT", kc)], writes=[PB(b2)])
            P.op("dve", lambda e: e.tensor_tensor(out=t1, in0=ps[:, b2, 0:TT], in1=sga[:], op=ALU.mult),
                 reads=[PB(b2), ("F32B", 0), ("F32A", r)], writes=[("F32A", r, 0)])
            b3 = bank()
            proj_fm(rgb, n * 128, 128, hbuf, b3)
            P.op("act", lambda e: e.activation(out=sgb[:], in_=ps[:, b3, 0:TT], func=AF.Sigmoid), reads=[PB(b3)], writes=[("F32B", 1)])

        def M_b(s, c):
            half, n = c // 4, c % 4
            r = c % 2
            rwb = ring_take(s, "wb%d" % half)
            t1 = F32A[r][:, 0:512]
            t2 = F32A[r][:, 512:1024]
            sgb = F32B[1]
            b4 = bank()
            for kc in range(KC):
                P.op("pe", lambda e, kc=kc: e.matmul(ps[:, b4, 0:TT], lhsT=ring[rwb][:, kc, n * 128:(n + 1) * 128], rhs=onT[:, kc, :],
                                                     start=(kc == 0), stop=(kc == KC - 1)),
                     reads=[("ring", rwb), "onT"], writes=[PB(b4)])
            P.op("dve", lambda e: e.tensor_tensor(out=t2, in0=ps[:, b4, 0:TT], in1=sgb[:], op=ALU.mult),
                 reads=[PB(b4), ("F32B", 1), ("F32A", r)], writes=[("F32A", r, 1)])
            P.op("dve", lambda e: e.tensor_tensor(out=mT[:, c, :], in0=t1, in1=t2, op=ALU.add),
                 reads=[("F32A", r, 0), ("F32A", r, 1)], writes=[("mT", c)])
            if n == 3:
                for nm in ("ga", "wa", "gb", "wb"):
                    ring_release(s, "%s%d" % (nm, half))

        def M_n(s, c):
            M_a(s, c)
            M_b(s, c)

        def Y_load(s, j):
            G = s * NT + j
            g = G % NS
            r = g % 2
            P.dma("sync", lambda e: e.dma_start(out=xr[r][:], in_=x[G * 128:(G + 1) * 128, :]), "xr%d" % r, writes=[("xr", r)])

        def Y(s, j):
            G = s * NT + j
            g = G % NS
            r = g % 2
            js = slice(j * 128, (j + 1) * 128)
            by = bank_pair()
            for half in range(2):
                rw = ring_take(s, "wo%d" % half)
                for kc in range(KC):
                    P.op("pe", lambda e, kc=kc, half=half, rw=rw: e.matmul(ps[:, by + half, :], lhsT=mT[:, kc, js], rhs=ring[rw][:, kc, :],
                                                                          start=(kc == 0), stop=(kc == KC - 1)),
                         reads=[("ring", rw), ("mT", kc)], writes=[PB(by + half)])
            P.op("dve", lambda e: e.tensor_tensor(out=xr[r][:], in0=ps[:, by:by + 2, :].rearrange("p a b -> p (a b)"), in1=xr[r][:], op=ALU.add),
                 reads=[PB(by), PB(by + 1), ("xr", r)], writes=[("xr", r)])
            P.op("act", lambda e: e.activation(out=junk[:], in_=xr[r][:], func=AF.Square, accum_out=ssf[:, g:g + 1]),
                 reads=[("xr", r)], writes=[("ssf", g)] + JK)
            rstd_chain(ssf[:, g:g + 1], rsf[:, g:g + 1], 1.0 / D, EPS, ("ssf", g), ("rsf", g))
            P.op("dve", lambda e: e.scalar_tensor_tensor(out=xr[r][:], in0=xr[r][:], scalar=rsf[:, g:g + 1], in1=Gf[:], op0=ALU.mult, op1=ALU.mult),
                 reads=[("xr", r), ("rsf", g), "Gf"], writes=[("xr", r)])
            P.dma("sync", lambda e: e.dma_start(out=out[G * 128:(G + 1) * 128, :], in_=xr[r][:]), "st%d" % r, reads=[("xr", r)], final=True)
            if j == NT - 1:
                ring_release(s, "wo0")
                ring_release(s, "wo1")

        zgs = szb

        def B_zg(s, j):
            for h in range(4):
                P.op("dve", lambda e, h=h: e.tensor_tensor(out=szb[j][:, h * 256:(h + 1) * 256], in0=szb[j][:, h * 256:(h + 1) * 256], in1=ggb[:], op=ALU.mult),
                     reads=[("szb", j), "ggb"], writes=[("szb", j)])

        for j in range(NT):
            fr_a(0, j)
            fr_b(0, j)
        Wraw = F32A[0][:, :].rearrange("p (h s) -> p h s", h=8)
        cload(Wraw, w_sp.rearrange("h t s -> t h s"), "c_wsp", writes=[("F32A", 0)])
        for hh in range(2):
            b = bank()
            for h4 in range(4):
                h = hh * 4 + h4
                P.op("pe", lambda e, b=b, h=h, h4=h4: e.matmul(ps[:, b, h4 * 128:(h4 + 1) * 128], lhsT=Wraw[:, h, :], rhs=identf,
                                                              start=True, stop=True),
                     reads=[("F32A", 0), "cst"], writes=[PB(b)])
            for h4 in range(4):
                h = hh * 4 + h4
                P.op("dve", lambda e, b=b, h=h, h4=h4: e.tensor_tensor(out=WsT[:, h, :], in0=ps[:, b, h4 * 128:(h4 + 1) * 128], in1=maskf,
                                                                      op=ALU.mult),
                     reads=[PB(b), "cst"], writes=["WsT"])
        bsf = F32A[1]
        cload(bsf[0:1, :], b_sp[0:1, :], "c_bs0", writes=[("F32A", 1)])
        cload(bsf[32:33, :], b_sp[0:1, :], "c_bs1", writes=[("F32A", 1)])
        P.op("dve", lambda e: e.tensor_copy(out=bspad[0:1, :], in_=bsf[0:1, :]), reads=[("F32A", 1)], writes=["bspad"])
        P.op("dve", lambda e: e.tensor_copy(out=BFs[0][32:33, :], in_=bsf[32:33, :]), reads=[("F32A", 1)], writes=[("BFs", 0)])
        P.op("dve", lambda e: e.tensor_tensor(out=bspad[32:33, :], in0=bsf[32:33, :], in1=BFs[0][32:33, :], op=ALU.subtract),
             reads=[("F32A", 1), ("BFs", 0)], writes=["bspad"])

        bcast_row(Gln, ln_g[0:1, :], D, xr[1], ("xr", 1), "c_gln", "Gln")
        bcast_row(Bln, ln_b[0:1, :], D, xr[0], ("xr", 0), "c_bln", "Bln")
        bcast_row(ggb, gla_g[0:1, :], 256, xr[1], ("xr", 1), "c_ggb", "ggb")
        bcast_row(Gf, fin_g[0:1, :], D, xr[0], ("xr", 0), "c_gf", "Gf")
        P.op("dve", lambda e: e.tensor_scalar(out=Gf[:], in0=Gf[:], scalar1=float(D ** 0.5), scalar2=None, op0=ALU.mult), reads=["Gf"], writes=["Gf"])
        P.op("dve", lambda e: e.tensor_scalar(out=ggb[:], in0=ggb[:], scalar1=16.0, scalar2=None, op0=ALU.mult), reads=["ggb"], writes=["ggb"])
        ring_pump()
        for s in range(NST):
            last = (s + 1 == NST)
            if s == 0:
                B_lr(s)
                B_logit(s, 0); B_logit(s, 1)
            B_vb(s, 0)
            B_cum_a(s, 0); B_cum_a(s, 1)
            B_logit(s, 2); B_logit(s, 3)
            B_cum_b(s, 0); B_cum_b(s, 1)
            B_vb(s, 1)
            B_vb(s, 2)
            B_cum_a(s, 2); B_cum_a(s, 3)
            B_cum_b(s, 2); B_cum_b(s, 3)
            B_vb(s, 3)
            B_zb(s, 0); B_zb(s, 1)
            for h in range(4):
                B_q(s, h)
            for h in range(4):
                B_k(s, h)
            B_zb(s, 2); B_zb(s, 3)
            for c in range(8):
                A_za(s, c)
            for j in range(NT):
                B_zg(s, j)
            for j in range(NT):
                R_sc(s, j)
                if j > 0:
                    A_sp(s, j - 1)
                    R_on(s, j - 1)
                A_v(s, j)
                R_o1(s, j)
                if j > 0:
                    R_tr(s, j - 1)
                A_u(s, 2 * j)
                A_u(s, 2 * j + 1)
                R_o2(s, j)
            A_sp(s, NT - 1)
            R_on(s, NT - 1)
            M_a(s, 0)
            R_tr(s, NT - 1)
            seq = {0: ("a", 0), 1: ("a", 1), 2: ("b", 0), 3: ("a", 2), 4: ("b", 1), 5: ("a", 3), 6: ("b", 2), 7: ("b", 3)}
            for c in range(8):
                if c > 0:
                    M_a(s, c)
                M_b(s, c)
                if not last:
                    kind, jj = seq[c]
                    if c == 7:
                        pass
                    elif kind == "a":
                        fr_a(s + 1, jj)
                    else:
                        fr_b(s + 1, jj)
                        if c == 6:
                            fr_b(s + 1, 3)
                    if c == 2:
                        pass
                if c >= 4:
                    Y_load(s, c - 4) if c - 4 < 2 else None
            if not last:
                B_lr(s + 1)
            for j in range(NT):
                Y(s, j)
                if j + 2 < NT:
                    Y_load(s, j + 2)
                if not last and j < 2:
                    B_logit(s + 1, j)
            if dbg and s == 0:
                P.dma("sync", lambda e: e.dma_start(out=d_hT[:, :, :], in_=hT[0][:]), "dbg0", reads=[("hT", 0)], final=True)
                P.dma("sync", lambda e: e.dma_start(out=d_aT[:, :, :], in_=guT[:]), "dbg1", reads=[("guT", c) for c in range(8)], final=True)
                P.dma("sync", lambda e: e.dma_start(out=d_onT[:, :, :], in_=onT[:]), "dbg2", reads=["onT"], final=True)
                P.dma("sync", lambda e: e.dma_start(out=d_mT[:, :, :], in_=mT[:]), "dbg3", reads=[("mT", c) for c in range(8)], final=True)

        print("sbuf bytes remaining", nc.sbuf_bytes_remaining)
        block = es.enter_context(nc.Block())
        P.emit(block)
    return nc


def _consts():
    c = np.zeros((128, 512), np.float32)
    c[0, 384:512] = 1.0
    c[:, 0:128] = np.eye(128, dtype=np.float32)
    tri = (np.arange(128)[:, None] <= np.arange(128)[None, :]).astype(np.float32)
    c[:, 128:256] = tri * np.float32(-1.0 / 16.0)
    c[:, 256:384] = tri
    return c


def make_in_maps(x, norm_g, w_in, ln_v_g, ln_v_b, w_spatial, b_spatial, w_gate_up, b_gate_up,
                 gla_norm_g, w_branch_a, w_branch_b, w_out, final_norm_g, n_cores, T):
    f = lambda a: np.ascontiguousarray(np.asarray(a, dtype=np.float32))
    shared = {
        "w_in": f(w_in[0]), "w_a": f(w_branch_a[0]), "w_b": f(w_branch_b[0]), "w_o": f(w_out[0]),
        "norm_g": f(norm_g[0]).reshape(1, D), "ln_g": f(ln_v_g[0]).reshape(1, D), "ln_b": f(ln_v_b[0]).reshape(1, D),
        "w_sp": f(w_spatial[0]), "b_sp": f(b_spatial[0]).reshape(1, 1024),
        "w_gu": f(w_gate_up[0]), "b_gu": f(b_gate_up[0]).reshape(1, 512),
        "gla_g": f(gla_norm_g[0]).reshape(1, 256), "fin_g": f(final_norm_g).reshape(1, D),
        "consts": _consts(),
    }
    maps = []
    for b in range(n_cores):
        m = dict(shared)
        m["x"] = f(np.asarray(x)[b, :T])
        maps.append(m)
    return maps


_NC_CACHE = {}


def kernel(x, norm_g, w_in, ln_v_g, ln_v_b, w_spatial, b_spatial, w_gate_up, b_gate_up,
           gla_norm_g, w_branch_a, w_branch_b, w_out, final_norm_g):
    x = np.asarray(x)
    B, T, _ = x.shape
    nc = build_program(T)
    in_maps = make_in_maps(x, norm_g, w_in, ln_v_g, ln_v_b, w_spatial, b_spatial, w_gate_up, b_gate_up,
                           gla_norm_g, w_branch_a, w_branch_b, w_out, final_norm_g, B, T)
    res = run_bass_kernel_spmd(nc, in_maps, core_ids=list(range(B)))
    return np.stack([np.asarray(r["out"], dtype=np.float32) for r in res.results], axis=0)
```

```python
import numpy as np
from contextlib import ExitStack
import concourse.bass as bass
import concourse.mybir as mybir
from concourse.bass_utils import run_bass_kernel_spmd

F32 = mybir.dt.float32
BF16 = mybir.dt.bfloat16
AF = mybir.ActivationFunctionType
ALU = mybir.AluOpType

D = 1024
NIN = 8208
NT = 4
TT = NT * 128
KC = 8
NB = 5
NS = 8
EPS = 1e-6
LN_EPS = 1e-5
C_U, C_V, C_ZA, C_Q, C_K, C_VB, C_ZB, C_LR, C_GA, C_GB = 0, 1024, 2048, 3072, 3584, 4096, 5120, 6144, 6160, 7184


class Prog:
    ENGS = ("pe", "act", "dve", "pool", "sync")

    def __init__(self, nc, es):
        self.nc = nc
        self.es = es
        self.ops = {e: [] for e in self.ENGS}
        self.cnt = {e: 0 for e in self.ENGS}
        self.sem = {e: es.enter_context(nc.semaphore("sem_" + e)) for e in ("pe", "act", "dve", "pool")}
        self.dsem = {}
        self.dcnt = {}
        self.res = {}
        self.known = {e: {} for e in self.ENGS}
        self.final_waits = []

    def _deps(self, reads, writes, eng=None):
        deps = []
        for k in reads:
            r = self.res.get(k)
            if r and r["w"] is not None:
                deps.append(r["w"])
            if r and isinstance(k, tuple) and k[0] == "ps":
                deps.extend(ev for ev in r["r"] if ev[0] != eng)
        for k in writes:
            r = self.res.get(k)
            if r:
                if r["w"] is not None:
                    deps.append(r["w"])
                deps.extend(r["r"])
        return deps

    def _record(self, ev, reads, writes):
        for k in reads:
            r = self.res.setdefault(k, {"w": None, "r": []})
            r["r"].append(ev)
        for k in writes:
            self.res[k] = {"w": ev, "r": []}

    def _waits(self, eng, deps):
        best = {}
        for (semname, val) in deps:
            if eng == "pe" and semname == "pe":
                continue
            if val > best.get(semname, 0):
                best[semname] = val
        waits = []
        kn = self.known[eng]
        for semname, val in best.items():
            if kn.get(semname, 0) >= val:
                continue
            kn[semname] = val
            waits.append((semname, val))
        return waits

    def op(self, eng, fn, reads=(), writes=()):
        deps = self._deps(reads, writes, eng)
        waits = self._waits(eng, deps)
        self.cnt[eng] += 1
        ev = (eng, self.cnt[eng])
        self.ops[eng].append((waits, fn, ev))
        self._record(ev, reads, writes)
        return ev

    def dma(self, eng, fn, semkey, reads=(), writes=(), final=False):
        if semkey not in self.dsem:
            self.dsem[semkey] = self.es.enter_context(self.nc.semaphore("d_" + semkey))
            self.dcnt[semkey] = 0
        deps = self._deps(reads, writes)
        waits = self._waits(eng, deps)
        self.dcnt[semkey] += 16
        ev = ("d:" + semkey, self.dcnt[semkey])
        self.ops[eng].append((waits, fn, ev))
        self._record(ev, reads, writes)
        if final:
            self.final_waits.append(ev)
        return ev

    def _semh(self, name):
        if name.startswith("d:"):
            return self.dsem[name[2:]]
        return self.sem[name]

    def emit(self, block):
        def run(engname):
            def body(e):
                for waits, fn, ev in self.ops[engname]:
                    for (sn, val) in waits:
                        e.wait_ge(self._semh(sn), val)
                    ins = fn(e)
                    ins.then_inc(self._semh(ev[0]), 16 if ev[0].startswith("d:") else 1)
                if engname == "sync":
                    best = {}
                    for sn, val in self.final_waits:
                        best[sn] = max(best.get(sn, 0), val)
                    for sn, val in best.items():
                        e.wait_ge(self._semh(sn), val)
            return body
        block.tensor(run("pe"))
        block.scalar(run("act"))
        block.vector(run("dve"))
        block.gpsimd(run("pool"))
        block.sync(run("sync"))


def build_program(T, dbg=False):
    NST = T // TT
    NTOT = T // 128
    nc = bass.Bass("TRN2", target_bir_lowering=False)

    def din(name, shape):
        return nc.dram_tensor(name, shape, F32, kind="ExternalInput").ap()

    x = din("x", [T, D])
    w_in = din("w_in", [D, NIN])
    w_a = din("w_a", [D, D])
    w_b = din("w_b", [D, D])
    w_o = din("w_o", [D, D])
    norm_g = din("norm_g", [1, D])
    ln_g = din("ln_g", [1, D])
    ln_b = din("ln_b", [1, D])
    w_sp = din("w_sp", [8, 128, 128])
    b_sp = din("b_sp", [1, 1024])
    w_gu = din("w_gu", [16, 512])
    b_gu = din("b_gu", [1, 512])
    gla_g = din("gla_g", [1, 256])
    fin_g = din("fin_g", [1, D])
    consts = din("consts", [128, 512])
    out = nc.dram_tensor("out", [T, D], F32, kind="ExternalOutput").ap()
    if dbg:
        d_hT = nc.dram_tensor("d_hT", [128, KC, TT], BF16, kind="ExternalOutput").ap()
        d_aT = nc.dram_tensor("d_aT", [128, KC, TT], BF16, kind="ExternalOutput").ap()
        d_onT = nc.dram_tensor("d_onT", [128, KC, TT], BF16, kind="ExternalOutput").ap()
        d_mT = nc.dram_tensor("d_mT", [128, KC, TT], BF16, kind="ExternalOutput").ap()

    NCG = 22
    wscr = nc.dram_tensor("wscr", [NCG, 128, KC, 512], BF16).ap()
    w_in_v = w_in.rearrange("(kc p) n -> p kc n", p=128)
    w_a_v = w_a.rearrange("(kc p) n -> p kc n", p=128)
    w_b_v = w_b.rearrange("(kc p) n -> p kc n", p=128)
    w_o_v = w_o.rearrange("(kc p) n -> p kc n", p=128)

    with ExitStack() as es:
        def sb(name, shape, dt):
            return es.enter_context(nc.sbuf_tensor(name, shape, dt))

        P = Prog(nc, es)
        ps = es.enter_context(nc.psum_tensor("ps", [128, 8, 512], F32))

        Gx = sb("Gx", [128, D], F32)
        Gf = sb("Gf", [128, D], F32)
        Gln = sb("Gln", [128, D], F32)
        Bln = sb("Bln", [128, D], F32)
        ggb = sb("ggb", [128, 256], F32)
        cst = sb("cst", [128, 512], F32)
        identf = cst[:, 0:128]
        Uneg = cst[:, 128:256]
        maskf = cst[:, 256:384]
        sel0 = cst[:, 384:512]
        ident = sb("ident", [128, 128], BF16)
        WsT = sb("WsT", [128, 8, 128], BF16)
        bspad = sb("bspad", [128, 1024], BF16)
        onespad = sb("onespad", [128, 128], BF16)
        wgu = sb("wgu", [128, 512], BF16)
        wlr = sb("wlr", [128, KC, 16], BF16)
        ring = [sb("ring%d" % i, [128, KC, 512], BF16) for i in range(NB)]
        hT = [sb("hT%d" % i, [128, KC, TT], BF16) for i in range(2)]
        xt = [sb("xt%d" % i, [128, D], F32) for i in range(2)]
        xr = [sb("xr%d" % i, [128, D], F32) for i in range(2)]
        junk = sb("junk", [128, D], BF16)
        BFs = [sb("BFs%d" % i, [128, D], BF16) for i in range(2)]
        guT = sb("guT", [128, KC, TT], BF16)
        sza = [sb("sza%d" % i, [128, TT], BF16) for i in range(2)]
        F32A = [sb("F32A%d" % i, [128, D], F32) for i in range(2)]
        F32B = [sb("F32B%d" % i, [128, 512], F32) for i in range(3)]
        lrT = [sb("lrT%d" % i, [128, TT], BF16) for i in range(2)]
        Et = sb("Et", [128, 4, TT], F32)
        Einv = sb("Einv", [128, 4, TT], F32)
        qiT = sb("qiT", [128, 4, TT], BF16)
        kiT = sb("kiT", [128, 4, TT], BF16)
        ktm = [sb("ktm%d" % i, [128, 512], BF16) for i in range(2)]
        vb = [sb("vb%d" % i, [128, D], BF16) for i in range(NT)]
        szb = [sb("szb%d" % i, [128, D], BF16) for i in range(NT)]
        scT = [sb("scT%d" % i, [128, 4, 128], BF16) for i in range(2)]
        S = sb("S", [128, 4, 256], F32)
        tkv4 = sb("tkv4", [128, 4, 256], F32)
        Sp = [sb("Sp%d" % i, [128, 4, 256], BF16) for i in range(2)]
        onT = sb("onT", [128, KC, TT], BF16)
        mT = sb("mT", [128, KC, TT], BF16)
        rconst = sb("rconst", [128, 16], F32)
        ssx = sb("ssx", [128, NS], F32)
        rsx = sb("rsx", [128, NS], F32)
        bst = sb("bst", [128, NS * 12], F32)
        mvv = sb("mvv", [128, NS * 2], F32)
        rsv = sb("rsv", [128, NS], F32)
        nbm = sb("nbm", [128, NS * 4], F32)
        pbm = sb("pbm", [128, NS * 4], F32)
        dlt = sb("dlt", [128, NS * 4], F32)
        emid = sb("emid", [128, NS * 4], F32)
        elast = sb("elast", [128, NS * 4], F32)
        edl = sb("edl", [128, NS * 4], F32)
        sso = sb("sso", [128, NS * 4], F32)
        rso = sb("rso", [128, NS * 4], F32)
        ssf = sb("ssf", [128, NS], F32)
        rsf = sb("rsf", [128, NS], F32)

        pstate = {"ptr": 0, "held": set()}

        def bank():
            while pstate["ptr"] % 8 in pstate["held"]:
                pstate["ptr"] += 1
            b = pstate["ptr"] % 8
            pstate["ptr"] += 1
            return b

        def bank_pair():
            while True:
                if pstate["ptr"] % 2 == 1:
                    pstate["ptr"] += 1
                b = pstate["ptr"] % 8
                if b in pstate["held"] or (b + 1) in pstate["held"]:
                    pstate["ptr"] += 2
                    continue
                pstate["ptr"] += 2
                return b

        def PB(b):
            return ("ps", b)

        CG_ORDER = ["vb0", "vb1", "zb0", "zb1", "q", "k", "za0", "za1", "v0", "v1", "u0", "u1",
                    "ga0", "wa0", "gb0", "wb0", "ga1", "wa1", "gb1", "wb1", "wo0", "wo1"]
        CG_SRC = {"vb0": (w_in_v, C_VB), "vb1": (w_in_v, C_VB + 512), "zb0": (w_in_v, C_ZB), "zb1": (w_in_v, C_ZB + 512),
                  "q": (w_in_v, C_Q), "k": (w_in_v, C_K), "v0": (w_in_v, C_V), "v1": (w_in_v, C_V + 512),
                  "u0": (w_in_v, C_U), "u1": (w_in_v, C_U + 512), "za0": (w_in_v, C_ZA), "za1": (w_in_v, C_ZA + 512),
                  "ga0": (w_in_v, C_GA), "ga1": (w_in_v, C_GA + 512), "gb0": (w_in_v, C_GB), "gb1": (w_in_v, C_GB + 512),
                  "wa0": (w_a_v, 0), "wa1": (w_a_v, 512), "wb0": (w_b_v, 0), "wb1": (w_b_v, 512),
                  "wo0": (w_o_v, 0), "wo1": (w_o_v, 512)}
        cgs = []
        cg_index = {}
        for s in range(NST):
            for ci, name in enumerate(CG_ORDER):
                cg_index[(s, name)] = len(cgs)
                cgs.append((s, ci, CG_SRC[name]))
        rstate = {"next_load": 0, "released": set()}

        def ring_pump():
            while rstate["next_load"] < len(cgs):
                i = rstate["next_load"]
                if i >= NB and (i - NB) not in rstate["released"]:
                    break
                st, ci, (view, c0) = cgs[i]
                buf = ring[i % NB]
                if st == 0:
                    P.dma("pool", lambda e, buf=buf, view=view, c0=c0: e.dma_start(out=buf[:], in_=view[:, :, c0:c0 + 512]),
                          "rp%d" % (i % NB), writes=[("ring", i % NB)])
                    if NST > 1:
                        P.dma("sync", lambda e, buf=buf, ci=ci: e.dma_start(out=wscr[ci], in_=buf[:]),
                              "scr%d" % ci, reads=[("ring", i % NB)], writes=[("scr", ci)])
                else:
                    P.dma("sync", lambda e, buf=buf, ci=ci: e.dma_start(out=buf[:], in_=wscr[ci]),
                          "rs%d" % (i % NB), reads=[("scr", ci)], writes=[("ring", i % NB)])
                rstate["next_load"] += 1

        def ring_take(s, name):
            i = cg_index[(s, name)]
            assert i < rstate["next_load"], ("CG not loaded yet", s, name)
            return i % NB

        def ring_release(s, name):
            rstate["released"].add(cg_index[(s, name)])
            ring_pump()

        def cload(dst, src, key, eng="sync", writes=()):
            P.dma(eng, lambda e: e.dma_start(out=dst, in_=src), key, writes=list(writes))

        cload(cst[:], consts[:, :], "c_cst", writes=["cst"])
        def bcast_row(dst, src_row, W, stg, stgkey, key, dkey):
            P.op("dve", lambda e: e.memset(stg[:, 0:W], 0.0), writes=[stgkey])
            P.dma("sync", lambda e: e.dma_start(out=stg[0:1, 0:W], in_=src_row), key, writes=[stgkey])
            for c0 in range(0, W, 512):
                w = min(512, W - c0)
                b = bank()
                P.op("pe", lambda e, b=b, c0=c0, w=w: e.matmul(ps[:, b, 0:w], lhsT=sel0, rhs=stg[:, c0:c0 + w], start=True, stop=True),
                     reads=[stgkey, "cst"], writes=[PB(b)])
                P.op("act", lambda e, b=b, c0=c0, w=w: e.activation(out=dst[:, c0:c0 + w], in_=ps[:, b, 0:w], func=AF.Copy),
                     reads=[PB(b)], writes=[dkey])

        bcast_row(Gx, norm_g[0:1, :], D, xr[0], ("xr", 0), "c_gx", "Gx")
        P.op("dve", lambda e: e.memset(wgu[:], 0.0), writes=["wgu"])
        P.op("dve", lambda e: e.memset(bspad[:], 0.0), writes=["bspad"])
        P.op("dve", lambda e: e.memset(onespad[:], 0.0), writes=["onespad"])
        P.op("dve", lambda e: e.memset(onespad[0:1, :], 1.0), writes=["onespad"])
        P.op("dve", lambda e: e.memset(onespad[32:33, :], 1.0), writes=["onespad"])
        for i in range(2):
            P.op("dve", lambda e, i=i: e.memset(lrT[i][:], 0.0), writes=[("lrT", i)])
            P.op("dve", lambda e, i=i: e.memset(lrT[i][32:33, :], 1.0), writes=[("lrT", i)])
        P.op("dve", lambda e: e.memset(S[:], 0.0), writes=["S"])
        P.op("dve", lambda e: e.memset(rconst[:, 0:4], float(D * EPS)), writes=["rconst"])
        P.op("dve", lambda e: e.memset(rconst[:, 4:8], float(LN_EPS)), writes=["rconst"])
        P.op("dve", lambda e: e.memset(rconst[:, 8:12], float(256 * EPS)), writes=["rconst"])
        P.op("dve", lambda e: e.memset(rconst[:, 12:16], -0.5), writes=["rconst"])
        P.op("dve", lambda e: e.tensor_scalar(out=Gx[:], in0=Gx[:], scalar1=float(D ** 0.5), scalar2=None, op0=ALU.mult), reads=["Gx"], writes=["Gx"])
        P.op("dve", lambda e: e.tensor_copy(out=ident[:], in_=identf), reads=["cst"], writes=["ident"])
        cload(wgu[0:16, :], w_gu[:, :], "c_wgu", eng="pool", writes=["wgu"])
        cload(wgu[32:33, :], b_gu[0:1, :], "c_bgu", eng="pool", writes=["wgu"])
        cload(wlr[:], w_in_v[:, :, C_LR:C_LR + 16], "c_wlr", eng="pool", writes=["wlr"])
        hn = [sb("hn%d" % i, [128, D], BF16) for i in range(2)]
        JK = [("junk", h) for h in range(4)]

        cur = {"st": -1}

        def rstd_chain(ss_ap, out_ap, scale, eps, key_in, key_out, w=1):
            keys_in = key_in if isinstance(key_in, list) else [key_in]
            ceps = {1.0 / D: 0, 1.0: 1, 1.0 / 256: 2}[scale]
            if cur["st"] == 0:
                cval = float({0: D * EPS, 1: LN_EPS, 2: 256 * EPS}[ceps])
                P.op("act", lambda e: e.activation(out=out_ap, in_=ss_ap, func=AF.Ln, bias=cval, scale=1.0),
                     reads=keys_in, writes=[key_out])
                P.op("act", lambda e: e.activation(out=out_ap, in_=out_ap, func=AF.Exp, scale=-0.5),
                     reads=[key_out], writes=[key_out])
                return
            P.op("pool", lambda e: e.tensor_tensor(out=out_ap, in0=ss_ap, in1=rconst[:, ceps * 4:ceps * 4 + w], op=ALU.add),
                 reads=keys_in + ["rconst"], writes=[key_out])
            P.op("pool", lambda e: e.tensor_tensor(out=out_ap, in0=out_ap, in1=rconst[:, 12:12 + w], op=ALU.pow),
                 reads=[key_out, "rconst"], writes=[key_out])

        def proj_fm(rb, c_off, M, hbuf, b):
            for kc in range(KC):
                P.op("pe", lambda e, kc=kc: e.matmul(ps[0:M, b, 0:TT], lhsT=ring[rb][:, kc, c_off:c_off + M], rhs=hT[hbuf][:, kc, :],
                                                     start=(kc == 0), stop=(kc == KC - 1)),
                     reads=[("ring", rb), ("hT", hbuf)], writes=[PB(b)])

        def proj_tm(rb, j, hbuf, b):
            for kc in range(KC):
                P.op("pe", lambda e, kc=kc: e.matmul(ps[:, b, :], lhsT=hT[hbuf][:, kc, j * 128:(j + 1) * 128], rhs=ring[rb][:, kc, :],
                                                     start=(kc == 0), stop=(kc == KC - 1)),
                     reads=[("ring", rb), ("hT", hbuf)], writes=[PB(b)])

        def fr_a(s, j):
            G = s * NT + j
            g = G % NS
            r = g % 2
            P.dma("sync", lambda e: e.dma_start(out=xt[r][:], in_=x[G * 128:(G + 1) * 128, :]), "xt%d" % r, writes=[("xt", r)])
            P.op("act", lambda e: e.activation(out=junk[:], in_=xt[r][:], func=AF.Square, accum_out=ssx[:, g:g + 1]),
                 reads=[("xt", r)], writes=[("ssx", g)] + JK)
            rstd_chain(ssx[:, g:g + 1], rsx[:, g:g + 1], 1.0 / D, EPS, ("ssx", g), ("rsx", g))
            P.op("dve", lambda e: e.scalar_tensor_tensor(out=hn[r][:], in0=xt[r][:], scalar=rsx[:, g:g + 1], in1=Gx[:],
                                                         op0=ALU.mult, op1=ALU.mult),
                 reads=[("xt", r), ("rsx", g), "Gx"], writes=[("hn", r)])

        def fr_b(s, j):
            G = s * NT + j
            g = G % NS
            r = g % 2
            hbuf = s % 2
            b = bank()
            pv = ps[:, b, :].bitcast(BF16)
            for kc in range(KC):
                P.op("pe", lambda e, kc=kc: e.transpose(pv[:, kc * 128:(kc + 1) * 128], hn[r][:, kc * 128:(kc + 1) * 128], ident[:]),
                     reads=[("hn", r), "ident"], writes=[PB(b)])
            P.op("dve", lambda e: e.tensor_copy(out=hT[hbuf][:, :, j * 128:(j + 1) * 128],
                                                in_=pv[:, 0:1024].rearrange("p (c t) -> p c t", c=8)),
                 reads=[PB(b)], writes=[("hT", hbuf)])

        def A_za(s, c):
            rb = ring_take(s, "za%d" % (c // 4))
            b = bank()
            proj_fm(rb, (c % 4) * 128, 128, s % 2, b)
            P.op("act", lambda e: e.activation(out=guT[:, c, :], in_=ps[:, b, 0:TT], func=AF.Silu),
                 reads=[PB(b)], writes=[("guT", c)])
            if c % 4 == 3:
                ring_release(s, "za%d" % (c // 4))

        def A_u(s, c):
            rb = ring_take(s, "u%d" % (c // 4))
            b = bank()
            proj_fm(rb, (c % 4) * 128, 128, s % 2, b)
            P.op("act", lambda e: e.activation(out=sza[c % 2][:], in_=ps[:, b, 0:TT], func=AF.Gelu),
                 reads=[PB(b)], writes=[("sza", c % 2)])
            P.op("dve" if cur["st"] == 0 else "pool", lambda e: e.tensor_tensor(out=guT[:, c, :], in0=guT[:, c, :], in1=sza[c % 2][:], op=ALU.mult),
                 reads=[("guT", c), ("sza", c % 2)], writes=[("guT", c)])
            if c % 4 == 3:
                ring_release(s, "u%d" % (c // 4))

        def A_v(s, j):
            G = s * NT + j
            g = G % NS
            r = g % 2
            gv = F32A[r]
            for half in range(2):
                rb = ring_take(s, "v%d" % half)
                b = bank()
                proj_tm(rb, j, s % 2, b)
                P.op("act", lambda e, b=b, half=half: e.activation(out=gv[:, half * 512:(half + 1) * 512], in_=ps[:, b, :], func=AF.Gelu),
                     reads=[PB(b)], writes=[("F32A", r)])
            for half in range(2):
                P.op("dve", lambda e, half=half: e.bn_stats(out=bst[:, g * 12 + half * 6:g * 12 + half * 6 + 6],
                                                            in_=gv[:, half * 512:(half + 1) * 512]),
                     reads=[("F32A", r)], writes=[("bst", g, half)])
            P.op("dve", lambda e: e.bn_aggr(out=mvv[:, g * 2:g * 2 + 2], in_=bst[:, g * 12:g * 12 + 12]),
                 reads=[("bst", g, 0), ("bst", g, 1)], writes=[("mvv", g)])
            rstd_chain(mvv[:, g * 2 + 1:g * 2 + 2], rsv[:, g:g + 1], 1.0, LN_EPS, ("mvv", g), ("rsv", g))
            P.op("dve", lambda e: e.scalar_tensor_tensor(out=gv[:], in0=gv[:], scalar=mvv[:, g * 2:g * 2 + 1], in1=Gln[:],
                                                         op0=ALU.subtract, op1=ALU.mult),
                 reads=[("F32A", r), ("mvv", g), "Gln"], writes=[("F32A", r)])
            P.op("dve", lambda e: e.scalar_tensor_tensor(out=BFs[r][:], in0=gv[:], scalar=rsv[:, g:g + 1], in1=Bln[:],
                                                         op0=ALU.mult, op1=ALU.add),
                 reads=[("F32A", r), ("rsv", g), "Bln"], writes=[("BFs", r)])
            if j == NT - 1:
                ring_release(s, "v0")
                ring_release(s, "v1")

        def A_sp(s, j):
            G = s * NT + j
            g = G % NS
            r = g % 2
            b2 = bank_pair()
            for h in range(8):
                bb = b2 + h // 4
                o = ps[:, bb, (h % 4) * 128:(h % 4 + 1) * 128]
                P.op("pe", lambda e, o=o, h=h: e.matmul(o, lhsT=BFs[r][:, h * 128:(h + 1) * 128], rhs=WsT[:, h, :], start=True, stop=False),
                     reads=[("BFs", r), "WsT"], writes=[PB(bb)])
                P.op("pe", lambda e, o=o, h=h: e.matmul(o, lhsT=onespad[:], rhs=bspad[:, h * 128:(h + 1) * 128], start=False, stop=True),
                     reads=["onespad", "bspad"], writes=[PB(bb)])
            for half in range(2):
                bb = b2 + half
                P.op("dve", lambda e, bb=bb, half=half: e.tensor_tensor(
                    out=guT[:, half * 4:(half + 1) * 4, j * 128:(j + 1) * 128],
                    in0=ps[:, bb, :].rearrange("p (h t) -> p h t", h=4),
                    in1=guT[:, half * 4:(half + 1) * 4, j * 128:(j + 1) * 128], op=ALU.mult),
                    reads=[PB(bb)] + [("guT", c) for c in range(half * 4, half * 4 + 4)],
                    writes=[("guT", c) for c in range(half * 4, half * 4 + 4)])

        def B_lr(s):
            hbuf = s % 2
            lb = s % 2
            b = bank()
            for kc in range(KC):
                P.op("pe", lambda e, kc=kc: e.matmul(ps[0:16, b, 0:TT], lhsT=wlr[:, kc, :], rhs=hT[hbuf][:, kc, :],
                                                     start=(kc == 0), stop=(kc == KC - 1)),
                     reads=["wlr", ("hT", hbuf)], writes=[PB(b)])
            P.op("dve", lambda e: e.tensor_copy(out=lrT[lb][0:16, :], in_=ps[0:16, b, 0:TT]), reads=[PB(b)], writes=[("lrT", lb)])

        def B_logit(s, j):
            G = s * NT + j
            g = G % NS
            lb = s % 2
            ev = F32B[0]
            ls = F32B[1 + g % 2]
            lskey = ("F32B", 1 + g % 2)
            b = bank()
            P.op("pe", lambda e: e.matmul(ps[:, b, :], lhsT=lrT[lb][:, j * 128:(j + 1) * 128], rhs=wgu[:], start=True, stop=True),
                 reads=[("lrT", lb), "wgu"], writes=[PB(b)])
            P.op("act", lambda e: e.activation(out=ev[:], in_=ps[:, b, :], func=AF.Exp, scale=-1.0),
                 reads=[PB(b)], writes=[("F32B", 0)])
            P.op("act", lambda e: e.activation(out=ls[:], in_=ev[:], func=AF.Ln, bias=1.0, scale=1.0),
                 reads=[("F32B", 0)], writes=[lskey])

        cumbank = {}

        def B_cum_a(s, j):
            G = s * NT + j
            g = G % NS
            ls = F32B[1 + g % 2]
            lskey = ("F32B", 1 + g % 2)
            bc = bank()
            for h in range(4):
                P.op("pe", lambda e, h=h: e.matmul(ps[:, bc, h * 128:(h + 1) * 128], lhsT=ls[:, h * 128:(h + 1) * 128], rhs=Uneg,
                                                   start=True, stop=True),
                     reads=[lskey, "cst"], writes=[PB(bc)])
            cumbank[g] = bc
            pstate["held"].add(bc)

        def B_cum_b(s, j):
            G = s * NT + j
            g = G % NS
            bc = cumbank[g]
            bv = ps[:, bc, :].rearrange("p (h t) -> p h t", h=4)
            sl = slice(g * 4, g * 4 + 4)
            P.op("dve", lambda e: e.tensor_copy(out=pbm[:, sl], in_=bv[:, :, 63]), reads=[PB(bc)], writes=[("pbm", g)])
            P.op("dve", lambda e: e.tensor_scalar(out=nbm[:, sl], in0=bv[:, :, 63], scalar1=-1.0, scalar2=None, op0=ALU.mult),
                 reads=[PB(bc)], writes=[("nbm", g)])
            P.op("dve", lambda e: e.tensor_tensor(out=dlt[:, sl], in0=bv[:, :, 127], in1=pbm[:, sl], op=ALU.subtract),
                 reads=[PB(bc), ("pbm", g)], writes=[("dlt", g)])
            for h in range(4):
                P.op("act", lambda e, h=h: e.activation(out=Et[:, h, j * 128:(j + 1) * 128], in_=ps[:, bc, h * 128:(h + 1) * 128],
                                                        func=AF.Exp, bias=nbm[:, g * 4 + h:g * 4 + h + 1], scale=1.0),
                     reads=[PB(bc), ("nbm", g)], writes=[("Et", h)])
                P.op("act", lambda e, h=h: e.activation(out=Einv[:, h, j * 128:(j + 1) * 128], in_=ps[:, bc, h * 128:(h + 1) * 128],
                                                        func=AF.Exp, bias=pbm[:, g * 4 + h:g * 4 + h + 1], scale=-1.0),
                     reads=[PB(bc), ("pbm", g)], writes=[("Einv", h)])
            P.op("act", lambda e: e.activation(out=emid[:, sl], in_=pbm[:, sl], func=AF.Exp), reads=[("pbm", g)], writes=[("emid", g)])
            P.op("act", lambda e: e.activation(out=elast[:, sl], in_=bv[:, :, 127], func=AF.Exp), reads=[PB(bc)], writes=[("elast", g)])
            P.op("act", lambda e: e.activation(out=edl[:, sl], in_=dlt[:, sl], func=AF.Exp), reads=[("dlt", g)], writes=[("edl", g)])
            pstate["held"].discard(bc)

        def B_q(s, h):
            rb = ring_take(s, "q")
            b = bank()
            proj_fm(rb, h * 128, 128, s % 2, b)
            P.op("dve", lambda e: e.scalar_tensor_tensor(out=qiT[:, h, :], in0=ps[:, b, 0:TT], scalar=float(128 ** -0.5), in1=Et[:, h, :],
                                                         op0=ALU.mult, op1=ALU.mult),
                 reads=[PB(b), ("Et", h)], writes=[("qiT", h)])
            if h == 3:
                ring_release(s, "q")

        def B_k(s, h):
            rb = ring_take(s, "k")
            b = bank()
            proj_fm(rb, h * 128, 128, s % 2, b)
            P.op("dve", lambda e: e.tensor_tensor(out=kiT[:, h, :], in0=ps[:, b, 0:TT], in1=Einv[:, h, :], op=ALU.mult),
                 reads=[PB(b), ("Einv", h)], writes=[("kiT", h)])
            if h == 3:
                ring_release(s, "k")

        def B_tm(s, j, name, dst, func, key):
            for half in range(2):
                rb = ring_take(s, "%s%d" % (name, half))
                b = bank()
                proj_tm(rb, j, s % 2, b)
                P.op("act", lambda e, b=b, half=half: e.activation(out=dst[j][:, half * 512:(half + 1) * 512], in_=ps[:, b, :], func=func),
                     reads=[PB(b)], writes=[(key, j)])
            if j == NT - 1:
                ring_release(s, name + "0")
                ring_release(s, name + "1")

        def B_vb(s, j):
            B_tm(s, j, "vb", vb, AF.Copy, "vb")

        def B_zb(s, j):
            B_tm(s, j, "zb", szb, AF.Silu, "szb")

        rbank = {}

        def R_sc(s, j):
            G = s * NT + j
            g = G % NS
            r = g % 2
            js = slice(j * 128, (j + 1) * 128)
            P.op("dve", lambda e: e.tensor_tensor(out=Sp[r][:], in0=S[:], in1=emid[:, g * 4:g * 4 + 4].unsqueeze(2).to_broadcast([128, 4, 256]), op=ALU.mult),
                 reads=["S", ("emid", g)], writes=[("Sp", r)])
            bt = bank()
            for h in range(4):
                P.op("pe", lambda e, h=h: e.matmul(ps[:, bt, h * 128:(h + 1) * 128], lhsT=kiT[:, h, js], rhs=ident[:], start=True, stop=True),
                     reads=[("kiT", h), "ident"], writes=[PB(bt)])
            P.op("act", lambda e: e.activation(out=ktm[r][:], in_=ps[:, bt, :], func=AF.Copy), reads=[PB(bt)], writes=[("ktm", r)])
            bs_ = bank()
            for h in range(4):
                P.op("pe", lambda e, h=h: e.matmul(ps[:, bs_, h * 128:(h + 1) * 128], lhsT=kiT[:, h, js], rhs=qiT[:, h, js], start=True, stop=True),
                     reads=[("kiT", h), ("qiT", h)], writes=[PB(bs_)])
            P.op("dve", lambda e: e.tensor_tensor(out=scT[r][:], in0=ps[:, bs_, :].rearrange("p (h t) -> p h t", h=4),
                                                  in1=maskf.unsqueeze(1).to_broadcast([128, 4, 128]), op=ALU.mult),
                 reads=[PB(bs_), "cst"], writes=[("scT", r)])

        def R_o1(s, j):
            G = s * NT + j
            g = G % NS
            r = g % 2
            bo = bank_pair()
            rbank[g] = bo
            pstate["held"].update((bo, bo + 1))
            for h in range(4):
                bb = bo + h // 2
                o = ps[:, bb, (h % 2) * 256:(h % 2 + 1) * 256]
                P.op("pe", lambda e, o=o, h=h: e.matmul(o, lhsT=scT[r][:, h, :], rhs=vb[j][:, h * 256:(h + 1) * 256], start=(h % 2 == 0), stop=False,
                                                        skip_group_check=True),
                     reads=[("scT", r), ("vb", j)], writes=[PB(bb)])
            bk = bank_pair()
            for h in range(4):
                bb = bk + h // 2
                o = ps[:, bb, (h % 2) * 256:(h % 2 + 1) * 256]
                P.op("pe", lambda e, o=o, h=h: e.matmul(o, lhsT=ktm[r][:, h * 128:(h + 1) * 128], rhs=vb[j][:, h * 256:(h + 1) * 256], start=True, stop=True),
                     reads=[("ktm", r), ("vb", j)], writes=[PB(bb)])
            for h in range(4):
                bb = bk + h // 2
                o = ps[:, bb, (h % 2) * 256:(h % 2 + 1) * 256]
                P.op("act", lambda e, o=o, h=h: e.activation(out=tkv4[:, h, :], in_=o, func=AF.Copy, scale=edl[:, g * 4 + h:g * 4 + h + 1]),
                     reads=[PB(bb), ("edl", g)], writes=[("tkv", h)])
            for h in range(4):
                P.op("dve", lambda e, h=h: e.scalar_tensor_tensor(out=S[:, h, :], in0=S[:, h, :], scalar=elast[:, g * 4 + h:g * 4 + h + 1],
                                                                 in1=tkv4[:, h, :], op0=ALU.mult, op1=ALU.add),
                     reads=["S", ("elast", g), ("tkv", h)], writes=["S"])

        def R_o2(s, j):
            G = s * NT + j
            g = G % NS
            r = g % 2
            js = slice(j * 128, (j + 1) * 128)
            bo = rbank[g]
            for h in range(4):
                bb = bo + h // 2
                o = ps[:, bb, (h % 2) * 256:(h % 2 + 1) * 256]
                P.op("pe", lambda e, o=o, h=h: e.matmul(o, lhsT=qiT[:, h, js], rhs=Sp[r][:, h, :], start=False, stop=True, skip_group_check=True),
                     reads=[("qiT", h), ("Sp", r)], writes=[PB(bb)])
            for h in range(4):
                bb = bo + h // 2
                o = ps[:, bb, (h % 2) * 256:(h % 2 + 1) * 256]
                P.op("act", lambda e, o=o, h=h: e.activation(out=junk[:, h * 256:(h + 1) * 256], in_=o, func=AF.Square, accum_out=sso[:, g * 4 + h:g * 4 + h + 1]),
                     reads=[PB(bb)], writes=[("sso", g, h), ("junk", h)])
            rstd_chain(sso[:, g * 4:g * 4 + 4], rso[:, g * 4:g * 4 + 4], 1.0 / 256, EPS, [("sso", g, h) for h in range(4)], ("rso", g), w=4)

        def R_on(s, j):
            G = s * NT + j
            g = G % NS
            r = g % 2
            bo = rbank[g]
            for h in range(4):
                bb = bo + h // 2
                o = ps[:, bb, (h % 2) * 256:(h % 2 + 1) * 256]
                P.op("dve", lambda e, o=o, h=h: e.scalar_tensor_tensor(out=BFs[r][:, h * 256:(h + 1) * 256], in0=o, scalar=rso[:, g * 4 + h:g * 4 + h + 1],
                                                                      in1=zgs[j][:, h * 256:(h + 1) * 256], op0=ALU.mult, op1=ALU.mult),
                     reads=[PB(bb), ("rso", g), ("szb", j)], writes=[("BFs", r)])
            pstate["held"].difference_update((bo, bo + 1))

        def R_tr(s, j):
            G = s * NT + j
            g = G % NS
            r = g % 2
            js = slice(j * 128, (j + 1) * 128)
            bt2 = bank()
            pv = ps[:, bt2, :].bitcast(BF16)
            for c in range(KC):
                P.op("pe", lambda e, c=c: e.transpose(pv[:, c * 128:(c + 1) * 128], BFs[r][:, c * 128:(c + 1) * 128], ident[:]),
                     reads=[("BFs", r), "ident"], writes=[PB(bt2)])
            P.op("act", lambda e: e.activation(out=onT[:, :, js], in_=pv[:, 0:1024].rearrange("p (c t) -> p c t", c=8), func=AF.Copy),
                 reads=[PB(bt2)], writes=["onT"])

        def M_a(s, c):
            hbuf = s % 2
            half, n = c // 4, c % 4
            r = c % 2
            rga = ring_take(s, "ga%d" % half)
            rwa = ring_take(s, "wa%d" % half)
            rgb = ring_take(s, "gb%d" % half)
            t1 = F32A[r][:, 0:512]
            sga = F32B[0]
            sgb = F32B[1]
            b = bank()
            proj_fm(rga, n * 128, 128, hbuf, b)
            P.op("act", lambda e: e.activation(out=sga[:], in_=ps[:, b, 0:TT], func=AF.Sigmoid), reads=[PB(b)], writes=[("F32B", 0)])
            b2 = bank()
            for kc in range(KC):
                P.op("pe", lambda e, kc=kc: e.matmul(ps[:, b2, 0:TT], lhsT=ring[rwa][:, kc, n * 128:(n + 1) * 128], rhs=guT[:, kc, :],
                                                     start=(kc == 0), stop=(kc == KC - 1)),
                     reads=[("ring", rwa), ("guT", kc)], writes=[PB(b2)])
            P.op("dve", lambda e: e.tensor_tensor(out=t1, in0=ps[:, b2, 0:TT], in1=sga[:], op=ALU.mult),
                 reads=[PB(b2), ("F32B", 0), ("F32A", r)], writes=[("F32A", r, 0)])
            b3 = bank()
            proj_fm(rgb, n * 128, 128, hbuf, b3)
            P.op("act", lambda e: e.activation(out=sgb[:], in_=ps[:, b3, 0:TT], func=AF.Sigmoid), reads=[PB(b3)], writes=[("F32B", 1)])

        def M_b(s, c):
            half, n = c // 4, c % 4
            r = c % 2
            rwb = ring_take(s, "wb%d" % half)
            t1 = F32A[r][:, 0:512]
            t2 = F32A[r][:, 512:1024]
            sgb = F32B[1]
            b4 = bank()
            for kc in range(KC):
                P.op("pe", lambda e, kc=kc: e.matmul(ps[:, b4, 0:TT], lhsT=ring[rwb][:, kc, n * 128:(n + 1) * 128], rhs=onT[:, kc, :],
                                                     start=(kc == 0), stop=(kc == KC - 1)),
                     reads=[("ring", rwb), "onT"], writes=[PB(b4)])
            P.op("dve", lambda e: e.tensor_tensor(out=t2, in0=ps[:, b4, 0:TT], in1=sgb[:], op=ALU.mult),
                 reads=[PB(b4), ("F32B", 1), ("F32A", r)], writes=[("F32A", r, 1)])
            P.op("dve", lambda e: e.tensor_tensor(out=mT[:, c, :], in0=t1, in1=t2, op=ALU.add),
                 reads=[("F32A", r, 0), ("F32A", r, 1)], writes=[("mT", c)])
            if n == 3:
                for nm in ("ga", "wa", "gb", "wb"):
                    ring_release(s, "%s%d" % (nm, half))

        def M_n(s, c):
            M_a(s, c)
            M_b(s, c)

        def Y_load(s, j):
            G = s * NT + j
            g = G % NS
            r = g % 2
            P.dma("sync", lambda e: e.dma_start(out=xr[r][:], in_=x[G * 128:(G + 1) * 128, :]), "xr%d" % r, writes=[("xr", r)])

        def Y(s, j):
            G = s * NT + j
            g = G % NS
            r = g % 2
            js = slice(j * 128, (j + 1) * 128)
            by = bank_pair()
            for half in range(2):
                rw = ring_take(s, "wo%d" % half)
                for kc in range(KC):
                    P.op("pe", lambda e, kc=kc, half=half, rw=rw: e.matmul(ps[:, by + half, :], lhsT=mT[:, kc, js], rhs=ring[rw][:, kc, :],
                                                                          start=(kc == 0), stop=(kc == KC - 1)),
                         reads=[("ring", rw), ("mT", kc)], writes=[PB(by + half)])
            P.op("dve", lambda e: e.tensor_tensor(out=xr[r][:], in0=ps[:, by:by + 2, :].rearrange("p a b -> p (a b)"), in1=xr[r][:], op=ALU.add),
                 reads=[PB(by), PB(by + 1), ("xr", r)], writes=[("xr", r)])
            P.op("act", lambda e: e.activation(out=junk[:], in_=xr[r][:], func=AF.Square, accum_out=ssf[:, g:g + 1]),
                 reads=[("xr", r)], writes=[("ssf", g)] + JK)
            rstd_chain(ssf[:, g:g + 1], rsf[:, g:g + 1], 1.0 / D, EPS, ("ssf", g), ("rsf", g))
            P.op("dve", lambda e: e.scalar_tensor_tensor(out=xr[r][:], in0=xr[r][:], scalar=rsf[:, g:g + 1], in1=Gf[:], op0=ALU.mult, op1=ALU.mult),
                 reads=[("xr", r), ("rsf", g), "Gf"], writes=[("xr", r)])
            P.dma("sync", lambda e: e.dma_start(out=out[G * 128:(G + 1) * 128, :], in_=xr[r][:]), "st%d" % r, reads=[("xr", r)], final=True)
            if j == NT - 1:
                ring_release(s, "wo0")
                ring_release(s, "wo1")

        zgs = szb

        def B_zg(s, j):
            for h in range(4):
                P.op("dve", lambda e, h=h: e.tensor_tensor(out=szb[j][:, h * 256:(h + 1) * 256], in0=szb[j][:, h * 256:(h + 1) * 256], in1=ggb[:], op=ALU.mult),
                     reads=[("szb", j), "ggb"], writes=[("szb", j)])

        for j in range(NT):
            fr_a(0, j)
            fr_b(0, j)
        Wraw = F32A[0][:, :].rearrange("p (h s) -> p h s", h=8)
        cload(Wraw, w_sp.rearrange("h t s -> t h s"), "c_wsp", writes=[("F32A", 0)])
        for hh in range(2):
            b = bank()
            for h4 in range(4):
                h = hh * 4 + h4
                P.op("pe", lambda e, b=b, h=h, h4=h4: e.matmul(ps[:, b, h4 * 128:(h4 + 1) * 128], lhsT=Wraw[:, h, :], rhs=identf,
                                                              start=True, stop=True),
                     reads=[("F32A", 0), "cst"], writes=[PB(b)])
            for h4 in range(4):
                h = hh * 4 + h4
                P.op("dve", lambda e, b=b, h=h, h4=h4: e.tensor_tensor(out=WsT[:, h, :], in0=ps[:, b, h4 * 128:(h4 + 1) * 128], in1=maskf,
                                                                      op=ALU.mult),
                     reads=[PB(b), "cst"], writes=["WsT"])
        bsf = F32A[1]
        cload(bsf[0:1, :], b_sp[0:1, :], "c_bs0", writes=[("F32A", 1)])
        cload(bsf[32:33, :], b_sp[0:1, :], "c_bs1", writes=[("F32A", 1)])
        P.op("dve", lambda e: e.tensor_copy(out=bspad[0:1, :], in_=bsf[0:1, :]), reads=[("F32A", 1)], writes=["bspad"])
        P.op("dve", lambda e: e.tensor_copy(out=BFs[0][32:33, :], in_=bsf[32:33, :]), reads=[("F32A", 1)], writes=[("BFs", 0)])
        P.op("dve", lambda e: e.tensor_tensor(out=bspad[32:33, :], in0=bsf[32:33, :], in1=BFs[0][32:33, :], op=ALU.subtract),
             reads=[("F32A", 1), ("BFs", 0)], writes=["bspad"])

        bcast_row(Gln, ln_g[0:1, :], D, xr[1], ("xr", 1), "c_gln", "Gln")
        bcast_row(Bln, ln_b[0:1, :], D, xr[0], ("xr", 0), "c_bln", "Bln")
        bcast_row(ggb, gla_g[0:1, :], 256, xr[1], ("xr", 1), "c_ggb", "ggb")
        bcast_row(Gf, fin_g[0:1, :], D, xr[0], ("xr", 0), "c_gf", "Gf")
        P.op("dve", lambda e: e.tensor_scalar(out=Gf[:], in0=Gf[:], scalar1=float(D ** 0.5), scalar2=None, op0=ALU.mult), reads=["Gf"], writes=["Gf"])
        P.op("dve", lambda e: e.tensor_scalar(out=ggb[:], in0=ggb[:], scalar1=16.0, scalar2=None, op0=ALU.mult), reads=["ggb"], writes=["ggb"])
        ring_pump()
        for s in range(NST):
            last = (s + 1 == NST)
            cur["st"] = s
            if s == 0:
                B_lr(s)
                B_logit(s, 0); B_logit(s, 1)
            B_vb(s, 0)
            B_cum_a(s, 0); B_cum_a(s, 1)
            B_logit(s, 2); B_logit(s, 3)
            B_cum_b(s, 0); B_cum_b(s, 1)
            B_vb(s, 1)
            B_vb(s, 2)
            B_cum_a(s, 2); B_cum_a(s, 3)
            B_cum_b(s, 2); B_cum_b(s, 3)
            B_vb(s, 3)
            B_zb(s, 0); B_zb(s, 1)
            for h in range(4):
                B_q(s, h)
            for h in range(4):
                B_k(s, h)
            B_zb(s, 2); B_zb(s, 3)
            for c in range(8):
                A_za(s, c)
            for j in range(NT):
                B_zg(s, j)
            for j in range(NT):
                R_sc(s, j)
                if j > 0:
                    A_sp(s, j - 1)
                    R_on(s, j - 1)
                A_v(s, j)
                R_o1(s, j)
                if j > 0:
                    R_tr(s, j - 1)
                A_u(s, 2 * j)
                A_u(s, 2 * j + 1)
                R_o2(s, j)
            A_sp(s, NT - 1)
            R_on(s, NT - 1)
            M_a(s, 0)
            R_tr(s, NT - 1)
            seq = {0: ("a", 0), 1: ("a", 1), 2: ("b", 0), 3: ("a", 2), 4: ("b", 1), 5: ("a", 3), 6: ("b", 2), 7: ("b", 3)}
            for c in range(8):
                if c > 0:
                    M_a(s, c)
                M_b(s, c)
                if not last:
                    kind, jj = seq[c]
                    if c == 7:
                        pass
                    elif kind == "a":
                        fr_a(s + 1, jj)
                    else:
                        fr_b(s + 1, jj)
                        if c == 6:
                            fr_b(s + 1, 3)
                    if c == 2:
                        pass
                if c >= 4:
                    Y_load(s, c - 4) if c - 4 < 2 else None
            if not last:
                B_lr(s + 1)
            for j in range(NT):
                Y(s, j)
                if j + 2 < NT:
                    Y_load(s, j + 2)
                if not last and j < 2:
                    B_logit(s + 1, j)
            if dbg and s == 0:
                P.dma("sync", lambda e: e.dma_start(out=d_hT[:, :, :], in_=hT[0][:]), "dbg0", reads=[("hT", 0)], final=True)
                P.dma("sync", lambda e: e.dma_start(out=d_aT[:, :, :], in_=guT[:]), "dbg1", reads=[("guT", c) for c in range(8)], final=True)
                P.dma("sync", lambda e: e.dma_start(out=d_onT[:, :, :], in_=onT[:]), "dbg2", reads=["onT"], final=True)
                P.dma("sync", lambda e: e.dma_start(out=d_mT[:, :, :], in_=mT[:]), "dbg3", reads=[("mT", c) for c in range(8)], final=True)

        print("sbuf bytes remaining", nc.sbuf_bytes_remaining)
        block = es.enter_context(nc.Block())
        P.emit(block)
    return nc


def _consts():
    c = np.zeros((128, 512), np.float32)
    c[0, 384:512] = 1.0
    c[:, 0:128] = np.eye(128, dtype=np.float32)
    tri = (np.arange(128)[:, None] <= np.arange(128)[None, :]).astype(np.float32)
    c[:, 128:256] = tri * np.float32(-1.0 / 16.0)
    c[:, 256:384] = tri
    return c


def make_in_maps(x, norm_g, w_in, ln_v_g, ln_v_b, w_spatial, b_spatial, w_gate_up, b_gate_up,
                 gla_norm_g, w_branch_a, w_branch_b, w_out, final_norm_g, n_cores, T):
    f = lambda a: np.ascontiguousarray(np.asarray(a, dtype=np.float32))
    shared = {
        "w_in": f(w_in[0]), "w_a": f(w_branch_a[0]), "w_b": f(w_branch_b[0]), "w_o": f(w_out[0]),
        "norm_g": f(norm_g[0]).reshape(1, D), "ln_g": f(ln_v_g[0]).reshape(1, D), "ln_b": f(ln_v_b[0]).reshape(1, D),
        "w_sp": f(w_spatial[0]), "b_sp": f(b_spatial[0]).reshape(1, 1024),
        "w_gu": f(w_gate_up[0]), "b_gu": f(b_gate_up[0]).reshape(1, 512),
        "gla_g": f(gla_norm_g[0]).reshape(1, 256), "fin_g": f(final_norm_g).reshape(1, D),
        "consts": _consts(),
    }
    maps = []
    for b in range(n_cores):
        m = dict(shared)
        m["x"] = f(np.asarray(x)[b, :T])
        maps.append(m)
    return maps


_NC_CACHE = {}


def kernel(x, norm_g, w_in, ln_v_g, ln_v_b, w_spatial, b_spatial, w_gate_up, b_gate_up,
           gla_norm_g, w_branch_a, w_branch_b, w_out, final_norm_g):
    x = np.asarray(x)
    B, T, _ = x.shape
    nc = build_program(T)
    in_maps = make_in_maps(x, norm_g, w_in, ln_v_g, ln_v_b, w_spatial, b_spatial, w_gate_up, b_gate_up,
                           gla_norm_g, w_branch_a, w_branch_b, w_out, final_norm_g, B, T)
    res = run_bass_kernel_spmd(nc, in_maps, core_ids=list(range(B)))
    return np.stack([np.asarray(r["out"], dtype=np.float32) for r in res.results], axis=0)
```
